# Optimizing a Trainium2 kernel written in Bass

```python
import math
import jax, jax.numpy as jnp
from jax import lax
import numpy as np

D_MODEL = 1024
BATCH = 32
SEQ = 256
DEPTH = 2
DEC_BATCH = 4
DEC_SEQ = 2048
PAST_LEN = 512

GRID_W = 64
N_EVEN = (DEPTH + 1) // 2
N_ODD = DEPTH // 2
HALF = D_MODEL // 2
S5_P = 16
S5_G = HALF // S5_P
S5_N = 64
S5_DT_MIN = 0.001
S5_DT_MAX = 0.1
DA_DK = 64
DA_DV = 2 * DA_DK
DA_HEADS = HALF // DA_DV
ROPE_THETA = 10000.0
ROPE_F = DA_DK // 4
Q_BLOCK = 128
SC_WIDTH = 3
LRU_WIDTH = HALF
LRU_BLOCKS = 8
LRU_BS = LRU_WIDTH // LRU_BLOCKS
LRU_CONV = 4
LRU_C = 8.0
D_FF = 2816
FFN_CONV = 3
AB_IN = 4 * HALF
CD_IN = 3 * HALF + 2 * LRU_WIDTH
EPS = 1e-6

kernel_name = "hybrid_s5_diffattn_shortconv_rglru_dit_step"


def rmsnorm(x, g):
    x32 = x.astype(jnp.float32)
    y = x32 * lax.rsqrt(jnp.mean(x32 * x32, axis=-1, keepdims=True) + EPS)
    return (y * g.astype(jnp.float32)).astype(x.dtype)


def dwconv(x, w, left):
    k = w.shape[0]
    L = x.shape[1]
    xp = jnp.pad(x, ((0, 0), (left, k - 1 - left), (0, 0)))
    out = xp[:, 0:L] * w[0]
    for j in range(1, k):
        out = out + xp[:, j:j + L] * w[j]
    return out


def rope_tables(L):
    rows = L // GRID_W
    t_row = jnp.repeat(jnp.arange(rows), GRID_W).astype(jnp.float32)
    t_col = jnp.tile(jnp.arange(GRID_W), rows).astype(jnp.float32)
    inv = ROPE_THETA ** (-jnp.arange(ROPE_F, dtype=jnp.float32) / ROPE_F)
    ang = jnp.stack([t_row[:, None] * inv, t_col[:, None] * inv], axis=1)
    return jnp.cos(ang), jnp.sin(ang)


def rope2d(x, cos, sin):
    xs = x.reshape(x.shape[:-1] + (2, 2, ROPE_F))
    x1, x2 = xs[..., 0, :], xs[..., 1, :]
    c = cos[:, None, None].astype(x.dtype)
    s = sin[:, None, None].astype(x.dtype)
    o = jnp.stack([x1 * c - x2 * s, x2 * c + x1 * s], axis=-2)
    return o.reshape(x.shape)


def modulation(cv, w_mod, b_mod):
    return (jax.nn.silu(cv) @ w_mod + b_mod).reshape(cv.shape[0], 6, D_MODEL)


def _pre(x, m, g, j):
    return rmsnorm(x, g) * (1 + m[:, 3 * j + 1, None]) + m[:, 3 * j, None]


def _post(x, y, m, g, j):
    return x + m[:, 3 * j + 2, None] * rmsnorm(y, g)


def _cplx_comb(e1, e2):
    a1r, a1i, b1r, b1i = e1
    a2r, a2i, b2r, b2i = e2
    return (a2r * a1r - a2i * a1i, a2r * a1i + a2i * a1r,
            a2r * b1r - a2i * b1i + b2r, a2r * b1i + a2i * b1r + b2i)


def _real_comb(e1, e2):
    a1, b1 = e1
    a2, b2 = e2
    return (a1 * a2, a2 * b1 + b2)


def s5_direction(u, lam_re, lam_im, log_dt, b_w, c_w, h0, reverse):
    lam_re = lam_re.astype(jnp.float32)
    lam_im = lam_im.astype(jnp.float32)
    dt = jnp.exp(log_dt.astype(jnp.float32))[:, None]
    mag = jnp.exp(lam_re * dt)
    ang = lam_im * dt
    ab_r, ab_i = mag * jnp.cos(ang), mag * jnp.sin(ang)
    den = lam_re * lam_re + lam_im * lam_im
    nr, ni = ab_r - 1.0, ab_i
    f_r = (nr * lam_re + ni * lam_im) / den
    f_i = (ni * lam_re - nr * lam_im) / den
    b_w = b_w.astype(jnp.float32)
    c_w = c_w.astype(jnp.float32)
    bu_r = jnp.einsum('blgp,gnp->blgn', u, b_w[0])
    bu_i = jnp.einsum('blgp,gnp->blgn', u, b_w[1])
    br = f_r * bu_r - f_i * bu_i
    bi = f_r * bu_i + f_i * bu_r
    if reverse:
        br, bi = jnp.flip(br, axis=1), jnp.flip(bi, axis=1)
    if h0 is not None:
        h0r, h0i = h0[0], h0[1]
        br = br.at[:, 0].add(ab_r * h0r - ab_i * h0i)
        bi = bi.at[:, 0].add(ab_r * h0i + ab_i * h0r)
    ar = jnp.broadcast_to(ab_r, br.shape)
    ai = jnp.broadcast_to(ab_i, br.shape)
    _, _, hr, hi = lax.associative_scan(_cplx_comb, (ar, ai, br, bi), axis=1)
    final = jnp.stack([hr[:, -1], hi[:, -1]], axis=1)
    if reverse:
        hr, hi = jnp.flip(hr, axis=1), jnp.flip(hi, axis=1)
    y = jnp.einsum('blgn,gpn->blgp', hr, c_w[0]) - jnp.einsum('blgn,gpn->blgp', hi, c_w[1])
    return y, final


def diff_attention(q, keys, vals, lam):
    b, lq = q.shape[0], q.shape[1]
    nb = lq // Q_BLOCK
    qb = q.reshape((b, nb, Q_BLOCK) + q.shape[2:]).swapaxes(0, 1)
    scale = 1.0 / math.sqrt(DA_DK)

    def one_block(qblk):
        s = jnp.einsum('bqhmd,bkhmd->bhmqk', qblk, keys, preferred_element_type=jnp.float32) * scale
        p = jax.nn.softmax(s, axis=-1)
        w = p[:, :, 0] - lam * p[:, :, 1]
        return jnp.einsum('bhqk,bkhe->bqhe', w.astype(vals.dtype), vals)

    o = lax.map(one_block, qb)
    return o.swapaxes(0, 1).reshape(b, lq, DA_HEADS, DA_DV)


def mixer_ab(h, w_in, w_out, lam_re, lam_im, log_dt, s5_b, s5_c, s5_d, w_glu, b_glu,
             da_lam, da_g, lam_init, rope, k_ctx, v_ctx, s5_h0):
    b, L, _ = h.shape
    proj = h @ w_in
    u, q, k, v = jnp.split(proj, [HALF, 2 * HALF, 3 * HALF], axis=-1)
    u4 = u.reshape(b, L, S5_G, S5_P).astype(jnp.float32)
    ys = u4 * s5_d.reshape(S5_G, S5_P).astype(jnp.float32)
    finals = []
    for d in range(2):
        h0 = None if s5_h0 is None else jnp.moveaxis(s5_h0[:, d].astype(jnp.float32), 1, 0)
        y_d, fin = s5_direction(u4, lam_re[d], lam_im[d], log_dt[d], s5_b[d], s5_c[d], h0, d == 1)
        ys = ys + y_d
        finals.append(fin)
    s5_state = jnp.stack(finals, axis=1).astype(h.dtype)
    ys = jax.nn.gelu(ys.reshape(b, L, HALF)).astype(h.dtype)
    y_a = ys * jax.nn.sigmoid(ys @ w_glu + b_glu)
    q = q.reshape(b, L, DA_HEADS, 2, DA_DK)
    k = k.reshape(b, L, DA_HEADS, 2, DA_DK)
    v = v.reshape(b, L, DA_HEADS, DA_DV)
    dl = da_lam.astype(jnp.float32)
    lam = jnp.exp(jnp.sum(dl[0] * dl[1])) - jnp.exp(jnp.sum(dl[2] * dl[3])) + lam_init
    if k_ctx is None:
        keys, vals, qr = k, v, q
    else:
        cos, sin = rope
        qr = rope2d(q, cos, sin)
        keys = jnp.concatenate([rope2d(k, cos, sin), k_ctx.astype(h.dtype)], axis=1)
        vals = jnp.concatenate([v, v_ctx.astype(h.dtype)], axis=1)
    o = diff_attention(qr, keys, vals, lam)
    o = rmsnorm(o, da_g) * (1.0 - lam_init)
    y_b = o.reshape(b, L, HALF)
    out = jnp.concatenate([y_a, y_b], axis=-1) @ w_out
    return out, k, v, s5_state


def mixer_cd(h, w_in, w_out, sc_w, conv_w, conv_b, w_a, b_a, w_x, b_x, lru_lam, h0):
    b, L, _ = h.shape
    proj = h @ w_in
    xin, bg, cg, xr, gb = jnp.split(proj, [HALF, 2 * HALF, 3 * HALF, 3 * HALF + LRU_WIDTH], axis=-1)
    y_c = bg * dwconv(cg * xin, sc_w, 1)
    xc = dwconv(xr, conv_w, 2) + conv_b
    xc32 = xc.astype(jnp.float32)
    xb = xc32.reshape(b, L, LRU_BLOCKS, LRU_BS)
    hsum = None
    finals = []
    for d in range(2):
        r = jax.nn.sigmoid(jnp.einsum('blkc,kcd->blkd', xb, w_a[d].astype(jnp.float32)).reshape(b, L, LRU_WIDTH) + b_a[d])
        i = jax.nn.sigmoid(jnp.einsum('blkc,kcd->blkd', xb, w_x[d].astype(jnp.float32)).reshape(b, L, LRU_WIDTH) + b_x[d])
        log_a = -LRU_C * r * jax.nn.softplus(-lru_lam[d].astype(jnp.float32))
        a = jnp.exp(log_a)
        bval = jnp.sqrt(-jnp.expm1(2.0 * log_a)) * (i * xc32)
        if d == 1:
            a, bval = jnp.flip(a, axis=1), jnp.flip(bval, axis=1)
        if h0 is not None:
            bval = bval.at[:, 0].add(a[:, 0] * h0[:, d].astype(jnp.float32))
        _, hs = lax.associative_scan(_real_comb, (a, bval), axis=1)
        finals.append(hs[:, -1])
        if d == 1:
            hs = jnp.flip(hs, axis=1)
        hsum = hs if hsum is None else hsum + hs
    y_d = hsum.astype(h.dtype) * jax.nn.gelu(gb)
    out = jnp.concatenate([y_c, y_d], axis=-1) @ w_out
    return out, jnp.stack(finals, axis=1).astype(h.dtype)


def conv_ffn(h, w_up, cw, cb, w_down):
    u = dwconv(h @ w_up, cw, 1) + cb
    g, val = jnp.split(u, 2, axis=-1)
    return (jax.nn.gelu(g) * val) @ w_down


def setup_inputs(seed: int = 0) -> dict:
    key = jax.random.key(seed)
    ks = iter(jax.random.split(key, 48))
    f32 = jnp.float32

    def nrm(shape, scale):
        return jax.random.normal(next(ks), shape, f32) * scale

    lam_im = math.pi * jnp.arange(S5_N, dtype=f32)
    s5_lam_im = jnp.broadcast_to(lam_im, (N_EVEN, 2, S5_G, S5_N)) + nrm((N_EVEN, 2, S5_G, S5_N), 0.01)
    a_init = jax.random.uniform(next(ks), (N_ODD, 2, LRU_WIDTH), f32, 0.9, 0.999) ** (1.0 / LRU_C)
    return {
        "x_prompt": nrm((BATCH, SEQ, D_MODEL), 1.0),
        "x_sample": nrm((DEC_BATCH, DEC_SEQ, D_MODEL), 1.0),
        "cache_attn_k": nrm((DEC_BATCH, N_EVEN, PAST_LEN, DA_HEADS, 2, DA_DK), 1.0),
        "cache_attn_v": nrm((DEC_BATCH, N_EVEN, PAST_LEN, DA_HEADS, DA_DV), 1.0),
        "state_s5": nrm((DEC_BATCH, N_EVEN, 2, 2, S5_G, S5_N), 0.5),
        "state_rglru": nrm((DEC_BATCH, N_ODD, 2, LRU_WIDTH), 0.5),
        "c": nrm((DEC_BATCH, D_MODEL), 1.0),
        "c_ctx": nrm((D_MODEL,), 1.0),
        "w_mod": nrm((DEPTH, D_MODEL, 6 * D_MODEL), 0.5 * D_MODEL ** -0.5),
        "b_mod": nrm((DEPTH, 6 * D_MODEL), 0.02),
        "norm_g": 1.0 + nrm((DEPTH, 4, D_MODEL), 0.02),
        "w_in_ab": nrm((N_EVEN, D_MODEL, AB_IN), D_MODEL ** -0.5),
        "w_out_ab": nrm((N_EVEN, 2 * HALF, D_MODEL), (2 * HALF) ** -0.5),
        "s5_lam_re": -0.5 + nrm((N_EVEN, 2, S5_G, S5_N), 0.01),
        "s5_lam_im": s5_lam_im,
        "s5_log_dt": jax.random.uniform(next(ks), (N_EVEN, 2, S5_G), f32, math.log(S5_DT_MIN), math.log(S5_DT_MAX)),
        "s5_b": nrm((N_EVEN, 2, 2, S5_G, S5_N, S5_P), S5_P ** -0.5),
        "s5_c": nrm((N_EVEN, 2, 2, S5_G, S5_P, S5_N), (2 * S5_N) ** -0.5),
        "s5_d": nrm((N_EVEN, HALF), 1.0),
        "s5_w_glu": nrm((N_EVEN, HALF, HALF), HALF ** -0.5),
        "s5_b_glu": nrm((N_EVEN, HALF), 0.02),
        "da_lam": nrm((N_EVEN, 4, DA_DK), 0.1),
        "da_g": 1.0 + nrm((N_EVEN, DA_DV), 0.02),
        "w_in_cd": nrm((N_ODD, D_MODEL, CD_IN), D_MODEL ** -0.5),
        "w_out_cd": nrm((N_ODD, 2 * HALF, D_MODEL), (2 * HALF) ** -0.5),
        "sc_conv_w": nrm((N_ODD, SC_WIDTH, HALF), SC_WIDTH ** -0.5),
        "lru_conv_w": nrm((N_ODD, LRU_CONV, LRU_WIDTH), LRU_CONV ** -0.5),
        "lru_conv_b": nrm((N_ODD, LRU_WIDTH), 0.02),
        "lru_w_a": nrm((N_ODD, 2, LRU_BLOCKS, LRU_BS, LRU_BS), LRU_BS ** -0.5),
        "lru_b_a": nrm((N_ODD, 2, LRU_WIDTH), 0.1),
        "lru_w_x": nrm((N_ODD, 2, LRU_BLOCKS, LRU_BS, LRU_BS), LRU_BS ** -0.5),
        "lru_b_x": nrm((N_ODD, 2, LRU_WIDTH), 0.1),
        "lru_lam": jnp.log(a_init / (1.0 - a_init)),
        "ffn_w_up": nrm((DEPTH, D_MODEL, 2 * D_FF), D_MODEL ** -0.5),
        "ffn_conv_w": nrm((DEPTH, FFN_CONV, 2 * D_FF), FFN_CONV ** -0.5),
        "ffn_conv_b": nrm((DEPTH, 2 * D_FF), 0.02),
        "ffn_w_down": nrm((DEPTH, D_FF, D_MODEL), D_FF ** -0.5),
    }


def reference(x_prompt, x_sample, cache_attn_k, cache_attn_v, state_s5, state_rglru, c, c_ctx,
              w_mod, b_mod, norm_g, w_in_ab, w_out_ab, s5_lam_re, s5_lam_im, s5_log_dt, s5_b, s5_c,
              s5_d, s5_w_glu, s5_b_glu, da_lam, da_g, w_in_cd, w_out_cd, sc_conv_w, lru_conv_w,
              lru_conv_b, lru_w_a, lru_b_a, lru_w_x, lru_b_x, lru_lam, ffn_w_up, ffn_conv_w,
              ffn_conv_b, ffn_w_down):
    rope_lat = rope_tables(x_sample.shape[1])
    xp, xs = x_prompt, x_sample
    new_k, new_v, new_s5, new_lru = [], [], [], []
    for l in range(DEPTH):
        m_p = modulation(c_ctx[None], w_mod[l], b_mod[l])
        m_s = modulation(c, w_mod[l], b_mod[l])
        hp = _pre(xp, m_p, norm_g[l, 0], 0)
        hs = _pre(xs, m_s, norm_g[l, 0], 0)
        e = l // 2
        if l % 2 == 0:
            lam_init = 0.8 - 0.6 * math.exp(-0.3 * l)
            wts = (w_in_ab[e], w_out_ab[e], s5_lam_re[e], s5_lam_im[e], s5_log_dt[e], s5_b[e], s5_c[e],
                   s5_d[e], s5_w_glu[e], s5_b_glu[e], da_lam[e], da_g[e], lam_init)
            yp, kc, vc, s5c = mixer_ab(hp, *wts, None, None, None, None)
            ys, _, _, _ = mixer_ab(hs, *wts, rope_lat, cache_attn_k[:, e], cache_attn_v[:, e], state_s5[:, e])
            new_k.append(kc)
            new_v.append(vc)
            new_s5.append(s5c)
        else:
            wts = (w_in_cd[e], w_out_cd[e], sc_conv_w[e], lru_conv_w[e], lru_conv_b[e], lru_w_a[e],
                   lru_b_a[e], lru_w_x[e], lru_b_x[e], lru_lam[e])
            yp, lc = mixer_cd(hp, *wts, None)
            ys, _ = mixer_cd(hs, *wts, state_rglru[:, e])
            new_lru.append(lc)
        xp = _post(xp, yp, m_p, norm_g[l, 1], 0)
        xs = _post(xs, ys, m_s, norm_g[l, 1], 0)
        ffw = (ffn_w_up[l], ffn_conv_w[l], ffn_conv_b[l], ffn_w_down[l])
        xp = _post(xp, conv_ffn(_pre(xp, m_p, norm_g[l, 2], 1), *ffw), m_p, norm_g[l, 3], 1)
        xs = _post(xs, conv_ffn(_pre(xs, m_s, norm_g[l, 2], 1), *ffw), m_s, norm_g[l, 3], 1)
    y_prompt, y_sample = xp, xs
    new_attn_k = jnp.stack(new_k, axis=1)
    new_attn_v = jnp.stack(new_v, axis=1)
    new_s5_state = jnp.stack(new_s5, axis=1)
    new_rglru_state = jnp.stack(new_lru, axis=1)
    return (y_prompt, y_sample, new_attn_k, new_attn_v, new_s5_state, new_rglru_state)
```

```python
import contextlib
import math
import numpy as np
import concourse.bass as bass
import concourse.mybir as mybir
from concourse.bass_utils import run_bass_kernel_spmd

F32 = mybir.dt.float32
BF16 = mybir.dt.bfloat16
ALU = mybir.AluOpType
AF = mybir.ActivationFunctionType

NT = 2048
D = 1024
KT = 8
NSEG = 8
SEG = 256
SEGW = SEG + 2
EPS = 1e-6
DFF = 2816
NCT = 22
SEM_MAX = 24000


class _State:
    __slots__ = ("w", "r")

    def __init__(self):
        self.w = None
        self.r = {}


class Buf:
    def __init__(self, k, name, shape, dtype, space="sbuf"):
        self.k = k
        k.nbuf += 1
        name = "b%d_%s" % (k.nbuf, name)
        self.name = name
        self.psum = space != "sbuf"
        if space == "sbuf":
            self.h = k.es_cur.enter_context(k.nc.sbuf_tensor(name, list(shape), dtype))
        else:
            self.h = k.es_cur.enter_context(k.nc.psum_tensor(name, list(shape), dtype))
        self.states = {None: _State()}
        self.states[None].r = dict(k.freed)
        k.live.setdefault(id(k.es_cur), []).append(self)
        self.dsem = None
        self.dcnt = 0

    def __getitem__(self, idx):
        return V(self.h[idx], self, None)

    def at(self, key, idx):
        return V(self.h[idx], self, key)

    def states_for(self, key):
        if key is None:
            return list(self.states.values())
        if key not in self.states:
            self.states[key] = _State()
        return [self.states[key], self.states[None]]

    def states_upd(self, key):
        if key is None:
            return list(self.states.values())
        if key not in self.states:
            self.states[key] = _State()
        return [self.states[key]]


class V:
    __slots__ = ("ap", "buf", "key")

    def __init__(self, ap, buf, key):
        self.ap = ap
        self.buf = buf
        self.key = key

    def re(self, fn):
        return V(fn(self.ap), self.buf, self.key)


class Eng:
    def __init__(self, k, name, h):
        self.k = k
        self.name = name
        self.h = h
        self.sem = None
        self.cnt = 0
        self.nsem = 0
        self.waited = {}
        self.own = set()

    def next_event(self):
        if self.sem is None or self.cnt >= SEM_MAX:
            self.sem = self.k.new_sem("e_%s_%d" % (self.name, self.nsem))
            self.own.add(id(self.sem))
            self.nsem += 1
            self.cnt = 0
        self.cnt += 1
        return (self.sem, self.cnt)

    def wait(self, ev):
        sem, val = ev
        key = id(sem)
        if self.name == "pe" and key in self.own:
            return
        if self.waited.get(key, 0) < val:
            self.h.wait_ge(sem, val)
            self.waited[key] = val


class K:
    def __init__(self):
        self.nc = bass.Bass("TRN2", target_bir_lowering=False)
        self.es = contextlib.ExitStack()
        self.es_cur = self.es
        self.sems = []
        nc = self.nc
        self.pe = Eng(self, "pe", nc.tensor)
        self.dve = Eng(self, "dve", nc.vector)
        self.act = Eng(self, "act", nc.scalar)
        self.pool = Eng(self, "pool", nc.gpsimd)
        self.sp = Eng(self, "sp", nc.sync)
        self.dma_bufs = []
        self.nsem = 0
        self.nbuf = 0
        self.freed = {}
        self.live = {}

    def new_sem(self, name):
        s = self.es.enter_context(self.nc.semaphore(name))
        self.nsem += 1
        return s

    def _deps(self, reads, writes):
        deps = {}

        def add(ev):
            key = id(ev[0])
            if key not in deps or deps[key][1] < ev[1]:
                deps[key] = ev

        for v in reads:
            for st in v.buf.states_for(v.key):
                if st.w is not None:
                    add(st.w)
                if v.buf.psum:
                    for ev in st.r.values():
                        add(ev)
        for v in writes:
            for st in v.buf.states_for(v.key):
                if st.w is not None:
                    add(st.w)
                for ev in st.r.values():
                    add(ev)
        return deps

    def _update(self, ev, reads, writes):
        for v in reads:
            for st in v.buf.states_upd(v.key):
                key = id(ev[0])
                if key not in st.r or st.r[key][1] < ev[1]:
                    st.r[key] = ev
        for v in writes:
            for st in v.buf.states_upd(v.key):
                st.w = ev
                st.r = {}

    def emit(self, eng, fn, reads, writes):
        deps = self._deps(reads, writes)
        for ev in deps.values():
            eng.wait(ev)
        ins = fn()
        ev = eng.next_event()
        ins.then_inc(ev[0], 1)
        self._update(ev, reads, writes)

    def emit_group(self, eng, fns, reads, writes):
        deps = self._deps(reads, writes)
        for ev in deps.values():
            eng.wait(ev)
        ins = None
        for fn in fns:
            ins = fn()
        ev = eng.next_event()
        ins.then_inc(ev[0], 1)
        self._update(ev, reads, writes)

    def dma(self, out, in_, q=None, slow=False):
        q = q or self.sp
        sb = out if isinstance(out, V) else in_
        is_load = isinstance(out, V)
        buf = sb.buf
        if buf.dsem is None:
            buf.dsem = self.new_sem("d_" + buf.name)
            self.dma_bufs.append(buf)
        reads, writes = ([], [sb]) if is_load else ([sb], [])
        deps = self._deps(reads, writes)
        for ev in deps.values():
            q.wait(ev)
        oa = out.ap if isinstance(out, V) else out
        ia = in_.ap if isinstance(in_, V) else in_
        if slow:
            ins = q.h.dma_start(out=oa, in_=ia, allow_slow_non_contiguous=True)
        else:
            ins = q.h.dma_start(out=oa, in_=ia)
        buf.dcnt += 16
        ins.then_inc(buf.dsem, 16)
        ev = (buf.dsem, buf.dcnt)
        self._update(ev, reads, writes)

    def barrier(self):
        evs = []
        for e in (self.pe, self.dve, self.act, self.pool):
            if e.sem is not None:
                evs.append((e.sem, e.cnt))
        for b in self.dma_bufs:
            evs.append((b.dsem, b.dcnt))
        for e in (self.pe, self.dve, self.act, self.pool, self.sp):
            for ev in evs:
                e.wait(ev)

    @contextlib.contextmanager
    def scope(self):
        prev = self.es_cur
        with contextlib.ExitStack() as s:
            self.es_cur = s
            try:
                yield s
            finally:
                for b in self.live.pop(id(s), []):
                    for st in b.states.values():
                        evs = list(st.r.values()) + ([st.w] if st.w is not None else [])
                        for ev in evs:
                            key = id(ev[0])
                            if key not in self.freed or self.freed[key][1] < ev[1]:
                                self.freed[key] = ev
                self.es_cur = prev

    def finish(self):
        for b in self.dma_bufs:
            self.sp.wait((b.dsem, b.dcnt))
        for e in (self.pe, self.dve, self.act, self.pool):
            if e.sem is not None:
                self.sp.wait((e.sem, e.cnt))

    def _eng(self, e):
        return {"v": self.dve, "g": self.pool, "a": self.act}[e]

    def tt(self, e, out, in0, in1, op):
        eng = self._eng(e)
        self.emit(eng, lambda: eng.h.tensor_tensor(out=out.ap, in0=in0.ap, in1=in1.ap, op=op),
                  [in0, in1], [out])

    def ts(self, e, out, in0, s1, op0, s2=None, op1=None):
        eng = self._eng(e)
        reads = [in0]
        a1 = s1
        if isinstance(s1, V):
            reads.append(s1)
            a1 = s1.ap
        a2 = s2
        if isinstance(s2, V):
            reads.append(s2)
            a2 = s2.ap
        if op1 is None:
            fn = lambda: eng.h.tensor_scalar(out=out.ap, in0=in0.ap, scalar1=a1, scalar2=None, op0=op0)
        else:
            fn = lambda: eng.h.tensor_scalar(out=out.ap, in0=in0.ap, scalar1=a1, scalar2=a2, op0=op0, op1=op1)
        self.emit(eng, fn, reads, [out])

    def stt(self, out, in0, scalar, in1, op0, op1):
        eng = self.dve
        reads = [in0, in1]
        sc = scalar
        if isinstance(scalar, V):
            reads.append(scalar)
            sc = scalar.ap
        self.emit(eng, lambda: eng.h.scalar_tensor_tensor(out=out.ap, in0=in0.ap, scalar=sc, in1=in1.ap,
                                                          op0=op0, op1=op1), reads, [out])

    def scan(self, out, d0, d1, init):
        eng = self.dve
        reads = [d0, d1]
        ini = init
        if isinstance(init, V):
            reads.append(init)
            ini = init.ap
        self.emit(eng, lambda: eng.h.tensor_tensor_scan(out=out.ap, data0=d0.ap, data1=d1.ap, initial=ini,
                                                        op0=ALU.mult, op1=ALU.add), reads, [out])

    def copy(self, e, out, in_):
        if e == "a":
            return self.actf(out, in_, AF.Copy)
        eng = self._eng(e)
        self.emit(eng, lambda: eng.h.tensor_copy(out=out.ap, in_=in_.ap), [in_], [out])

    def memset(self, e, out, val):
        eng = self._eng(e)
        self.emit(eng, lambda: eng.h.memset(out.ap, val), [], [out])

    def recip(self, out, in_):
        eng = self.dve
        self.emit(eng, lambda: eng.h.reciprocal(out=out.ap, in_=in_.ap), [in_], [out])

    def reduce_sum(self, out, in_):
        eng = self.dve
        self.emit(eng, lambda: eng.h.reduce_sum(out=out.ap, in_=in_.ap, axis=mybir.AxisListType.X), [in_], [out])

    def actf(self, out, in_, func, bias=None, scale=None):
        eng = self.act
        reads = [in_]
        kw = {}
        if bias is not None:
            if isinstance(bias, V):
                reads.append(bias)
                kw["bias"] = bias.ap
            else:
                kw["bias"] = bias
        if scale is not None:
            if isinstance(scale, V):
                reads.append(scale)
                kw["scale"] = scale.ap
            else:
                kw["scale"] = scale
        self.emit(eng, lambda: eng.h.activation(out=out.ap, in_=in_.ap, func=func, **kw), reads, [out])

    def mm(self, out, pairs, tile_position=None):
        eng = self.pe
        n = len(pairs)
        fns = []
        reads = []
        for i, (l, r) in enumerate(pairs):
            reads += [l, r]
            kw = {}
            if tile_position is not None:
                kw["tile_position"] = tile_position

            def fn(l=l, r=r, i=i, kw=kw):
                return eng.h.matmul(out.ap, l.ap, r.ap, start=(i == 0), stop=(i == n - 1), **kw)
            fns.append(fn)
        self.emit_group(eng, fns, reads, [out])

    def transpose(self, out, in_, ident):
        eng = self.pe
        self.emit(eng, lambda: eng.h.transpose(out.ap, in_.ap, ident.ap), [in_, ident], [out])


class Pack:
    def __init__(self):
        self.cols = {}
        self.n = 0
        self.parts = []

    def add(self, name, arr):
        arr = np.ascontiguousarray(arr, dtype=np.float32).reshape(128, -1)
        self.cols[name] = (self.n, arr.shape[1])
        self.n += arr.shape[1]
        self.parts.append(arr)

    def build(self):
        return np.concatenate(self.parts, axis=1)


def _pp_layout():
    L = {}
    n = 0
    for name, w in [("cond", 8), ("flag", 2), ("bmod", 96), ("ng", 64),
                    ("s5lre", 32), ("s5lim", 32), ("s5ldt", 32), ("s5h0", 64), ("s5d", 4), ("bglu", 4),
                    ("dag", 1), ("dalam", 256),
                    ("scw", 12), ("lcw", 16), ("lcb", 4), ("lba", 8), ("lbx", 8), ("llam", 8), ("lh0", 8),
                    ("fcw", 2 * 44 * 3), ("fcb", 2 * 44), ("abias", 160)]:
        L[name] = (n, w)
        n += w
    return L, n


PPL, PPN = _pp_layout()


def host_pack(cond, flagS, b_mod, norm_g, s5_lam_re, s5_lam_im, s5_log_dt, s5h0, s5_d, s5_b_glu, da_g, da_lam,
              sc_conv_w, lru_conv_w, lru_conv_b, lru_b_a, lru_b_x, lru_lam, lruh0, ffn_conv_w, ffn_conv_b, abias):
    pk = Pack()
    pk.add("cond", cond.reshape(8, 128).T)
    pk.add("flag", np.full((128, 2), flagS, np.float32))
    pk.add("bmod", b_mod.reshape(2, 48, 128).transpose(2, 0, 1))
    pk.add("ng", norm_g.reshape(2, 4, 8, 128).transpose(3, 0, 1, 2))

    def st(a):
        return a.reshape(2, 16, 2, 64).transpose(2, 3, 0, 1).reshape(128, 32)
    pk.add("s5lre", st(s5_lam_re[0]))
    pk.add("s5lim", st(s5_lam_im[0]))
    pk.add("s5ldt", st(np.broadcast_to(s5_log_dt[0][:, :, None], (2, 32, 64))))
    pk.add("s5h0", s5h0.reshape(2, 2, 16, 2, 64).transpose(3, 4, 0, 1, 2).reshape(128, 64))
    pk.add("s5d", s5_d[0].reshape(4, 128).T)
    pk.add("bglu", s5_b_glu[0].reshape(4, 128).T)
    pk.add("dag", da_g[0].reshape(128, 1))
    pk.add("dalam", np.broadcast_to(da_lam[0].reshape(1, 256), (128, 256)))
    pk.add("scw", sc_conv_w[0].reshape(3, 4, 128).transpose(2, 1, 0))
    pk.add("lcw", lru_conv_w[0].reshape(4, 4, 128).transpose(2, 1, 0))
    pk.add("lcb", lru_conv_b[0].reshape(4, 128).T)
    pk.add("lba", lru_b_a[0].reshape(2, 4, 128).transpose(2, 0, 1))
    pk.add("lbx", lru_b_x[0].reshape(2, 4, 128).transpose(2, 0, 1))
    pk.add("llam", lru_lam[0].reshape(2, 4, 128).transpose(2, 0, 1))
    pk.add("lh0", lruh0.reshape(2, 4, 128).transpose(2, 0, 1))
    pk.add("fcw", ffn_conv_w.reshape(2, 3, 44, 128).transpose(3, 0, 2, 1))
    pk.add("fcb", ffn_conv_b.reshape(2, 44, 128).transpose(2, 0, 1))
    pk.add("abias", abias)
    for name, (o, w) in pk.cols.items():
        assert PPL[name] == (o, w), (name, PPL[name], (o, w))
    return pk.build()


def build_program(phases=("ab", "ffn0", "cd", "ffn1"), skip=()):
    k = K()
    nc = k.nc

    def din(name, shape, dt=F32):
        return nc.dram_tensor(name, list(shape), dt, kind="ExternalInput").ap()

    def dout(name, shape, dt=F32):
        return nc.dram_tensor(name, list(shape), dt, kind="ExternalOutput").ap()

    x_d = din("x", [NT, D])
    pp_d = din("pp", [128, PPN])
    ident_d = din("ident", [128, 128])
    perm_d = din("perm", [128, 128])
    ropec_d = din("ropec", [128, NT])
    ropes_d = din("ropes", [128, NT])
    kc_d = din("kc", [512, 512])
    vc_d = din("vc", [512, 512])
    w_mod_d = din("w_mod", [2, D, 6 * D])
    w_in_ab_d = din("w_in_ab", [D, 2048])
    w_out_ab_d = din("w_out_ab", [D, D])
    s5bt_d = din("s5bt", [128, 16 * 128])
    s5ct_d = din("s5ct", [128, 64 * 128])
    w_glu_d = din("w_glu", [512, 512])
    w_in_cd_d = din("w_in_cd", [D, 2560])
    w_out_cd_d = din("w_out_cd", [D, D])
    lruw_d = din("lruw", [128, 16 * 128])
    w_up_d = din("w_up", [2, D, 2 * DFF])
    w_down_d = din("w_down", [2, DFF, D])

    y_d = dout("y", [NT, D])
    newk_d = dout("newk", [NT, 512])
    newv_d = dout("newv", [NT, 512])
    news5_d = dout("news5", [8, 2, 2, 32, 64])
    newlru_d = dout("newlru", [8, 2, 512])

    GC1 = 0.044715
    GC2 = 2.0 * math.sqrt(2.0 / math.pi)

    with k.es:
        xT = Buf(k, "xT", [128, KT, NT], F32)
        hT = Buf(k, "hT", [128, KT, NSEG, SEGW], BF16)
        pp = Buf(k, "pp", [128, PPN], F32)
        cst = Buf(k, "cst", [128, 1312], F32)
        ident = Buf(k, "ident", [128, 128], F32)
        onesb = Buf(k, "onesb", [128, 128], BF16)
        mvec = Buf(k, "mvec", [128, 96], F32)
        ps = [Buf(k, "ps%d" % i, [128, 512], F32, space="psum") for i in range(8)]
        rr = {}

        def psum(lo=0, hi=8):
            i = rr.get((lo, hi), 0)
            rr[(lo, hi)] = i + 1
            return ps[lo + i % (hi - lo)]

        def P(name, i=0, w=1):
            o, _ = PPL[name]
            return pp[:, o + i:o + i + w]

        CO = {}
        co_n = [0]

        def C(name, w=1):
            if name not in CO:
                CO[name] = (co_n[0], w)
                co_n[0] += w
                assert co_n[0] <= 1312
            o, ww = CO[name]
            return cst.at(name, (slice(None), slice(o, o + ww)))

        def sub(v, lo, hi):
            return v.re(lambda a: a[:, lo:hi])

        def r3(v, s=2):
            return v.re(lambda a: a.rearrange("p (s t) -> p s t", s=s))

        def gelu_tanh(e_mul, out, x, t):
            k.actf(out, x, AF.Gelu_apprx_tanh)

        k.dma(pp[:, :], pp_d)
        k.dma(ident[:, :], ident_d)
        k.memset("v", onesb[:, :], 1.0)
        k.memset("g", hT[:, :, :, :], 0.0)
        flag = P("flag", 0)

        with k.scope():
            xst = [Buf(k, "xst%d" % i, [128, 4, D], F32) for i in range(2)]
            x_v = x_d.rearrange("(a p) f -> p a f", p=128)
            for tb in range(4):
                st = xst[tb % 2]
                k.dma(st[:, :, :], x_v[:, tb * 4:(tb + 1) * 4, :])
                for kt in range(KT):
                    pb = psum()
                    for a in range(4):
                        k.transpose(pb[:, a * 128:(a + 1) * 128], st[:, a, kt * 128:(kt + 1) * 128], ident[:, :])
                    k.copy("v" if kt % 2 == 0 else "a", xT.at(tb, (slice(None), kt, slice(tb * 512, (tb + 1) * 512))),
                           pb[:, :])

        with k.scope():
            sc = Buf(k, "silu_c", [128, 8], F32)
            k.actf(sc[:, :], P("cond", 0, 8), AF.Silu)
            wm = [Buf(k, "wm%d" % i, [128, KT, 512], F32) for i in range(3)]
            loads = []
            for l in range(2):
                wv = w_mod_d[l].rearrange("(kt p) o -> p kt o", p=128)
                for oc in range(12):
                    loads.append(wv[:, :, oc * 512:(oc + 1) * 512])
            nld = [0]

            def ensure(i):
                while nld[0] <= min(i, len(loads) - 1):
                    j = nld[0]
                    k.dma(wm[j % 3][:, :, :], loads[j], q=(k.sp if j % 2 == 0 else k.act))
                    nld[0] += 1
            mrow = Buf(k, "mrow", [1, 6 * D], F32)
            i = 0
            for l in range(2):
                for oc in range(12):
                    ensure(i + 2)
                    wb = wm[i % 3]
                    i += 1
                    pr = psum()
                    k.mm(pr[0:1, :], [(sc[:, kt:kt + 1], wb[:, kt, :]) for kt in range(KT)])
                    k.copy("a" if oc % 2 == 0 else "v", mrow[0:1, oc * 512:(oc + 1) * 512], pr[0:1, :])
                pb = psum()
                for o in range(48):
                    k.mm(pb[:, o:o + 1], [(mrow[0:1, o * 128:(o + 1) * 128], ident[0:1, 0:1])])
                bo, _ = PPL["bmod"]
                k.tt("v", mvec[:, l * 48:(l + 1) * 48], pb[:, 0:48], pp[:, bo + l * 48:bo + (l + 1) * 48], ALU.add)

        def M(l, j):
            return mvec[:, l * 48 + j * 8:l * 48 + j * 8 + 8]

        def NG(l, i):
            o, _ = PPL["ng"]
            return pp[:, o + (l * 4 + i) * 8:o + (l * 4 + i) * 8 + 8]

        for l in range(2):
            for j in range(2):
                a = C("A%d%d" % (l, j), 8)
                k.ts("v", a, M(l, 3 * j + 1), 1.0, ALU.add, 32.0, ALU.mult)
                k.tt("v", a, a, NG(l, 2 * j), ALU.mult)
                g = C("G%d%d" % (l, j), 8)
                k.ts("v", g, M(l, 3 * j + 2), 32.0, ALU.mult)
                k.tt("v", g, g, NG(l, 2 * j + 1), ALU.mult)

        def hseg(kt, tb):
            return hT.at(("h", tb), (slice(None), kt, slice(2 * tb, 2 * tb + 2), slice(1, 1 + SEG)))

        def xblk(kt, tb):
            return xT.at(tb, (slice(None), kt, slice(tb * 512, (tb + 1) * 512)))

        def rsqrt_eps(out, in_, eps_n):
            k.ts("v", out, in_, float(eps_n), ALU.add)
            k.actf(out, out, AF.Sqrt)
            k.recip(out, out)

        def norm_work(pfx):
            sq = [Buf(k, pfx + "_sq%d" % i, [128, 512], BF16) for i in range(2)]
            rstd = Buf(k, pfx + "_rstd", [128, 512], F32)
            tmp = [Buf(k, pfx + "_tmp%d" % i, [128, 512], F32) for i in range(2)]
            return (sq, rstd, tmp)

        def sumsq_rstd(srcs, work, eps_n):
            sq, rstd, tmp = work
            pb = psum()
            n = len(srcs)
            for i, s in enumerate(srcs):
                k.actf(sq[i % 2][:, :], s, AF.Square)
                k.emit(k.pe, (lambda i=i, pb=pb: k.pe.h.matmul(pb[:, :].ap, onesb[:, :].ap, sq[i % 2][:, :].ap,
                                                                start=(i == 0), stop=(i == n - 1))),
                       [onesb[:, :], sq[i % 2][:, :]], [pb[:, :]])
            rsqrt_eps(rstd[:, :], pb[:, :], eps_n)

        def pre_norm(l, j):
            A = C("A%d%d" % (l, j), 8)
            S = M(l, 3 * j)
            with k.scope():
                work = norm_work("pn")
                sq, rstd, tmp = work
                for tb in range(4):
                    sumsq_rstd([xblk(kt, tb) for kt in range(KT)], work, D * EPS)
                    for kt in range(KT):
                        t = tmp[kt % 2]
                        k.tt("v", t[:, :], xblk(kt, tb), rstd[:, :], ALU.mult)
                        k.actf(hseg(kt, tb), r3(t[:, :]), AF.Identity, bias=sub(S, kt, kt + 1), scale=sub(A, kt, kt + 1))

        def halos():
            k.ts("v", hT[:, :, 1:NSEG, 0], hT[:, :, 0:NSEG - 1, SEG], flag, ALU.mult)
            k.ts("v", hT[:, :, 0:NSEG - 1, SEG + 1], hT[:, :, 1:NSEG, 1], flag, ALU.mult)

        def post_norm_block(l, j, tb, yo, work):
            G = C("G%d%d" % (l, j), 8)
            sq, rstd, tmp = work
            sumsq_rstd([yo[:, o, :] for o in range(KT)], work, D * EPS)
            for o in range(KT):
                t = tmp[o % 2]
                k.tt("v", t[:, :], yo[:, o, :], rstd[:, :], ALU.mult)
                k.stt(xblk(o, tb), t[:, :], sub(G, o, o + 1), xblk(o, tb), ALU.mult, ALU.add)

        def out_proj_post(l, w_d, ymix):
            with k.scope():
                work = norm_work("op")
                yo = Buf(k, "op_yo", [128, KT, 512], F32)
                wos = [Buf(k, "op_wo%d" % o, [128, KT, 128], BF16) for o in range(KT)]
                wov = w_d.rearrange("(kt p) o -> p kt o", p=128)
                for o in range(KT):
                    k.dma(wos[o][:, :, :], wov[:, :, o * 128:(o + 1) * 128], q=k.pool)
                for tb in range(4):
                    for o in range(KT):
                        pb = psum()
                        k.mm(pb[:, :], [(wos[o][:, kt, :], ymix[:, kt, tb * 512:(tb + 1) * 512]) for kt in range(KT)])
                        k.copy("a", yo[:, o, :], pb[:, :])
                    post_norm_block(l, 0, tb, yo, work)

        class Stream:
            def __init__(self, bufs, loads, ahead):
                self.bufs, self.loads, self.ahead, self.n = bufs, loads, ahead, 0

            def get(self, i):
                while self.n <= min(i + self.ahead, len(self.loads) - 1):
                    j = self.n
                    for (dst_fn, src) in self.loads[j]:
                        k.dma(dst_fn(self.bufs[j % len(self.bufs)]), src, q=k.pool)
                    self.n += 1
                return self.bufs[i % len(self.bufs)]

        def ffn(l):
            pre_norm(l, 1)
            halos()
            with k.scope():
                work = norm_work("f")
                actb = Buf(k, "f_act", [128, NCT, 1024], BF16)
                wup = [Buf(k, "f_wup%d" % i, [128, KT, 256], BF16) for i in range(3)]
                wdn = [Buf(k, "f_wdn%d" % i, [128, NCT, 128], BF16) for i in range(2)]
                ca = [Buf(k, "f_ca%d" % i, [128, SEG], F32) for i in range(8)]
                yo = Buf(k, "f_yo", [128, KT, 512], F32)
                wupv = w_up_d[l].rearrange("(kt p) c -> p kt c", p=128)
                wdnv = w_down_d[l].rearrange("(c p) o -> p c o", p=128)
                fo, _ = PPL["fcw"]
                bo, _ = PPL["fcb"]

                def cw(ct, j):
                    o = fo + (l * 44 + ct) * 3 + j
                    return pp[:, o:o + 1]

                def cb(ct):
                    o = bo + l * 44 + ct
                    return pp[:, o:o + 1]

                up_loads = []
                for half in range(2):
                    for c in range(NCT):
                        up_loads.append([(lambda b: b[:, :, 0:128], wupv[:, :, c * 128:(c + 1) * 128]),
                                         (lambda b: b[:, :, 128:256], wupv[:, :, DFF + c * 128:DFF + (c + 1) * 128])])
                dn_loads = []
                for tb in range(4):
                    for o in range(KT):
                        dn_loads.append([(lambda b: b[:, :, :], wdnv[:, :, o * 128:(o + 1) * 128])])
                ups = Stream(wup, up_loads, 2)
                dns = Stream(wdn, dn_loads, 1)
                for half in range(2):
                    items = [(c, sg) for c in range(NCT) for sg in range(4)]
                    resv = {}

                    def stA(i):
                        c, sg = items[i]
                        wb = ups.get(half * NCT + c)
                        seg = half * 4 + sg
                        res = []
                        for which in range(2):
                            ct = c + which * NCT
                            pb = psum()
                            k.mm(pb[:, 0:SEGW], [(wb[:, kt, which * 128:(which + 1) * 128], hT[:, kt, seg, :])
                                                 for kt in range(KT)])
                            a = ca[(2 * i + which) % 8]
                            k.actf(a[:, :], pb[:, 1:1 + SEG], AF.Identity, bias=cb(ct), scale=cw(ct, 1))
                            k.stt(a[:, :], pb[:, 0:SEG], cw(ct, 0), a[:, :], ALU.mult, ALU.add)
                            k.stt(a[:, :], pb[:, 2:2 + SEG], cw(ct, 2), a[:, :], ALU.mult, ALU.add)
                            res.append(a)
                        resv[i] = res

                    def stB(i):
                        c, sg = items[i]
                        res = resv.pop(i)
                        k.actf(res[0][:, :], res[0][:, :], AF.Gelu_apprx_tanh)
                        k.tt("v", actb[:, c, sg * SEG:(sg + 1) * SEG], res[0][:, :], res[1][:, :], ALU.mult)
                    LAGF = 2
                    for t in range(len(items) + LAGF):
                        if t - LAGF >= 0:
                            stB(t - LAGF)
                        if t < len(items):
                            stA(t)
                    for tbh in range(2):
                        tb = half * 2 + tbh
                        for o in range(KT):
                            wd = dns.get(tb * KT + o)
                            pb = psum()
                            k.mm(pb[:, :], [(wd[:, c, :], actb[:, c, tbh * 512:(tbh + 1) * 512]) for c in range(NCT)])
                            k.copy("a", yo[:, o, :], pb[:, :])
                        post_norm_block(l, 1, tb, yo, work)

        def mixer_cd(l):
            pre_norm(l, 0)
            with k.scope():
                ymix = Buf(k, "c_ymix", [128, KT, NT], BF16)
                win = [Buf(k, "c_win%d" % i, [128, KT, 128], BF16) for i in range(3)]
                winv = w_in_cd_d.rearrange("(kt p) c -> p kt c", p=128)
                order = []
                for c in range(4):
                    order += [0 + c * 128, 1024 + c * 128, 512 + c * 128]
                for c in range(4):
                    order += [1536 + c * 128, 2048 + c * 128]
                wst = Stream(win, [[(lambda b: b[:, :, :], winv[:, :, c0:c0 + 128])] for c0 in order], 1)
                ip = [0]

                def proj(col0):
                    assert order[ip[0]] == col0
                    wb = wst.get(ip[0])
                    ip[0] += 1
                    outs = []
                    for tb in range(4):
                        pb = psum()
                        k.mm(pb[:, :], [(wb[:, kt, :], hseg(kt, tb)) for kt in range(KT)])
                        outs.append(pb)
                    return outs

                with k.scope():
                    pbuf = Buf(k, "c_p", [128, NSEG, SEGW], F32)
                    xin = Buf(k, "c_xin", [128, NT], F32)
                    acc = Buf(k, "c_acc", [128, NT], F32)
                    k.memset("g", pbuf[:, :, :], 0.0)
                    so, _ = PPL["scw"]
                    for c in range(4):
                        px = proj(0 + c * 128)
                        for tb in range(4):
                            k.copy("a", xin[:, tb * 512:(tb + 1) * 512], px[tb][:, :])
                        pc = proj(1024 + c * 128)
                        for tb in range(4):
                            k.tt("v", pbuf[:, 2 * tb:2 * tb + 2, 1:1 + SEG], r3(pc[tb][:, :]),
                                 r3(xin[:, tb * 512:(tb + 1) * 512]), ALU.mult)
                        k.ts("v", pbuf[:, 1:NSEG, 0], pbuf[:, 0:NSEG - 1, SEG], flag, ALU.mult)
                        k.ts("v", pbuf[:, 0:NSEG - 1, SEG + 1], pbuf[:, 1:NSEG, 1], flag, ALU.mult)
                        w0, w1, w2 = (pp[:, so + c * 3 + j:so + c * 3 + j + 1] for j in range(3))
                        a3 = r3(acc[:, :], NSEG)
                        k.actf(a3, pbuf[:, :, 1:1 + SEG], AF.Identity, scale=w1)
                        k.stt(a3, pbuf[:, :, 0:SEG], w0, a3, ALU.mult, ALU.add)
                        k.stt(a3, pbuf[:, :, 2:2 + SEG], w2, a3, ALU.mult, ALU.add)
                        pbg = proj(512 + c * 128)
                        for tb in range(4):
                            k.tt("v", ymix[:, c, tb * 512:(tb + 1) * 512], pbg[tb][:, :], acc[:, tb * 512:(tb + 1) * 512],
                                 ALU.mult)

                with k.scope():
                    lw = Buf(k, "c_lw", [128, 16, 128], BF16)
                    k.dma(lw[:, :, :], lruw_d.rearrange("p (a o) -> p a o", a=16), q=k.pool)
                    lo, _ = PPL["llam"]
                    nsc = C("lru_nsc", 8)
                    zz = C("lru_z", 8)
                    ser = C("lru_ser", 8)
                    lnv = C("lru_ln", 8)
                    msk = C("lru_msk", 8)
                    k.actf(zz, pp[:, lo:lo + 8], AF.Exp, scale=-1.0)
                    k.ts("v", msk, zz, 0.25, ALU.min)
                    k.memset("v", ser, -1.0 / 12.0)
                    for kk in range(11, 0, -1):
                        k.tt("v", ser, ser, msk, ALU.mult)
                        k.ts("v", ser, ser, (1.0 if kk % 2 == 1 else -1.0) / kk, ALU.add)
                    k.tt("v", ser, ser, msk, ALU.mult)
                    k.ts("v", lnv, zz, 1.0, ALU.add)
                    k.actf(lnv, lnv, AF.Ln)
                    k.ts("v", msk, msk, 1.0, ALU.add)
                    k.actf(msk, msk, AF.Ln)
                    k.tt("v", lnv, lnv, msk, ALU.subtract)
                    k.tt("v", nsc, ser, lnv, ALU.add)
                    k.ts("v", nsc, nsc, -8.0, ALU.mult)
                    xr = Buf(k, "c_xr", [128, NSEG, SEG + 3], F32)
                    xc = Buf(k, "c_xc", [128, NT], F32)
                    xcb = Buf(k, "c_xcb", [128, NT], BF16)
                    ra = Buf(k, "c_ra", [128, NT], F32)
                    ia = Buf(k, "c_ia", [128, NT], F32)

                    class _T1:
                        def __getitem__(self, idx):
                            sl = idx[1]
                            lo = 0 if sl.start is None else sl.start
                            hi = NT if sl.stop is None else sl.stop
                            return xr[:, :, :].re(lambda a: a.rearrange("p s t -> p (s t)")[:, lo:hi])
                    t1 = _T1()
                    hs = [Buf(k, "c_hs%d" % i, [128, NT], F32) for i in range(2)]
                    fin = Buf(k, "c_fin", [128, 4, 2, NSEG], F32)
                    k.memset("g", xr[:, :, :], 0.0)
                    co, _ = PPL["lcw"]
                    cbo, _ = PPL["lcb"]
                    bao, _ = PPL["lba"]
                    bxo, _ = PPL["lbx"]
                    h0o, _ = PPL["lh0"]
                    for c in range(4):
                        px = proj(1536 + c * 128)
                        for tb in range(4):
                            k.copy("a", xr[:, 2 * tb:2 * tb + 2, 2:2 + SEG], r3(px[tb][:, :]))
                        k.memset("g", xr[:, 0, 0:2], 0.0)
                        k.memset("g", xr[:, NSEG - 1, SEG + 2:SEG + 3], 0.0)
                        k.ts("v", xr[:, 1:NSEG, 0:2], xr[:, 0:NSEG - 1, SEG:SEG + 2], flag, ALU.mult)
                        k.ts("v", xr[:, 0:NSEG - 1, SEG + 2], xr[:, 1:NSEG, 2], flag, ALU.mult)
                        w = [pp[:, co + c * 4 + j:co + c * 4 + j + 1] for j in range(4)]
                        x3 = r3(xc[:, :], NSEG)
                        k.actf(x3, xr[:, :, 2:2 + SEG], AF.Identity, scale=w[2], bias=pp[:, cbo + c:cbo + c + 1])
                        k.stt(x3, xr[:, :, 0:SEG], w[0], x3, ALU.mult, ALU.add)
                        k.stt(x3, xr[:, :, 1:1 + SEG], w[1], x3, ALU.mult, ALU.add)
                        k.stt(x3, xr[:, :, 3:3 + SEG], w[3], x3, ALU.mult, ALU.add)
                        k.copy("a", xcb[:, :], xc[:, :])
                        for d in range(2):
                            for tb in range(4):
                                sl = slice(tb * 512, (tb + 1) * 512)
                                ra_t = ra.at(tb, (slice(None), sl))
                                ia_t = ia.at(tb, (slice(None), sl))
                                xc_t = xc.at(tb, (slice(None), sl))
                                t1_t = V(xr.h[:, :, :].rearrange("p s t -> p (s t)")[:, sl], xr, ("t1", tb))
                                pa = psum()
                                k.mm(pa[:, :], [(lw[:, (0 * 2 + d) * 4 + c, :], xcb[:, sl])])
                                k.actf(ra_t, pa[:, :], AF.Sigmoid, bias=pp[:, bao + d * 4 + c:bao + d * 4 + c + 1])
                                pi = psum()
                                k.mm(pi[:, :], [(lw[:, (1 * 2 + d) * 4 + c, :], xcb[:, sl])])
                                k.actf(ia_t, pi[:, :], AF.Sigmoid, bias=pp[:, bxo + d * 4 + c:bxo + d * 4 + c + 1])
                                k.actf(ra_t, ra_t, AF.Exp, scale=sub(nsc, d * 4 + c, d * 4 + c + 1))
                                k.tt("v", t1_t, ra_t, ra_t, ALU.mult)
                                k.ts("v", t1_t, t1_t, -1.0, ALU.mult, 1.0, ALU.add)
                                k.actf(t1_t, t1_t, AF.Sqrt)
                                k.tt("v", ia_t, ia_t, xc_t, ALU.mult)
                                k.tt("v", ia_t, ia_t, t1_t, ALU.mult)
                            a3 = r3(ra[:, :], NSEG)
                            h0 = pp[:, h0o + d * 4 + c:h0o + d * 4 + c + 1]
                            if d == 0:
                                rs_ = a3.re(lambda a: a[:, 1:NSEG, 0])
                                k.ts("v", rs_, rs_, flag, ALU.mult)
                                k.scan(hs[0][:, :], ra[:, :], ia[:, :], h0)
                                k.copy("v", fin[:, c, 0, :], r3(hs[0][:, :], NSEG).re(lambda a: a[:, :, SEG - 1]))
                            else:
                                rs_ = a3.re(lambda a: a[:, 0:NSEG - 1, SEG - 1])
                                k.ts("v", rs_, rs_, flag, ALU.mult)
                                rv = lambda v: v.re(lambda a: a[:, ::-1])
                                k.scan(rv(hs[1][:, :]), rv(ra[:, :]), rv(ia[:, :]), h0)
                                k.copy("v", fin[:, c, 1, :], r3(hs[1][:, :], NSEG).re(lambda a: a[:, :, 0]))
                        k.tt("v", hs[0][:, :], hs[0][:, :], hs[1][:, :], ALU.add)
                        pg = proj(2048 + c * 128)
                        for tb in range(4):
                            sl = slice(tb * 512, (tb + 1) * 512)
                            k.copy("a", xc[:, sl], pg[tb][:, :])
                            gelu_tanh("v", t1[:, sl], xc[:, sl], t1[:, sl])
                            k.tt("v", ymix[:, 4 + c, sl], hs[0][:, sl], t1[:, sl], ALU.mult)
                    for c in range(4):
                        for d in range(2):
                            k.dma(newlru_d[:, d, c * 128:(c + 1) * 128].rearrange("s p -> p s"), fin[:, c, d, :],
                                  slow=True)
                out_proj_post(l, w_out_cd_d, ymix)

        def mixer_ab(l):
            lam_init = 0.8 - 0.6 * math.exp(-0.3 * l)
            pre_norm(l, 0)
            with k.scope():
                ymix = Buf(k, "a_ymix", [128, KT, NT], BF16)
                winv = w_in_ab_d.rearrange("(kt p) c -> p kt c", p=128)
                if "s5" not in skip:
                    s5_part(ymix, winv)
                else:
                    k.memset("g", ymix[:, 0:4, :], 0.0)
                if "attn" not in skip:
                    attn_part(ymix, winv, lam_init)
                else:
                    k.memset("g", ymix[:, 4:8, :], 0.0)
                out_proj_post(l, w_out_ab_d, ymix)

        def s5_part(ymix, winv):
            dt = C("s5_dt", 32)
            mag = C("s5_mag", 32)
            ang = C("s5_ang", 32)
            c1 = C("s5_c1", 32)
            s1 = C("s5_s1", 32)
            fr = C("s5_fr", 32)
            fi = C("s5_fi", 32)
            ta = C("s5_ta", 32)
            tb_ = C("s5_tb", 32)
            tc = C("s5_tc", 32)
            lre = P("s5lre", 0, 32)
            lim = P("s5lim", 0, 32)
            k.actf(dt, P("s5ldt", 0, 32), AF.Exp)
            k.tt("v", mag, lre, dt, ALU.mult)
            k.actf(mag, mag, AF.Exp)
            k.tt("v", ang, lim, dt, ALU.mult)
            MAGIC = 12582912.0

            def sin_of(out, shift):
                k.ts("v", tb_, ang, float(shift), ALU.add)
                k.ts("v", ta, tb_, 1.0 / (2 * math.pi), ALU.mult)
                k.ts("v", ta, ta, MAGIC, ALU.add)
                k.ts("v", ta, ta, -MAGIC, ALU.add)
                k.stt(ta, ta, -2 * math.pi, tb_, ALU.mult, ALU.add)
                k.ts("v", ta, ta, math.pi, ALU.min, -math.pi, ALU.max)
                k.actf(out, ta, AF.Sin)
            sin_of(s1, 0.0)
            sin_of(c1, 0.5 * math.pi)
            k.tt("v", ta, mag, c1, ALU.mult)
            k.ts("v", ta, ta, -1.0, ALU.add)
            k.tt("v", tb_, mag, s1, ALU.mult)
            k.tt("v", tc, lre, lre, ALU.mult)
            k.tt("v", fr, lim, lim, ALU.mult)
            k.tt("v", tc, tc, fr, ALU.add)
            k.recip(tc, tc)
            k.tt("v", fr, ta, lre, ALU.mult)
            k.tt("v", fi, tb_, lim, ALU.mult)
            k.tt("v", fr, fr, fi, ALU.add)
            k.tt("v", fr, fr, tc, ALU.mult)
            k.tt("v", fi, tb_, lre, ALU.mult)
            k.tt("v", tb_, ta, lim, ALU.mult)
            k.tt("v", fi, fi, tb_, ALU.subtract)
            k.tt("v", fi, fi, tc, ALU.mult)
            cpw = C("s5_cpw", 9 * 32)
            spw = C("s5_spw", 9 * 32)
            k.copy("v", sub(cpw, 0, 32), c1)
            k.copy("v", sub(spw, 0, 32), s1)
            for m in range(1, 9):
                cp = sub(cpw, (m - 1) * 32, m * 32)
                sp_ = sub(spw, (m - 1) * 32, m * 32)
                cn = sub(cpw, m * 32, (m + 1) * 32)
                sn = sub(spw, m * 32, (m + 1) * 32)
                k.tt("v", ta, cp, cp, ALU.mult)
                k.tt("v", tb_, sp_, sp_, ALU.mult)
                k.tt("v", cn, ta, tb_, ALU.subtract)
                k.tt("v", ta, cp, sp_, ALU.mult)
                k.ts("v", sn, ta, 2.0, ALU.mult)
            c256 = C("s5_c256", 32)
            s256 = C("s5_s256", 32)
            k.ts("v", c256, sub(cpw, 8 * 32, 9 * 32), flag, ALU.mult)
            k.ts("v", s256, sub(spw, 8 * 32, 9 * 32), flag, ALU.mult)
            ns256 = C("s5_ns256", 32)
            k.ts("v", ns256, s256, -1.0, ALU.mult)
            g0r = C("s5_g0r", 32)
            g0i = C("s5_g0i", 32)
            ho, _ = PPL["s5h0"]
            h3 = pp[:, ho:ho + 64].re(lambda a: a.rearrange("p (d r j) -> p d r j", d=2, r=2))
            hr_ = h3.re(lambda a: a[:, :, 0, :])
            hi_ = h3.re(lambda a: a[:, :, 1, :])

            def v3(v):
                return v.re(lambda a: a.rearrange("p (d j) -> p d j", d=2))
            k.tt("v", ta, fr, fr, ALU.mult)
            k.tt("v", tb_, fi, fi, ALU.mult)
            k.tt("v", ta, ta, tb_, ALU.add)
            k.recip(ta, ta)
            k.tt("v", v3(tb_), hr_, v3(fr), ALU.mult)
            k.tt("v", v3(tc), hi_, v3(fi), ALU.mult)
            k.tt("v", tb_, tb_, tc, ALU.add)
            k.tt("v", g0r, tb_, ta, ALU.mult)
            k.tt("v", v3(tb_), hi_, v3(fr), ALU.mult)
            k.tt("v", v3(tc), hr_, v3(fi), ALU.mult)
            k.tt("v", tb_, tb_, tc, ALU.subtract)
            k.tt("v", g0i, tb_, ta, ALU.mult)
            k.tt("v", ta, g0r, c1, ALU.mult)
            k.tt("v", tb_, g0i, s1, ALU.mult)
            k.tt("v", tc, g0r, s1, ALU.mult)
            k.tt("v", g0r, ta, tb_, ALU.subtract)
            k.tt("v", ta, g0i, c1, ALU.mult)
            k.tt("v", g0i, ta, tc, ALU.add)

            with k.scope():
                bt = Buf(k, "s_bt", [128, 16, 128], BF16)
                k.dma(bt[:, :, :], s5bt_d.rearrange("p (a o) -> p a o", a=16), q=k.pool)
                ctb = [Buf(k, "s_ct%d" % i, [128, 16, 128], BF16) for i in range(1)]
                wu = [Buf(k, "s_wu%d" % i, [128, KT, 128], BF16) for i in range(2)]
                wus = Stream(wu, [[(lambda b: b[:, :, :], winv[:, :, c * 128:(c + 1) * 128])] for c in range(4)], 1)
                uT = Buf(k, "s_uT", [128, NT], BF16)
                ys = Buf(k, "s_ys", [128, NT], F32)
                Ec = Buf(k, "s_Ec", [128, 4, SEG + 1], F32)
                Es = Buf(k, "s_Es", [128, 4, SEG + 1], F32)
                Fc = Buf(k, "s_Fc", [128, 4, SEG], F32)
                Fs = Buf(k, "s_Fs", [128, 4, SEG], F32)
                rho = Buf(k, "s_rho", [128, 4, SEG], F32)
                carry = Buf(k, "s_carry", [128, 4, 2], F32)
                glast = Buf(k, "s_glast", [128, 4, 2, NSEG], F32)
                fin = Buf(k, "s_fin", [128, 4, 2, NSEG], F32)
                NW = 2
                wsi = [0]
                bsb = [Buf(k, "s_b%d" % r, [128, 2 * SEG], F32) for r in range(NW)]
                mt = [Buf(k, "s_m%d" % r, [128, 4 * SEG], F32) for r in range(NW)]
                gin = [Buf(k, "s_gin%d" % r, [128, 2 * SEG], F32) for r in range(NW)]
                gs = [Buf(k, "s_gs%d" % r, [128, 2 * SEG], F32) for r in range(NW)]
                hb = [Buf(k, "s_hb%d" % r, [128, 2 * SEG], BF16) for r in range(NW)]
                tn = Buf(k, "s_tn", [128, 8], F32)
                do_, _ = PPL["s5d"]
                for c in range(4):
                    wb = wus.get(c)
                    k.dma(ctb[0][:, :, :], s5ct_d[:, c * 2048:(c + 1) * 2048].rearrange("p (a o) -> p a o", a=16),
                          q=k.pool)
                    for tb in range(4):
                        pb = psum(4, 8)
                        k.mm(pb[:, :], [(wb[:, kt, :], hseg(kt, tb)) for kt in range(KT)])
                        sl = slice(tb * 512, (tb + 1) * 512)
                        k.actf(ys[:, sl], pb[:, :], AF.Identity, scale=pp[:, do_ + c:do_ + c + 1])
                        k.copy("a", uT[:, sl], pb[:, :])
                    for d in range(2):
                        i0 = d * 16 + 4 * c

                        def cs(v, m=None, w=None):
                            o = i0 if m is None else m * 32 + i0
                            vv = v.re(lambda a: a[:, o:o + 4])
                            if w is not None:
                                vv = vv.re(lambda a: a.unsqueeze(2).to_broadcast([128, 4, w]))
                            return vv
                        k.memset("g", Ec[:, :, 0:1], 1.0)
                        k.memset("g", Es[:, :, 0:1], 0.0)
                        k.copy("v", Ec[:, :, 1], cs(cpw, 0))
                        k.copy("v", Es[:, :, 1], cs(spw, 0))
                        e0f = mt[0][:, 0:512].re(lambda a: a.rearrange("p (j t) -> p j t", j=4))
                        e1f = mt[1][:, 0:512].re(lambda a: a.rearrange("p (j t) -> p j t", j=4))
                        for m in range(1, 9):
                            n = 2 ** m
                            hi = min(2 * n, SEG + 1)
                            w = hi - n
                            cm, sm = cs(cpw, m, w), cs(spw, m, w)
                            e0 = e0f.re(lambda a: a[:, :, 0:w])
                            e1 = e1f.re(lambda a: a[:, :, 0:w])
                            k.tt("v", e0, Ec[:, :, 0:w], cm, ALU.mult)
                            k.tt("v", e1, Es[:, :, 0:w], sm, ALU.mult)
                            k.tt("v", Ec[:, :, n:hi], e0, e1, ALU.subtract)
                            k.tt("v", e0, Ec[:, :, 0:w], sm, ALU.mult)
                            k.tt("v", e1, Es[:, :, 0:w], cm, ALU.mult)
                            k.tt("v", Es[:, :, n:hi], e0, e1, ALU.add)
                        for half in range(2):
                            hs_ = slice(half * 128, (half + 1) * 128)
                            fbr, fbi = cs(fr, None, 128), cs(fi, None, 128)
                            k.tt("v", e0f, Ec[:, :, hs_], fbr, ALU.mult)
                            k.tt("v", e1f, Es[:, :, hs_], fbi, ALU.mult)
                            k.tt("v", Fc[:, :, hs_], e0f, e1f, ALU.subtract)
                            k.tt("v", e0f, Ec[:, :, hs_], fbi, ALU.mult)
                            k.tt("v", e1f, Es[:, :, hs_], fbr, ALU.mult)
                            k.tt("v", Fs[:, :, hs_], e0f, e1f, ALU.add)
                        k.copy("a", rho[:, :, :], cs(mag, None, SEG))
                        k.copy("v", carry[:, :, 0], cs(g0r))
                        k.copy("v", carry[:, :, 1], cs(g0i))
                        items = [(sstep, jj) for sstep in range(NSEG) for jj in range(4)]
                        rv = (lambda v: v)
                        rvm = (lambda v: v) if d == 0 else (lambda v: v.re(lambda a: a[:, ::-1]))
                        lastc = SEG - 1
                        ypbs = {}

                        def segof(sstep):
                            return sstep if d == 0 else NSEG - 1 - sstep

                        def tabf(T, jj):
                            return rv(T[:, jj, 0:SEG])

                        def views(i):
                            ws = i % NW
                            return dict(
                                br=bsb[ws][:, 0:SEG], bi=bsb[ws][:, SEG:2 * SEG],
                                m2=mt[ws][:, 0:SEG], m4=mt[ws][:, SEG:2 * SEG],
                                tA=mt[ws][:, 2 * SEG:3 * SEG], tB=mt[ws][:, 3 * SEG:4 * SEG],
                                gir=gin[ws][:, 0:SEG], gii=gin[ws][:, SEG:2 * SEG],
                                gsr=gs[ws][:, 0:SEG], gsi=gs[ws][:, SEG:2 * SEG],
                                glr=gs[ws][:, lastc:lastc + 1], gli=gs[ws][:, SEG + lastc:SEG + lastc + 1],
                                hr=hb[ws][:, 0:SEG], hi=hb[ws][:, SEG:2 * SEG], bb=bsb[ws][:, :])

                        def S0(i):
                            sstep, jj = items[i]
                            seg = segof(sstep)
                            sl = slice(seg * SEG, (seg + 1) * SEG)
                            v = views(i)
                            pbb = psum(0, 4)
                            rs = slice(32 * jj, 32 * jj + 32)
                            k.mm(pbb[:, 0:SEG], [(bt[rs, (c * 2 + d) * 2 + 0, :], rvm(uT[rs, sl]))], tile_position=(32 * jj, 0))
                            k.mm(pbb[:, SEG:2 * SEG], [(bt[rs, (c * 2 + d) * 2 + 1, :], rvm(uT[rs, sl]))],
                                 tile_position=(32 * jj, 0))
                            k.copy("a", v["bb"], pbb[:, :])

                        def S1a(i):
                            sstep, jj = items[i]
                            v = views(i)
                            k.tt("v", v["gir"], v["br"], tabf(Ec, jj), ALU.mult)
                            k.tt("v", v["m2"], v["bi"], tabf(Es, jj), ALU.mult)
                            k.tt("v", v["m4"], v["br"], tabf(Es, jj), ALU.mult)
                            k.tt("v", v["gii"], v["bi"], tabf(Ec, jj), ALU.mult)

                        def S1b(i):
                            v = views(i)
                            k.tt("v", v["gir"], v["gir"], v["m2"], ALU.add)
                            k.tt("v", v["gii"], v["gii"], v["m4"], ALU.subtract)

                        def S2(i):
                            sstep, jj = items[i]
                            seg = segof(sstep)
                            v = views(i)
                            cr_ = carry.at(jj, (slice(None), jj, slice(0, 1)))
                            ci_ = carry.at(jj, (slice(None), jj, slice(1, 2)))
                            k.scan(rv(v["gsr"]), rho[:, jj, :], rv(v["gir"]), cr_)
                            k.scan(rv(v["gsi"]), rho[:, jj, :], rv(v["gii"]), ci_)
                            k.copy("a", glast.at(jj, (slice(None), jj, 0, slice(seg, seg + 1))), v["glr"])
                            k.copy("a", glast.at(jj, (slice(None), jj, 1, slice(seg, seg + 1))), v["gli"])
                            cc = sub(c256, i0 + jj, i0 + jj + 1)
                            sc_ = sub(s256, i0 + jj, i0 + jj + 1)
                            nsc_ = sub(ns256, i0 + jj, i0 + jj + 1)
                            t_a = tn.at(jj, (slice(None), slice(2 * jj, 2 * jj + 1)))
                            t_b = tn.at(jj, (slice(None), slice(2 * jj + 1, 2 * jj + 2)))
                            k.actf(t_a, v["gli"], AF.Identity, scale=nsc_)
                            k.actf(t_b, v["glr"], AF.Identity, scale=sc_)
                            k.actf(cr_, v["glr"], AF.Identity, scale=cc, bias=t_a)
                            k.actf(ci_, v["gli"], AF.Identity, scale=cc, bias=t_b)

                        def S3a(i):
                            sstep, jj = items[i]
                            v = views(i)
                            k.tt("v", v["tA"], v["gsr"], tabf(Fs, jj), ALU.mult)
                            k.tt("v", v["tB"], v["gsi"], tabf(Fs, jj), ALU.mult)
                            k.tt("v", v["gsr"], v["gsr"], tabf(Fc, jj), ALU.mult)
                            k.tt("v", v["gsi"], v["gsi"], tabf(Fc, jj), ALU.mult)

                        def S3b(i):
                            sstep, jj = items[i]
                            seg = segof(sstep)
                            sl = slice(seg * SEG, (seg + 1) * SEG)
                            v = views(i)
                            k.tt("v", v["hr"], v["gsr"], v["tB"], ALU.subtract)
                            k.stt(v["hi"], v["gsi"], -1.0, v["tA"], ALU.mult, ALU.subtract)
                            if jj == 0:
                                ypbs[sstep] = psum(4, 8)
                            ypb = ypbs[sstep]
                            for ri, hsrc in ((0, v["hr"]), (1, v["hi"])):
                                cv_ = ctb[0][:, (jj * 2 + d) * 2 + ri, :]
                                first = (jj == 0 and ri == 0)
                                last = (jj == 3 and ri == 1)
                                hsrc = rvm(hsrc)
                                k.emit(k.pe, (lambda cv_=cv_, hsrc=hsrc, first=first, last=last, ypb=ypb:
                                              k.pe.h.matmul(ypb[:, 0:SEG].ap, cv_.ap, hsrc.ap, start=first, stop=last)),
                                       [cv_, hsrc], [ypb[:, 0:SEG]])
                            if jj == 3:
                                k.tt("v", ys[:, sl], ys[:, sl], ypb[:, 0:SEG], ALU.add)
                                del ypbs[sstep]

                        stages = [S0, S1a, S1b, S2, S3a, S3b]
                        nI = len(items)
                        for t in range(nI + len(stages) - 1):
                            for si in range(len(stages) - 1, -1, -1):
                                i = t - si
                                if 0 <= i < nI:
                                    stages[si](i)
                        lastt = SEG - 1
                        Fcl = Fc[:, :, lastt:lastt + 1].re(lambda a: a.to_broadcast([128, 4, NSEG]))
                        Fsl = Fs[:, :, lastt:lastt + 1].re(lambda a: a.to_broadcast([128, 4, NSEG]))
                        e0 = e0f.re(lambda a: a[:, :, 0:NSEG])
                        e1 = e1f.re(lambda a: a[:, :, 0:NSEG])
                        k.tt("v", e0, glast[:, :, 0, :], Fcl, ALU.mult)
                        k.tt("v", e1, glast[:, :, 1, :], Fsl, ALU.mult)
                        k.tt("v", fin[:, :, 0, :], e0, e1, ALU.subtract)
                        k.tt("v", e0, glast[:, :, 1, :], Fcl, ALU.mult)
                        k.tt("v", e1, glast[:, :, 0, :], Fsl, ALU.mult)
                        k.tt("v", fin[:, :, 1, :], e0, e1, ALU.add)
                        for jj in range(4):
                            j = 4 * c + jj
                            for ri in range(2):
                                for g2 in range(2):
                                    k.dma(news5_d[:, d, ri, 2 * j + g2, :].rearrange("s n -> n s"),
                                          fin[64 * g2:64 * g2 + 64, jj, ri, :], slow=True)
                    for tb in range(4):
                        sl = slice(tb * 512, (tb + 1) * 512)
                        gelu_tanh("v", ymix[:, c, sl], ys[:, sl], None)
            with k.scope():
                wg = Buf(k, "s_wg", [128, 4, 512], BF16)
                k.dma(wg[:, :, :], w_glu_d.rearrange("(kt p) o -> p kt o", p=128), q=k.pool)
                bgo, _ = PPL["bglu"]
                zs = Buf(k, "s_zs", [128, 4, NT], BF16)
                for co_ in range(4):
                    for tb in range(4):
                        sl = slice(tb * 512, (tb + 1) * 512)
                        pb = psum()
                        k.mm(pb[:, :], [(wg[:, ci, co_ * 128:(co_ + 1) * 128], ymix[:, ci, sl]) for ci in range(4)])
                        k.actf(zs[:, co_, sl], pb[:, :], AF.Sigmoid, bias=pp[:, bgo + co_:bgo + co_ + 1])
                for co_ in range(4):
                    k.tt("v", ymix[:, co_, :], ymix[:, co_, :], zs[:, co_, :], ALU.mult)

        def attn_part(ymix, winv, lam_init):
            dlo, _ = PPL["dalam"]
            nlam = C("da_nlam", 1)
            e01 = C("da_e01", 1)
            e23 = C("da_e23", 1)
            lt = C("da_lt", 128)
            k.tt("v", sub(lt, 0, 64), pp[:, dlo:dlo + 64], pp[:, dlo + 64:dlo + 128], ALU.mult)
            k.tt("v", sub(lt, 64, 128), pp[:, dlo + 128:dlo + 192], pp[:, dlo + 192:dlo + 256], ALU.mult)
            k.reduce_sum(e01, sub(lt, 0, 64))
            k.reduce_sum(e23, sub(lt, 64, 128))
            k.actf(e01, e01, AF.Exp)
            k.actf(e23, e23, AF.Exp)
            k.tt("v", nlam, e23, e01, ALU.subtract)
            k.ts("v", nlam, nlam, -lam_init, ALU.add)
            gq = C("da_gq", 1)
            k.ts("v", gq, P("dag", 0), float((1.0 - lam_init) * math.sqrt(128.0)), ALU.mult)
            with k.scope():
                Vall = Buf(k, "t_V", [128, 20, 512], BF16)
                KTc = Buf(k, "t_KTc", [128, 4, 512], BF16)
                with k.scope():
                    wkvs = [Buf(k, "t_wkv%d" % i, [128, KT, 256], BF16) for i in range(4)]
                    for part in range(4):
                        k.dma(wkvs[part][:, :, :], winv[:, :, 1024 + part * 256:1024 + (part + 1) * 256], q=k.pool)
                    k.dma(Vall[:, 16:20, :], vc_d.rearrange("(a p) f -> p a f", p=128), q=k.pool)
                    kcs = Buf(k, "t_kcs", [128, 4, 512], F32)
                    k.dma(kcs[:, :, :], kc_d.rearrange("(a p) f -> p a f", p=128))
                    kst = [Buf(k, "t_kst%d" % i, [128, 512], F32) for i in range(2)]
                    vst = [Buf(k, "t_vst%d" % i, [128, 512], F32) for i in range(2)]
                    for h in range(4):
                        pb = psum()
                        for a in range(4):
                            k.transpose(pb[:, a * 128:(a + 1) * 128], kcs[:, a, h * 128:(h + 1) * 128], ident[:, :])
                        k.copy("a", KTc[:, h, :], pb[:, :])
                    for tt_ in range(16):
                        seg, off = tt_ // 2, 1 + (tt_ % 2) * 128
                        pk_ = psum()
                        pv_ = psum()
                        for part in range(4):
                            dstp = (pk_ if part < 2 else pv_)[:, (part % 2) * 256:(part % 2 + 1) * 256]
                            k.mm(dstp, [(hT[:, kt, seg, off:off + 128], wkvs[part][:, kt, :]) for kt in range(KT)])
                        ks_, vs_ = kst[tt_ % 2], vst[tt_ % 2]
                        k.copy("a", ks_[:, :], pk_[:, :])
                        k.copy("v", vs_[:, :], pv_[:, :])
                        k.dma(newk_d[tt_ * 128:(tt_ + 1) * 128, :], ks_[:, :])
                        k.dma(newv_d[tt_ * 128:(tt_ + 1) * 128, :], vs_[:, :])
                        k.copy("a", Vall[:, tt_, :], vs_[:, :])
                ropec = Buf(k, "t_ropec", [128, NT], F32)
                ropes = Buf(k, "t_ropes", [128, NT], F32)
                perm = Buf(k, "t_perm", [128, 128], BF16)
                k.dma(ropec[:, :], ropec_d)
                k.dma(ropes[:, :], ropes_d)
                k.dma(perm[:, :], perm_d, q=k.pool)
                QT = Buf(k, "t_QT", [128, NT], BF16)
                KTb = Buf(k, "t_KT", [128, NT], BF16)
                wq = [Buf(k, "t_wq%d" % i, [128, KT, 128], BF16) for i in range(2)]
                qorder = []
                for h in range(4):
                    qorder += [512 + h * 128, 1024 + h * 128]
                wqs = Stream(wq, [[(lambda b: b[:, :, :], winv[:, :, c0:c0 + 128])] for c0 in qorder], 1)
                qraw = Buf(k, "t_qraw", [128, 512], BF16)
                rt = [Buf(k, "t_rt%d" % i, [128, 512], F32) for i in range(4)]
                Pm = [Buf(k, "t_P%d" % i, [128, 512], BF16) for i in range(6)]
                osq = Buf(k, "t_osq", [128, 512], BF16)
                abo, _ = PPL["abias"]
                ip = [0]
                for h in range(4 if "noheads" not in skip else 0):
                    for which, dst in ((0, QT), (1, KTb)):
                        wb = wqs.get(2 * h + which)
                        for tb in range(4):
                            sl = slice(tb * 512, (tb + 1) * 512)
                            pb = psum(4, 8)
                            k.mm(pb[:, :], [(wb[:, kt, :], hseg(kt, tb)) for kt in range(KT)])
                            k.copy("a", qraw[:, :], pb[:, :])
                            pb2 = psum(4, 8)
                            k.mm(pb2[:, :], [(perm[:, :], qraw[:, :])])
                            t1, t2 = rt[(tb % 2) * 2], rt[(tb % 2) * 2 + 1]
                            k.tt("v", t1[:, :], pb[:, :], ropec[:, sl], ALU.mult)
                            k.tt("v", t2[:, :], pb2[:, :], ropes[:, sl], ALU.mult)
                            k.tt("v", dst[:, sl], t1[:, :], t2[:, :], ALU.add)
                    for qsb in range(8 if "noattn" not in skip else 0):
                        qs = slice(qsb * SEG, (qsb + 1) * SEG)
                        acc = [ps[0], ps[1], ps[2], ps[3]]
                        items = list(range(10))
                        LAG = 1
                        pms = {}

                        def kview(kt_, rs):
                            if kt_ < 16:
                                return KTb[rs, kt_ * 128:(kt_ + 1) * 128]
                            return KTc[rs, h, (kt_ - 16) * 128:(kt_ - 15) * 128]

                        def stage1(i):
                            ktp = items[i]
                            sps = [psum(4, 8), psum(4, 8)]
                            for j2 in range(2):
                                for m in range(2):
                                    rs = slice(64 * m, 64 * m + 64)
                                    k.mm(sps[m][:, j2 * SEG:(j2 + 1) * SEG], [(kview(2 * ktp + j2, rs), QT[rs, qs])],
                                         tile_position=(64 * m, 0))
                            col = abo + (2 * ktp) * 8 + qsb
                            cur = []
                            for m in range(2):
                                pm = Pm[ip[0] % len(Pm)]
                                ip[0] += 1
                                k.actf(pm[:, :], sps[m][:, :], AF.Exp, bias=pp[:, col:col + 1], scale=0.125)
                                cur.append(pm)
                            pms[i] = cur

                        def stage2(i):
                            ktp = items[i]
                            cur = pms.pop(i)
                            for m in range(2):
                                pm = cur[m]
                                for j2 in range(2):
                                    kt_ = 2 * ktp + j2
                                    vt = Vall[:, kt_, h * 128:(h + 1) * 128]
                                    pv = pm[:, j2 * SEG:(j2 + 1) * SEG]
                                    k.emit(k.pe, (lambda vt=vt, pv=pv, kt_=kt_, m=m: k.pe.h.matmul(
                                        acc[m][:, 0:SEG].ap, vt.ap, pv.ap, start=(kt_ == 0), stop=(kt_ == 19))),
                                        [vt, pv], [acc[m][:, 0:SEG]])
                            for m in range(2):
                                pm = cur[m]
                                k.emit(k.pe, (lambda pm=pm, m=m: k.pe.h.matmul(
                                    acc[2 + m][:, :].ap, onesb[:, :].ap, pm[:, :].ap, start=(ktp == 0), stop=(ktp == 9))),
                                    [onesb[:, :], pm[:, :]], [acc[2 + m][:, :]])
                        for i in range(len(items) + LAG):
                            if i < len(items):
                                stage1(i)
                            if i >= LAG:
                                stage2(i - LAG)
                        r0, r1, t0, t1 = (rt[i][:, 0:SEG] for i in range(4))
                        k.copy("a", r0, acc[2][:, SEG:2 * SEG])
                        k.copy("a", r1, acc[3][:, SEG:2 * SEG])
                        k.tt("v", r0, acc[2][:, 0:SEG], r0, ALU.add)
                        k.tt("v", r1, acc[3][:, 0:SEG], r1, ALU.add)
                        k.recip(r0, r0)
                        k.recip(r1, r1)
                        k.tt("v", t0, acc[0][:, 0:SEG], r0, ALU.mult)
                        k.tt("v", t1, acc[1][:, 0:SEG], r1, ALU.mult)
                        k.stt(t0, t1, nlam, t0, ALU.mult, ALU.add)
                        k.actf(osq[:, 0:SEG], t0, AF.Square)
                        pb = psum(4, 8)
                        k.mm(pb[:, 0:SEG], [(onesb[:, :], osq[:, 0:SEG])])
                        rsqrt_eps(r0, pb[:, 0:SEG], 128 * EPS)
                        k.tt("v", t0, t0, r0, ALU.mult)
                        k.actf(ymix[:, 4 + h, qs], t0, AF.Identity, scale=gq)

        for ph in phases:
            if ph == "ab":
                mixer_ab(0)
            elif ph == "ffn0":
                ffn(0)
            elif ph == "cd":
                mixer_cd(1)
            elif ph == "ffn1":
                ffn(1)

        with k.scope():
            yst = [Buf(k, "yst%d" % i, [128, D], F32) for i in range(3)]
            for tt_ in range(16):
                st = yst[tt_ % 3]
                for half in range(2):
                    pb = psum()
                    for a in range(4):
                        kt = half * 4 + a
                        k.transpose(pb[:, a * 128:(a + 1) * 128], xT[:, kt, tt_ * 128:(tt_ + 1) * 128], ident[:, :])
                    k.copy("v" if half == 0 else "a", st[:, half * 512:(half + 1) * 512], pb[:, :])
                k.dma(y_d[tt_ * 128:(tt_ + 1) * 128, :], st[:, :])
        k.finish()
    return k


def _rope_tables():
    GRID_W, ROPE_F, THETA = 64, 16, 10000.0
    t = np.arange(NT)
    t_row = (t // GRID_W).astype(np.float32)
    t_col = (t % GRID_W).astype(np.float32)
    inv = (np.float32(THETA) ** (-np.arange(ROPE_F, dtype=np.float32) / np.float32(ROPE_F))).astype(np.float32)
    ang = np.stack([t_row[:, None] * inv, t_col[:, None] * inv], axis=1).astype(np.float32)
    cos, sin = np.cos(ang).astype(np.float32), np.sin(ang).astype(np.float32)
    rc = np.zeros((128, NT), np.float32)
    rs = np.zeros((128, NT), np.float32)
    for m in range(2):
        for a in range(2):
            for hh in range(2):
                for f in range(16):
                    r = m * 64 + a * 32 + hh * 16 + f
                    rc[r] = cos[:, a, f]
                    rs[r] = sin[:, a, f] * (-1.0 if hh == 0 else 1.0)
    return rc, rs


def _make_inputs(inputs):
    f = lambda n: np.asarray(inputs[n], dtype=np.float32)
    x_prompt, x_sample = f("x_prompt"), f("x_sample")
    ck, cv = f("cache_attn_k"), f("cache_attn_v")
    st_s5, st_lru = f("state_s5"), f("state_rglru")
    c, c_ctx = f("c"), f("c_ctx")
    ident = np.eye(128, dtype=np.float32)
    perm = np.zeros((128, 128), np.float32)
    for r in range(128):
        perm[r ^ 16, r] = 1.0
    rc, rs = _rope_tables()
    ones_c, zeros_s = np.ones((128, NT), np.float32), np.zeros((128, NT), np.float32)
    s5b = f("s5_b")[0]
    s5c = f("s5_c")[0]
    bt = np.zeros((128, 4, 2, 2, 128), np.float32)
    ctm = np.zeros((128, 16, 2, 2, 128), np.float32)
    for cc in range(4):
        for jj in range(4):
            j = 4 * cc + jj
            for g2 in range(2):
                g = 2 * j + g2
                for d in range(2):
                    for ri in range(2):
                        bt[32 * jj + 16 * g2:32 * jj + 16 * g2 + 16, cc, d, ri, 64 * g2:64 * g2 + 64] = s5b[d, ri, g].T
                        ctm[64 * g2:64 * g2 + 64, j, d, ri, 32 * jj + 16 * g2:32 * jj + 16 * g2 + 16] = s5c[d, ri, g].T
    lwa, lwx = f("lru_w_a")[0], f("lru_w_x")[0]
    lruw = np.zeros((128, 2, 2, 4, 128), np.float32)
    for ax, wsrc in enumerate((lwa, lwx)):
        for d in range(2):
            for cc in range(4):
                for k2 in range(2):
                    lruw[64 * k2:64 * k2 + 64, ax, d, cc, 64 * k2:64 * k2 + 64] = wsrc[d, 2 * cc + k2]
    shared = {
        "ident": ident, "perm": perm,
        "w_mod": f("w_mod"), "w_in_ab": f("w_in_ab")[0], "w_out_ab": f("w_out_ab")[0],
        "s5bt": bt.reshape(128, -1), "s5ct": ctm.reshape(128, -1), "w_glu": f("s5_w_glu")[0],
        "w_in_cd": f("w_in_cd")[0], "w_out_cd": f("w_out_cd")[0], "lruw": lruw.reshape(128, -1),
        "w_up": f("ffn_w_up"), "w_down": f("ffn_w_down"),
    }
    in_maps = []
    for core in range(8):
        prompt = core < 4
        if prompt:
            x = x_prompt[8 * core:8 * core + 8].reshape(NT, D)
            cond, flagS = c_ctx, 0.0
            kc = np.zeros((512, 512), np.float32)
            vc = np.zeros((512, 512), np.float32)
            s5h0 = np.zeros((2, 2, 32, 64), np.float32)
            lh0 = np.zeros((2, 512), np.float32)
            ropec, ropes = ones_c, zeros_s
            ab = np.full((20, 8), -30000.0, np.float32)
            for kt_ in range(16):
                ab[kt_, kt_ // 2] = 0.0
        else:
            b = core - 4
            x = x_sample[b]
            cond, flagS = c[b], 1.0
            kc = ck[b, 0].reshape(512, 512)
            vc = cv[b, 0].reshape(512, 512)
            s5h0 = st_s5[b, 0]
            lh0 = st_lru[b, 0]
            ropec, ropes = rc, rs
            ab = np.zeros((20, 8), np.float32)
        abias = np.broadcast_to(ab.reshape(1, 160), (128, 160))
        pp = host_pack(cond, flagS, f("b_mod"), f("norm_g"), f("s5_lam_re"), f("s5_lam_im"), f("s5_log_dt"), s5h0,
                       f("s5_d"), f("s5_b_glu"), f("da_g"), f("da_lam"), f("sc_conv_w"), f("lru_conv_w"),
                       f("lru_conv_b"), f("lru_b_a"), f("lru_b_x"), f("lru_lam"), lh0, f("ffn_conv_w"),
                       f("ffn_conv_b"), abias)
        m = dict(shared)
        m.update({"x": np.ascontiguousarray(x), "pp": pp, "ropec": ropec, "ropes": ropes,
                  "kc": np.ascontiguousarray(kc), "vc": np.ascontiguousarray(vc)})
        in_maps.append(m)
    return in_maps


_PROG = {}


def run(inputs, phases=("ab", "ffn0", "cd", "ffn1"), skip=(), trace=False):
    key = (tuple(phases), tuple(skip))
    if key not in _PROG:
        _PROG[key] = build_program(phases, skip)
    k = _PROG[key]
    in_maps = _make_inputs(inputs)
    res = run_bass_kernel_spmd(k.nc, in_maps, core_ids=list(range(8)), trace=trace)
    return res


def kernel(**inputs):
    res = run(inputs)
    r = res.results
    y_prompt = np.concatenate([r[c]["y"].reshape(8, 256, D) for c in range(4)], axis=0)
    y_sample = np.stack([r[c]["y"] for c in range(4, 8)], axis=0)
    newk = np.concatenate([r[c]["newk"].reshape(8, 256, 4, 2, 64) for c in range(4)], axis=0)[:, None]
    newv = np.concatenate([r[c]["newv"].reshape(8, 256, 4, 128) for c in range(4)], axis=0)[:, None]
    news5 = np.concatenate([r[c]["news5"] for c in range(4)], axis=0)[:, None]
    newlru = np.concatenate([r[c]["newlru"] for c in range(4)], axis=0)[:, None]
    return (y_prompt.astype(np.float32), y_sample.astype(np.float32), newk.astype(np.float32),
            newv.astype(np.float32), news5.astype(np.float32), newlru.astype(np.float32))
```

```python
import contextlib
import math
import numpy as np
import concourse.bass as bass
import concourse.mybir as mybir
from concourse.bass_utils import run_bass_kernel_spmd

F32 = mybir.dt.float32
BF16 = mybir.dt.bfloat16
ALU = mybir.AluOpType
AF = mybir.ActivationFunctionType

NT = 2048
D = 1024
KT = 8
NSEG = 8
SEG = 256
SEGW = SEG + 2
EPS = 1e-6
DFF = 2816
NCT = 22
SEM_MAX = 24000


class _State:
    __slots__ = ("w", "r")

    def __init__(self):
        self.w = None
        self.r = {}


class Buf:
    def __init__(self, k, name, shape, dtype, space="sbuf"):
        self.k = k
        k.nbuf += 1
        name = "b%d_%s" % (k.nbuf, name)
        self.name = name
        self.psum = space != "sbuf"
        if space == "sbuf":
            self.h = k.es_cur.enter_context(k.nc.sbuf_tensor(name, list(shape), dtype))
        else:
            self.h = k.es_cur.enter_context(k.nc.psum_tensor(name, list(shape), dtype))
        self.states = {None: _State()}
        self.states[None].r = dict(k.freed)
        k.live.setdefault(id(k.es_cur), []).append(self)
        self.dsem = None
        self.dcnt = 0

    def __getitem__(self, idx):
        return V(self.h[idx], self, None)

    def at(self, key, idx):
        return V(self.h[idx], self, key)

    def states_for(self, key):
        if key is None:
            return list(self.states.values())
        if key not in self.states:
            self.states[key] = _State()
        return [self.states[key], self.states[None]]

    def states_upd(self, key):
        if key is None:
            return list(self.states.values())
        if key not in self.states:
            self.states[key] = _State()
        return [self.states[key]]


class V:
    __slots__ = ("ap", "buf", "key")

    def __init__(self, ap, buf, key):
        self.ap = ap
        self.buf = buf
        self.key = key

    def re(self, fn):
        return V(fn(self.ap), self.buf, self.key)


class Eng:
    def __init__(self, k, name, h):
        self.k = k
        self.name = name
        self.h = h
        self.sem = None
        self.cnt = 0
        self.nsem = 0
        self.waited = {}
        self.own = set()

    def next_event(self):
        if self.sem is None or self.cnt >= SEM_MAX:
            self.sem = self.k.new_sem("e_%s_%d" % (self.name, self.nsem))
            self.own.add(id(self.sem))
            self.nsem += 1
            self.cnt = 0
        self.cnt += 1
        return (self.sem, self.cnt)

    def wait(self, ev):
        sem, val = ev
        key = id(sem)
        if self.name == "pe" and key in self.own:
            return
        if self.waited.get(key, 0) < val:
            self.h.wait_ge(sem, val)
            self.waited[key] = val


class K:
    def __init__(self):
        self.nc = bass.Bass("TRN2", target_bir_lowering=False)
        self.es = contextlib.ExitStack()
        self.es_cur = self.es
        self.sems = []
        nc = self.nc
        self.pe = Eng(self, "pe", nc.tensor)
        self.dve = Eng(self, "dve", nc.vector)
        self.act = Eng(self, "act", nc.scalar)
        self.pool = Eng(self, "pool", nc.gpsimd)
        self.sp = Eng(self, "sp", nc.sync)
        self.dma_bufs = []
        self.nsem = 0
        self.nbuf = 0
        self.freed = {}
        self.live = {}

    def new_sem(self, name):
        s = self.es.enter_context(self.nc.semaphore(name))
        self.nsem += 1
        return s

    def _deps(self, reads, writes):
        deps = {}

        def add(ev):
            key = id(ev[0])
            if key not in deps or deps[key][1] < ev[1]:
                deps[key] = ev

        for v in reads:
            for st in v.buf.states_for(v.key):
                if st.w is not None:
                    add(st.w)
                if v.buf.psum:
                    for ev in st.r.values():
                        add(ev)
        for v in writes:
            for st in v.buf.states_for(v.key):
                if st.w is not None:
                    add(st.w)
                for ev in st.r.values():
                    add(ev)
        return deps

    def _update(self, ev, reads, writes):
        for v in reads:
            for st in v.buf.states_upd(v.key):
                key = id(ev[0])
                if key not in st.r or st.r[key][1] < ev[1]:
                    st.r[key] = ev
        for v in writes:
            for st in v.buf.states_upd(v.key):
                st.w = ev
                st.r = {}

    def emit(self, eng, fn, reads, writes):
        deps = self._deps(reads, writes)
        for ev in deps.values():
            eng.wait(ev)
        ins = fn()
        ev = eng.next_event()
        ins.then_inc(ev[0], 1)
        self._update(ev, reads, writes)

    def emit_group(self, eng, fns, reads, writes):
        deps = self._deps(reads, writes)
        for ev in deps.values():
            eng.wait(ev)
        ins = None
        for fn in fns:
            ins = fn()
        ev = eng.next_event()
        ins.then_inc(ev[0], 1)
        self._update(ev, reads, writes)

    def dma(self, out, in_, q=None, slow=False):
        q = q or self.sp
        sb = out if isinstance(out, V) else in_
        is_load = isinstance(out, V)
        buf = sb.buf
        if buf.dsem is None:
            buf.dsem = self.new_sem("d_" + buf.name)
            self.dma_bufs.append(buf)
        reads, writes = ([], [sb]) if is_load else ([sb], [])
        deps = self._deps(reads, writes)
        for ev in deps.values():
            q.wait(ev)
        oa = out.ap if isinstance(out, V) else out
        ia = in_.ap if isinstance(in_, V) else in_
        if slow:
            ins = q.h.dma_start(out=oa, in_=ia, allow_slow_non_contiguous=True)
        else:
            ins = q.h.dma_start(out=oa, in_=ia)
        buf.dcnt += 16
        ins.then_inc(buf.dsem, 16)
        ev = (buf.dsem, buf.dcnt)
        self._update(ev, reads, writes)

    def barrier(self):
        evs = []
        for e in (self.pe, self.dve, self.act, self.pool):
            if e.sem is not None:
                evs.append((e.sem, e.cnt))
        for b in self.dma_bufs:
            evs.append((b.dsem, b.dcnt))
        for e in (self.pe, self.dve, self.act, self.pool, self.sp):
            for ev in evs:
                e.wait(ev)

    @contextlib.contextmanager
    def scope(self):
        prev = self.es_cur
        with contextlib.ExitStack() as s:
            self.es_cur = s
            try:
                yield s
            finally:
                for b in self.live.pop(id(s), []):
                    for st in b.states.values():
                        evs = list(st.r.values()) + ([st.w] if st.w is not None else [])
                        for ev in evs:
                            key = id(ev[0])
                            if key not in self.freed or self.freed[key][1] < ev[1]:
                                self.freed[key] = ev
                self.es_cur = prev

    def finish(self):
        for b in self.dma_bufs:
            self.sp.wait((b.dsem, b.dcnt))
        for e in (self.pe, self.dve, self.act, self.pool):
            if e.sem is not None:
                self.sp.wait((e.sem, e.cnt))

    def _eng(self, e):
        return {"v": self.dve, "g": self.pool, "a": self.act}[e]

    def tt(self, e, out, in0, in1, op):
        eng = self._eng(e)
        self.emit(eng, lambda: eng.h.tensor_tensor(out=out.ap, in0=in0.ap, in1=in1.ap, op=op),
                  [in0, in1], [out])

    def ts(self, e, out, in0, s1, op0, s2=None, op1=None):
        eng = self._eng(e)
        reads = [in0]
        a1 = s1
        if isinstance(s1, V):
            reads.append(s1)
            a1 = s1.ap
        a2 = s2
        if isinstance(s2, V):
            reads.append(s2)
            a2 = s2.ap
        if op1 is None:
            fn = lambda: eng.h.tensor_scalar(out=out.ap, in0=in0.ap, scalar1=a1, scalar2=None, op0=op0)
        else:
            fn = lambda: eng.h.tensor_scalar(out=out.ap, in0=in0.ap, scalar1=a1, scalar2=a2, op0=op0, op1=op1)
        self.emit(eng, fn, reads, [out])

    def stt(self, out, in0, scalar, in1, op0, op1):
        eng = self.dve
        reads = [in0, in1]
        sc = scalar
        if isinstance(scalar, V):
            reads.append(scalar)
            sc = scalar.ap
        self.emit(eng, lambda: eng.h.scalar_tensor_tensor(out=out.ap, in0=in0.ap, scalar=sc, in1=in1.ap,
                                                          op0=op0, op1=op1), reads, [out])

    def scan(self, out, d0, d1, init):
        eng = self.dve
        reads = [d0, d1]
        ini = init
        if isinstance(init, V):
            reads.append(init)
            ini = init.ap
        self.emit(eng, lambda: eng.h.tensor_tensor_scan(out=out.ap, data0=d0.ap, data1=d1.ap, initial=ini,
                                                        op0=ALU.mult, op1=ALU.add), reads, [out])

    def copy(self, e, out, in_):
        if e == "a":
            return self.actf(out, in_, AF.Copy)
        eng = self._eng(e)
        self.emit(eng, lambda: eng.h.tensor_copy(out=out.ap, in_=in_.ap), [in_], [out])

    def memset(self, e, out, val):
        eng = self._eng(e)
        self.emit(eng, lambda: eng.h.memset(out.ap, val), [], [out])

    def recip(self, out, in_):
        eng = self.dve
        self.emit(eng, lambda: eng.h.reciprocal(out=out.ap, in_=in_.ap), [in_], [out])

    def reduce_sum(self, out, in_):
        eng = self.dve
        self.emit(eng, lambda: eng.h.reduce_sum(out=out.ap, in_=in_.ap, axis=mybir.AxisListType.X), [in_], [out])

    def actf(self, out, in_, func, bias=None, scale=None):
        eng = self.act
        reads = [in_]
        kw = {}
        if bias is not None:
            if isinstance(bias, V):
                reads.append(bias)
                kw["bias"] = bias.ap
            else:
                kw["bias"] = bias
        if scale is not None:
            if isinstance(scale, V):
                reads.append(scale)
                kw["scale"] = scale.ap
            else:
                kw["scale"] = scale
        self.emit(eng, lambda: eng.h.activation(out=out.ap, in_=in_.ap, func=func, **kw), reads, [out])

    def mm(self, out, pairs, tile_position=None):
        eng = self.pe
        n = len(pairs)
        fns = []
        reads = []
        for i, (l, r) in enumerate(pairs):
            reads += [l, r]
            kw = {}
            if tile_position is not None:
                kw["tile_position"] = tile_position

            def fn(l=l, r=r, i=i, kw=kw):
                return eng.h.matmul(out.ap, l.ap, r.ap, start=(i == 0), stop=(i == n - 1), **kw)
            fns.append(fn)
        self.emit_group(eng, fns, reads, [out])

    def transpose(self, out, in_, ident):
        eng = self.pe
        self.emit(eng, lambda: eng.h.transpose(out.ap, in_.ap, ident.ap), [in_, ident], [out])


class Pack:
    def __init__(self):
        self.cols = {}
        self.n = 0
        self.parts = []

    def add(self, name, arr):
        arr = np.ascontiguousarray(arr, dtype=np.float32).reshape(128, -1)
        self.cols[name] = (self.n, arr.shape[1])
        self.n += arr.shape[1]
        self.parts.append(arr)

    def build(self):
        return np.concatenate(self.parts, axis=1)


def _pp_layout():
    L = {}
    n = 0
    for name, w in [("cond", 8), ("flag", 2), ("bmod", 96), ("ng", 64),
                    ("s5lre", 32), ("s5lim", 32), ("s5ldt", 32), ("s5h0", 64), ("s5d", 4), ("bglu", 4),
                    ("dag", 1), ("dalam", 256),
                    ("scw", 12), ("lcw", 16), ("lcb", 4), ("lba", 8), ("lbx", 8), ("llam", 8), ("lh0", 8),
                    ("fcw", 2 * 44 * 3), ("fcb", 2 * 44), ("abias", 160)]:
        L[name] = (n, w)
        n += w
    return L, n


PPL, PPN = _pp_layout()


def host_pack(cond, flagS, b_mod, norm_g, s5_lam_re, s5_lam_im, s5_log_dt, s5h0, s5_d, s5_b_glu, da_g, da_lam,
              sc_conv_w, lru_conv_w, lru_conv_b, lru_b_a, lru_b_x, lru_lam, lruh0, ffn_conv_w, ffn_conv_b, abias):
    pk = Pack()
    pk.add("cond", cond.reshape(8, 128).T)
    pk.add("flag", np.full((128, 2), flagS, np.float32))
    pk.add("bmod", b_mod.reshape(2, 48, 128).transpose(2, 0, 1))
    pk.add("ng", norm_g.reshape(2, 4, 8, 128).transpose(3, 0, 1, 2))

    def st(a):
        return a.reshape(2, 16, 2, 64).transpose(2, 3, 0, 1).reshape(128, 32)
    pk.add("s5lre", st(s5_lam_re[0]))
    pk.add("s5lim", st(s5_lam_im[0]))
    pk.add("s5ldt", st(np.broadcast_to(s5_log_dt[0][:, :, None], (2, 32, 64))))
    pk.add("s5h0", s5h0.reshape(2, 2, 16, 2, 64).transpose(3, 4, 0, 1, 2).reshape(128, 64))
    pk.add("s5d", s5_d[0].reshape(4, 128).T)
    pk.add("bglu", s5_b_glu[0].reshape(4, 128).T)
    pk.add("dag", da_g[0].reshape(128, 1))
    pk.add("dalam", np.broadcast_to(da_lam[0].reshape(1, 256), (128, 256)))
    pk.add("scw", sc_conv_w[0].reshape(3, 4, 128).transpose(2, 1, 0))
    pk.add("lcw", lru_conv_w[0].reshape(4, 4, 128).transpose(2, 1, 0))
    pk.add("lcb", lru_conv_b[0].reshape(4, 128).T)
    pk.add("lba", lru_b_a[0].reshape(2, 4, 128).transpose(2, 0, 1))
    pk.add("lbx", lru_b_x[0].reshape(2, 4, 128).transpose(2, 0, 1))
    pk.add("llam", lru_lam[0].reshape(2, 4, 128).transpose(2, 0, 1))
    pk.add("lh0", lruh0.reshape(2, 4, 128).transpose(2, 0, 1))
    pk.add("fcw", ffn_conv_w.reshape(2, 3, 44, 128).transpose(3, 0, 2, 1))
    pk.add("fcb", ffn_conv_b.reshape(2, 44, 128).transpose(2, 0, 1))
    pk.add("abias", abias)
    for name, (o, w) in pk.cols.items():
        assert PPL[name] == (o, w), (name, PPL[name], (o, w))
    return pk.build()


def build_program(phases=("ab", "ffn0", "cd", "ffn1"), skip=()):
    k = K()
    nc = k.nc

    def din(name, shape, dt=F32):
        return nc.dram_tensor(name, list(shape), dt, kind="ExternalInput").ap()

    def dout(name, shape, dt=F32):
        return nc.dram_tensor(name, list(shape), dt, kind="ExternalOutput").ap()

    x_d = din("x", [NT, D])
    pp_d = din("pp", [128, PPN])
    ident_d = din("ident", [128, 128])
    perm_d = din("perm", [128, 128])
    ropec_d = din("ropec", [128, NT])
    ropes_d = din("ropes", [128, NT])
    kc_d = din("kc", [512, 512])
    vc_d = din("vc", [512, 512])
    w_mod_d = din("w_mod", [2, D, 6 * D])
    w_in_ab_d = din("w_in_ab", [D, 2048])
    w_out_ab_d = din("w_out_ab", [D, D])
    s5bt_d = din("s5bt", [128, 16 * 128])
    s5ct_d = din("s5ct", [128, 64 * 128])
    w_glu_d = din("w_glu", [512, 512])
    w_in_cd_d = din("w_in_cd", [D, 2560])
    w_out_cd_d = din("w_out_cd", [D, D])
    lruw_d = din("lruw", [128, 16 * 128])
    w_up_d = din("w_up", [2, D, 2 * DFF])
    w_down_d = din("w_down", [2, DFF, D])

    y_d = dout("y", [NT, D])
    newk_d = dout("newk", [NT, 512])
    newv_d = dout("newv", [NT, 512])
    news5_d = dout("news5", [8, 2, 2, 32, 64])
    newlru_d = dout("newlru", [8, 2, 512])

    GC1 = 0.044715
    GC2 = 2.0 * math.sqrt(2.0 / math.pi)

    with k.es:
        xT = Buf(k, "xT", [128, KT, NT], F32)
        hT = Buf(k, "hT", [128, KT, NSEG, SEGW], BF16)
        pp = Buf(k, "pp", [128, PPN], F32)
        cst = Buf(k, "cst", [128, 1312], F32)
        ident = Buf(k, "ident", [128, 128], F32)
        onesb = Buf(k, "onesb", [128, 128], BF16)
        mvec = Buf(k, "mvec", [128, 96], F32)
        ps = [Buf(k, "ps%d" % i, [128, 512], F32, space="psum") for i in range(8)]
        rr = {}

        def psum(lo=0, hi=8):
            i = rr.get((lo, hi), 0)
            rr[(lo, hi)] = i + 1
            return ps[lo + i % (hi - lo)]

        def P(name, i=0, w=1):
            o, _ = PPL[name]
            return pp[:, o + i:o + i + w]

        CO = {}
        co_n = [0]

        def C(name, w=1):
            if name not in CO:
                CO[name] = (co_n[0], w)
                co_n[0] += w
                assert co_n[0] <= 1312
            o, ww = CO[name]
            return cst.at(name, (slice(None), slice(o, o + ww)))

        def sub(v, lo, hi):
            return v.re(lambda a: a[:, lo:hi])

        def r3(v, s=2):
            return v.re(lambda a: a.rearrange("p (s t) -> p s t", s=s))

        def gelu_tanh(e_mul, out, x, t):
            k.actf(out, x, AF.Gelu_apprx_tanh)

        k.dma(pp[:, :], pp_d)
        k.dma(ident[:, :], ident_d)
        k.memset("v", onesb[:, :], 1.0)
        k.memset("g", hT[:, :, :, :], 0.0)
        flag = P("flag", 0)

        with k.scope():
            xst = [Buf(k, "xst%d" % i, [128, 4, D], F32) for i in range(2)]
            x_v = x_d.rearrange("(a p) f -> p a f", p=128)
            for tb in range(4):
                st = xst[tb % 2]
                k.dma(st[:, :, :], x_v[:, tb * 4:(tb + 1) * 4, :])
                for kt in range(KT):
                    pb = psum()
                    for a in range(4):
                        k.transpose(pb[:, a * 128:(a + 1) * 128], st[:, a, kt * 128:(kt + 1) * 128], ident[:, :])
                    k.copy("v" if kt % 2 == 0 else "a", xT.at(tb, (slice(None), kt, slice(tb * 512, (tb + 1) * 512))),
                           pb[:, :])

        with k.scope():
            sc = Buf(k, "silu_c", [128, 8], F32)
            k.actf(sc[:, :], P("cond", 0, 8), AF.Silu)
            wm = [Buf(k, "wm%d" % i, [128, KT, 512], F32) for i in range(3)]
            loads = []
            for l in range(2):
                wv = w_mod_d[l].rearrange("(kt p) o -> p kt o", p=128)
                for oc in range(12):
                    loads.append(wv[:, :, oc * 512:(oc + 1) * 512])
            nld = [0]

            def ensure(i):
                while nld[0] <= min(i, len(loads) - 1):
                    j = nld[0]
                    k.dma(wm[j % 3][:, :, :], loads[j], q=(k.sp if j % 2 == 0 else k.act))
                    nld[0] += 1
            mrow = Buf(k, "mrow", [1, 6 * D], F32)
            i = 0
            for l in range(2):
                for oc in range(12):
                    ensure(i + 2)
                    wb = wm[i % 3]
                    i += 1
                    pr = psum()
                    k.mm(pr[0:1, :], [(sc[:, kt:kt + 1], wb[:, kt, :]) for kt in range(KT)])
                    k.copy("a" if oc % 2 == 0 else "v", mrow[0:1, oc * 512:(oc + 1) * 512], pr[0:1, :])
                pb = psum()
                for o in range(48):
                    k.mm(pb[:, o:o + 1], [(mrow[0:1, o * 128:(o + 1) * 128], ident[0:1, 0:1])])
                bo, _ = PPL["bmod"]
                k.tt("v", mvec[:, l * 48:(l + 1) * 48], pb[:, 0:48], pp[:, bo + l * 48:bo + (l + 1) * 48], ALU.add)

        def M(l, j):
            return mvec[:, l * 48 + j * 8:l * 48 + j * 8 + 8]

        def NG(l, i):
            o, _ = PPL["ng"]
            return pp[:, o + (l * 4 + i) * 8:o + (l * 4 + i) * 8 + 8]

        for l in range(2):
            for j in range(2):
                a = C("A%d%d" % (l, j), 8)
                k.ts("v", a, M(l, 3 * j + 1), 1.0, ALU.add, 32.0, ALU.mult)
                k.tt("v", a, a, NG(l, 2 * j), ALU.mult)
                g = C("G%d%d" % (l, j), 8)
                k.ts("v", g, M(l, 3 * j + 2), 32.0, ALU.mult)
                k.tt("v", g, g, NG(l, 2 * j + 1), ALU.mult)

        def hseg(kt, tb):
            return hT.at(("h", tb), (slice(None), kt, slice(2 * tb, 2 * tb + 2), slice(1, 1 + SEG)))

        def xblk(kt, tb):
            return xT.at(tb, (slice(None), kt, slice(tb * 512, (tb + 1) * 512)))

        def rsqrt_eps(out, in_, eps_n):
            k.ts("v", out, in_, float(eps_n), ALU.add)
            k.actf(out, out, AF.Sqrt)
            k.recip(out, out)

        def norm_work(pfx):
            sq = [Buf(k, pfx + "_sq%d" % i, [128, 512], BF16) for i in range(2)]
            rstd = Buf(k, pfx + "_rstd", [128, 512], F32)
            tmp = [Buf(k, pfx + "_tmp%d" % i, [128, 512], F32) for i in range(2)]
            return (sq, rstd, tmp)

        def sumsq_rstd(srcs, work, eps_n):
            sq, rstd, tmp = work
            pb = psum()
            n = len(srcs)
            for i, s in enumerate(srcs):
                k.actf(sq[i % 2][:, :], s, AF.Square)
                k.emit(k.pe, (lambda i=i, pb=pb: k.pe.h.matmul(pb[:, :].ap, onesb[:, :].ap, sq[i % 2][:, :].ap,
                                                                start=(i == 0), stop=(i == n - 1))),
                       [onesb[:, :], sq[i % 2][:, :]], [pb[:, :]])
            rsqrt_eps(rstd[:, :], pb[:, :], eps_n)

        def pre_norm(l, j):
            A = C("A%d%d" % (l, j), 8)
            S = M(l, 3 * j)
            with k.scope():
                work = norm_work("pn")
                sq, rstd, tmp = work
                for tb in range(4):
                    sumsq_rstd([xblk(kt, tb) for kt in range(KT)], work, D * EPS)
                    for kt in range(KT):
                        t = tmp[kt % 2]
                        k.tt("v", t[:, :], xblk(kt, tb), rstd[:, :], ALU.mult)
                        k.actf(hseg(kt, tb), r3(t[:, :]), AF.Identity, bias=sub(S, kt, kt + 1), scale=sub(A, kt, kt + 1))

        def halos():
            k.ts("v", hT[:, :, 1:NSEG, 0], hT[:, :, 0:NSEG - 1, SEG], flag, ALU.mult)
            k.ts("v", hT[:, :, 0:NSEG - 1, SEG + 1], hT[:, :, 1:NSEG, 1], flag, ALU.mult)

        def post_norm_block(l, j, tb, yo, work):
            G = C("G%d%d" % (l, j), 8)
            sq, rstd, tmp = work
            sumsq_rstd([yo[:, o, :] for o in range(KT)], work, D * EPS)
            for o in range(KT):
                t = tmp[o % 2]
                k.tt("v", t[:, :], yo[:, o, :], rstd[:, :], ALU.mult)
                k.stt(xblk(o, tb), t[:, :], sub(G, o, o + 1), xblk(o, tb), ALU.mult, ALU.add)

        def out_proj_post(l, w_d, ymix):
            with k.scope():
                work = norm_work("op")
                yo = Buf(k, "op_yo", [128, KT, 512], F32)
                wos = [Buf(k, "op_wo%d" % o, [128, KT, 128], BF16) for o in range(KT)]
                wov = w_d.rearrange("(kt p) o -> p kt o", p=128)
                for o in range(KT):
                    k.dma(wos[o][:, :, :], wov[:, :, o * 128:(o + 1) * 128], q=k.pool)
                for tb in range(4):
                    for o in range(KT):
                        pb = psum()
                        k.mm(pb[:, :], [(wos[o][:, kt, :], ymix[:, kt, tb * 512:(tb + 1) * 512]) for kt in range(KT)])
                        k.copy("a", yo[:, o, :], pb[:, :])
                    post_norm_block(l, 0, tb, yo, work)

        class Stream:
            def __init__(self, bufs, loads, ahead):
                self.bufs, self.loads, self.ahead, self.n = bufs, loads, ahead, 0

            def get(self, i):
                while self.n <= min(i + self.ahead, len(self.loads) - 1):
                    j = self.n
                    for (dst_fn, src) in self.loads[j]:
                        k.dma(dst_fn(self.bufs[j % len(self.bufs)]), src, q=k.pool)
                    self.n += 1
                return self.bufs[i % len(self.bufs)]

        def ffn(l):
            pre_norm(l, 1)
            halos()
            with k.scope():
                work = norm_work("f")
                actb = Buf(k, "f_act", [128, NCT, 1024], BF16)
                wup = [Buf(k, "f_wup%d" % i, [128, KT, 256], BF16) for i in range(3)]
                wdn = [Buf(k, "f_wdn%d" % i, [128, NCT, 128], BF16) for i in range(2)]
                ca = [Buf(k, "f_ca%d" % i, [128, SEG], F32) for i in range(8)]
                yo = Buf(k, "f_yo", [128, KT, 512], F32)
                wupv = w_up_d[l].rearrange("(kt p) c -> p kt c", p=128)
                wdnv = w_down_d[l].rearrange("(c p) o -> p c o", p=128)
                fo, _ = PPL["fcw"]
                bo, _ = PPL["fcb"]

                def cw(ct, j):
                    o = fo + (l * 44 + ct) * 3 + j
                    return pp[:, o:o + 1]

                def cb(ct):
                    o = bo + l * 44 + ct
                    return pp[:, o:o + 1]

                up_loads = []
                for half in range(2):
                    for c in range(NCT):
                        up_loads.append([(lambda b: b[:, :, 0:128], wupv[:, :, c * 128:(c + 1) * 128]),
                                         (lambda b: b[:, :, 128:256], wupv[:, :, DFF + c * 128:DFF + (c + 1) * 128])])
                dn_loads = []
                for tb in range(4):
                    for o in range(KT):
                        dn_loads.append([(lambda b: b[:, :, :], wdnv[:, :, o * 128:(o + 1) * 128])])
                ups = Stream(wup, up_loads, 2)
                dns = Stream(wdn, dn_loads, 1)
                for half in range(2):
                    items = [(c, sg) for c in range(NCT) for sg in range(4)]
                    resv = {}

                    def stA(i):
                        c, sg = items[i]
                        wb = ups.get(half * NCT + c)
                        seg = half * 4 + sg
                        res = []
                        for which in range(2):
                            ct = c + which * NCT
                            pb = psum()
                            k.mm(pb[:, 0:SEGW], [(wb[:, kt, which * 128:(which + 1) * 128], hT[:, kt, seg, :])
                                                 for kt in range(KT)])
                            a = ca[(2 * i + which) % 8]
                            k.actf(a[:, :], pb[:, 1:1 + SEG], AF.Identity, bias=cb(ct), scale=cw(ct, 1))
                            k.stt(a[:, :], pb[:, 0:SEG], cw(ct, 0), a[:, :], ALU.mult, ALU.add)
                            k.stt(a[:, :], pb[:, 2:2 + SEG], cw(ct, 2), a[:, :], ALU.mult, ALU.add)
                            res.append(a)
                        resv[i] = res

                    def stB(i):
                        c, sg = items[i]
                        res = resv.pop(i)
                        k.actf(res[0][:, :], res[0][:, :], AF.Gelu_apprx_tanh)
                        k.tt("v", actb[:, c, sg * SEG:(sg + 1) * SEG], res[0][:, :], res[1][:, :], ALU.mult)
                    LAGF = 2
                    for t in range(len(items) + LAGF):
                        if t - LAGF >= 0:
                            stB(t - LAGF)
                        if t < len(items):
                            stA(t)
                    for tbh in range(2):
                        tb = half * 2 + tbh
                        for o in range(KT):
                            wd = dns.get(tb * KT + o)
                            pb = psum()
                            k.mm(pb[:, :], [(wd[:, c, :], actb[:, c, tbh * 512:(tbh + 1) * 512]) for c in range(NCT)])
                            k.copy("a", yo[:, o, :], pb[:, :])
                        post_norm_block(l, 1, tb, yo, work)

        def mixer_cd(l):
            pre_norm(l, 0)
            with k.scope():
                ymix = Buf(k, "c_ymix", [128, KT, NT], BF16)
                win = [Buf(k, "c_win%d" % i, [128, KT, 128], BF16) for i in range(3)]
                winv = w_in_cd_d.rearrange("(kt p) c -> p kt c", p=128)
                order = []
                for c in range(4):
                    order += [0 + c * 128, 1024 + c * 128, 512 + c * 128]
                for c in range(4):
                    order += [1536 + c * 128, 2048 + c * 128]
                wst = Stream(win, [[(lambda b: b[:, :, :], winv[:, :, c0:c0 + 128])] for c0 in order], 1)
                ip = [0]

                def proj(col0):
                    assert order[ip[0]] == col0
                    wb = wst.get(ip[0])
                    ip[0] += 1
                    outs = []
                    for tb in range(4):
                        pb = psum()
                        k.mm(pb[:, :], [(wb[:, kt, :], hseg(kt, tb)) for kt in range(KT)])
                        outs.append(pb)
                    return outs

                with k.scope():
                    pbuf = Buf(k, "c_p", [128, NSEG, SEGW], F32)
                    xin = Buf(k, "c_xin", [128, NT], F32)
                    acc = Buf(k, "c_acc", [128, NT], F32)
                    k.memset("g", pbuf[:, :, :], 0.0)
                    so, _ = PPL["scw"]
                    for c in range(4):
                        px = proj(0 + c * 128)
                        for tb in range(4):
                            k.copy("a", xin[:, tb * 512:(tb + 1) * 512], px[tb][:, :])
                        pc = proj(1024 + c * 128)
                        for tb in range(4):
                            k.tt("v", pbuf[:, 2 * tb:2 * tb + 2, 1:1 + SEG], r3(pc[tb][:, :]),
                                 r3(xin[:, tb * 512:(tb + 1) * 512]), ALU.mult)
                        k.ts("v", pbuf[:, 1:NSEG, 0], pbuf[:, 0:NSEG - 1, SEG], flag, ALU.mult)
                        k.ts("v", pbuf[:, 0:NSEG - 1, SEG + 1], pbuf[:, 1:NSEG, 1], flag, ALU.mult)
                        w0, w1, w2 = (pp[:, so + c * 3 + j:so + c * 3 + j + 1] for j in range(3))
                        a3 = r3(acc[:, :], NSEG)
                        k.actf(a3, pbuf[:, :, 1:1 + SEG], AF.Identity, scale=w1)
                        k.stt(a3, pbuf[:, :, 0:SEG], w0, a3, ALU.mult, ALU.add)
                        k.stt(a3, pbuf[:, :, 2:2 + SEG], w2, a3, ALU.mult, ALU.add)
                        pbg = proj(512 + c * 128)
                        for tb in range(4):
                            k.tt("v", ymix[:, c, tb * 512:(tb + 1) * 512], pbg[tb][:, :], acc[:, tb * 512:(tb + 1) * 512],
                                 ALU.mult)

                with k.scope():
                    lw = Buf(k, "c_lw", [128, 16, 128], BF16)
                    k.dma(lw[:, :, :], lruw_d.rearrange("p (a o) -> p a o", a=16), q=k.pool)
                    lo, _ = PPL["llam"]
                    nsc = C("lru_nsc", 8)
                    zz = C("lru_z", 8)
                    ser = C("lru_ser", 8)
                    lnv = C("lru_ln", 8)
                    msk = C("lru_msk", 8)
                    k.actf(zz, pp[:, lo:lo + 8], AF.Exp, scale=-1.0)
                    k.ts("v", msk, zz, 0.25, ALU.min)
                    k.memset("v", ser, -1.0 / 12.0)
                    for kk in range(11, 0, -1):
                        k.tt("v", ser, ser, msk, ALU.mult)
                        k.ts("v", ser, ser, (1.0 if kk % 2 == 1 else -1.0) / kk, ALU.add)
                    k.tt("v", ser, ser, msk, ALU.mult)
                    k.ts("v", lnv, zz, 1.0, ALU.add)
                    k.actf(lnv, lnv, AF.Ln)
                    k.ts("v", msk, msk, 1.0, ALU.add)
                    k.actf(msk, msk, AF.Ln)
                    k.tt("v", lnv, lnv, msk, ALU.subtract)
                    k.tt("v", nsc, ser, lnv, ALU.add)
                    k.ts("v", nsc, nsc, -8.0, ALU.mult)
                    xr = Buf(k, "c_xr", [128, NSEG, SEG + 3], F32)
                    xc = Buf(k, "c_xc", [128, NT], F32)
                    xcb = Buf(k, "c_xcb", [128, NT], BF16)
                    ra = Buf(k, "c_ra", [128, NT], F32)
                    ia = Buf(k, "c_ia", [128, NT], F32)

                    class _T1:
                        def __getitem__(self, idx):
                            sl = idx[1]
                            lo = 0 if sl.start is None else sl.start
                            hi = NT if sl.stop is None else sl.stop
                            return xr[:, :, :].re(lambda a: a.rearrange("p s t -> p (s t)")[:, lo:hi])
                    t1 = _T1()
                    hs = [Buf(k, "c_hs%d" % i, [128, NT], F32) for i in range(2)]
                    fin = Buf(k, "c_fin", [128, 4, 2, NSEG], F32)
                    k.memset("g", xr[:, :, :], 0.0)
                    co, _ = PPL["lcw"]
                    cbo, _ = PPL["lcb"]
                    bao, _ = PPL["lba"]
                    bxo, _ = PPL["lbx"]
                    h0o, _ = PPL["lh0"]
                    for c in range(4):
                        px = proj(1536 + c * 128)
                        for tb in range(4):
                            k.copy("a", xr[:, 2 * tb:2 * tb + 2, 2:2 + SEG], r3(px[tb][:, :]))
                        k.memset("g", xr[:, 0, 0:2], 0.0)
                        k.memset("g", xr[:, NSEG - 1, SEG + 2:SEG + 3], 0.0)
                        k.ts("v", xr[:, 1:NSEG, 0:2], xr[:, 0:NSEG - 1, SEG:SEG + 2], flag, ALU.mult)
                        k.ts("v", xr[:, 0:NSEG - 1, SEG + 2], xr[:, 1:NSEG, 2], flag, ALU.mult)
                        w = [pp[:, co + c * 4 + j:co + c * 4 + j + 1] for j in range(4)]
                        x3 = r3(xc[:, :], NSEG)
                        k.actf(x3, xr[:, :, 2:2 + SEG], AF.Identity, scale=w[2], bias=pp[:, cbo + c:cbo + c + 1])
                        k.stt(x3, xr[:, :, 0:SEG], w[0], x3, ALU.mult, ALU.add)
                        k.stt(x3, xr[:, :, 1:1 + SEG], w[1], x3, ALU.mult, ALU.add)
                        k.stt(x3, xr[:, :, 3:3 + SEG], w[3], x3, ALU.mult, ALU.add)
                        k.copy("a", xcb[:, :], xc[:, :])
                        for d in range(2):
                            for tb in range(4):
                                sl = slice(tb * 512, (tb + 1) * 512)
                                pa = psum()
                                k.mm(pa[:, :], [(lw[:, (0 * 2 + d) * 4 + c, :], xcb[:, sl])])
                                k.actf(ra[:, sl], pa[:, :], AF.Sigmoid, bias=pp[:, bao + d * 4 + c:bao + d * 4 + c + 1])
                                pi = psum()
                                k.mm(pi[:, :], [(lw[:, (1 * 2 + d) * 4 + c, :], xcb[:, sl])])
                                k.actf(ia[:, sl], pi[:, :], AF.Sigmoid, bias=pp[:, bxo + d * 4 + c:bxo + d * 4 + c + 1])
                            k.actf(ra[:, :], ra[:, :], AF.Exp, scale=sub(nsc, d * 4 + c, d * 4 + c + 1))
                            k.tt("v", t1[:, :], ra[:, :], ra[:, :], ALU.mult)
                            k.ts("v", t1[:, :], t1[:, :], -1.0, ALU.mult, 1.0, ALU.add)
                            k.actf(t1[:, :], t1[:, :], AF.Sqrt)
                            k.tt("v", ia[:, :], ia[:, :], xc[:, :], ALU.mult)
                            k.tt("v", ia[:, :], ia[:, :], t1[:, :], ALU.mult)
                            a3 = r3(ra[:, :], NSEG)
                            h0 = pp[:, h0o + d * 4 + c:h0o + d * 4 + c + 1]
                            if d == 0:
                                rs_ = a3.re(lambda a: a[:, 1:NSEG, 0])
                                k.ts("v", rs_, rs_, flag, ALU.mult)
                                k.scan(hs[0][:, :], ra[:, :], ia[:, :], h0)
                                k.copy("v", fin[:, c, 0, :], r3(hs[0][:, :], NSEG).re(lambda a: a[:, :, SEG - 1]))
                            else:
                                rs_ = a3.re(lambda a: a[:, 0:NSEG - 1, SEG - 1])
                                k.ts("v", rs_, rs_, flag, ALU.mult)
                                rv = lambda v: v.re(lambda a: a[:, ::-1])
                                k.scan(rv(hs[1][:, :]), rv(ra[:, :]), rv(ia[:, :]), h0)
                                k.copy("v", fin[:, c, 1, :], r3(hs[1][:, :], NSEG).re(lambda a: a[:, :, 0]))
                        k.tt("v", hs[0][:, :], hs[0][:, :], hs[1][:, :], ALU.add)
                        pg = proj(2048 + c * 128)
                        for tb in range(4):
                            sl = slice(tb * 512, (tb + 1) * 512)
                            k.copy("a", xc[:, sl], pg[tb][:, :])
                            gelu_tanh("v", t1[:, sl], xc[:, sl], t1[:, sl])
                            k.tt("v", ymix[:, 4 + c, sl], hs[0][:, sl], t1[:, sl], ALU.mult)
                    for c in range(4):
                        for d in range(2):
                            k.dma(newlru_d[:, d, c * 128:(c + 1) * 128].rearrange("s p -> p s"), fin[:, c, d, :],
                                  slow=True)
                out_proj_post(l, w_out_cd_d, ymix)

        def mixer_ab(l):
            lam_init = 0.8 - 0.6 * math.exp(-0.3 * l)
            pre_norm(l, 0)
            with k.scope():
                ymix = Buf(k, "a_ymix", [128, KT, NT], BF16)
                winv = w_in_ab_d.rearrange("(kt p) c -> p kt c", p=128)
                if "s5" not in skip:
                    s5_part(ymix, winv)
                else:
                    k.memset("g", ymix[:, 0:4, :], 0.0)
                if "attn" not in skip:
                    attn_part(ymix, winv, lam_init)
                else:
                    k.memset("g", ymix[:, 4:8, :], 0.0)
                out_proj_post(l, w_out_ab_d, ymix)

        def s5_part(ymix, winv):
            dt = C("s5_dt", 32)
            mag = C("s5_mag", 32)
            ang = C("s5_ang", 32)
            c1 = C("s5_c1", 32)
            s1 = C("s5_s1", 32)
            fr = C("s5_fr", 32)
            fi = C("s5_fi", 32)
            ta = C("s5_ta", 32)
            tb_ = C("s5_tb", 32)
            tc = C("s5_tc", 32)
            lre = P("s5lre", 0, 32)
            lim = P("s5lim", 0, 32)
            k.actf(dt, P("s5ldt", 0, 32), AF.Exp)
            k.tt("v", mag, lre, dt, ALU.mult)
            k.actf(mag, mag, AF.Exp)
            k.tt("v", ang, lim, dt, ALU.mult)
            MAGIC = 12582912.0

            def sin_of(out, shift):
                k.ts("v", tb_, ang, float(shift), ALU.add)
                k.ts("v", ta, tb_, 1.0 / (2 * math.pi), ALU.mult)
                k.ts("v", ta, ta, MAGIC, ALU.add)
                k.ts("v", ta, ta, -MAGIC, ALU.add)
                k.stt(ta, ta, -2 * math.pi, tb_, ALU.mult, ALU.add)
                k.ts("v", ta, ta, math.pi, ALU.min, -math.pi, ALU.max)
                k.actf(out, ta, AF.Sin)
            sin_of(s1, 0.0)
            sin_of(c1, 0.5 * math.pi)
            k.tt("v", ta, mag, c1, ALU.mult)
            k.ts("v", ta, ta, -1.0, ALU.add)
            k.tt("v", tb_, mag, s1, ALU.mult)
            k.tt("v", tc, lre, lre, ALU.mult)
            k.tt("v", fr, lim, lim, ALU.mult)
            k.tt("v", tc, tc, fr, ALU.add)
            k.recip(tc, tc)
            k.tt("v", fr, ta, lre, ALU.mult)
            k.tt("v", fi, tb_, lim, ALU.mult)
            k.tt("v", fr, fr, fi, ALU.add)
            k.tt("v", fr, fr, tc, ALU.mult)
            k.tt("v", fi, tb_, lre, ALU.mult)
            k.tt("v", tb_, ta, lim, ALU.mult)
            k.tt("v", fi, fi, tb_, ALU.subtract)
            k.tt("v", fi, fi, tc, ALU.mult)
            cpw = C("s5_cpw", 9 * 32)
            spw = C("s5_spw", 9 * 32)
            k.copy("v", sub(cpw, 0, 32), c1)
            k.copy("v", sub(spw, 0, 32), s1)
            for m in range(1, 9):
                cp = sub(cpw, (m - 1) * 32, m * 32)
                sp_ = sub(spw, (m - 1) * 32, m * 32)
                cn = sub(cpw, m * 32, (m + 1) * 32)
                sn = sub(spw, m * 32, (m + 1) * 32)
                k.tt("v", ta, cp, cp, ALU.mult)
                k.tt("v", tb_, sp_, sp_, ALU.mult)
                k.tt("v", cn, ta, tb_, ALU.subtract)
                k.tt("v", ta, cp, sp_, ALU.mult)
                k.ts("v", sn, ta, 2.0, ALU.mult)
            c256 = C("s5_c256", 32)
            s256 = C("s5_s256", 32)
            k.ts("v", c256, sub(cpw, 8 * 32, 9 * 32), flag, ALU.mult)
            k.ts("v", s256, sub(spw, 8 * 32, 9 * 32), flag, ALU.mult)
            ns256 = C("s5_ns256", 32)
            k.ts("v", ns256, s256, -1.0, ALU.mult)
            g0r = C("s5_g0r", 32)
            g0i = C("s5_g0i", 32)
            ho, _ = PPL["s5h0"]
            h3 = pp[:, ho:ho + 64].re(lambda a: a.rearrange("p (d r j) -> p d r j", d=2, r=2))
            hr_ = h3.re(lambda a: a[:, :, 0, :])
            hi_ = h3.re(lambda a: a[:, :, 1, :])

            def v3(v):
                return v.re(lambda a: a.rearrange("p (d j) -> p d j", d=2))
            k.tt("v", ta, fr, fr, ALU.mult)
            k.tt("v", tb_, fi, fi, ALU.mult)
            k.tt("v", ta, ta, tb_, ALU.add)
            k.recip(ta, ta)
            k.tt("v", v3(tb_), hr_, v3(fr), ALU.mult)
            k.tt("v", v3(tc), hi_, v3(fi), ALU.mult)
            k.tt("v", tb_, tb_, tc, ALU.add)
            k.tt("v", g0r, tb_, ta, ALU.mult)
            k.tt("v", v3(tb_), hi_, v3(fr), ALU.mult)
            k.tt("v", v3(tc), hr_, v3(fi), ALU.mult)
            k.tt("v", tb_, tb_, tc, ALU.subtract)
            k.tt("v", g0i, tb_, ta, ALU.mult)
            k.tt("v", ta, g0r, c1, ALU.mult)
            k.tt("v", tb_, g0i, s1, ALU.mult)
            k.tt("v", tc, g0r, s1, ALU.mult)
            k.tt("v", g0r, ta, tb_, ALU.subtract)
            k.tt("v", ta, g0i, c1, ALU.mult)
            k.tt("v", g0i, ta, tc, ALU.add)

            with k.scope():
                bt = Buf(k, "s_bt", [128, 16, 128], BF16)
                k.dma(bt[:, :, :], s5bt_d.rearrange("p (a o) -> p a o", a=16), q=k.pool)
                ctb = [Buf(k, "s_ct%d" % i, [128, 16, 128], BF16) for i in range(1)]
                wu = [Buf(k, "s_wu%d" % i, [128, KT, 128], BF16) for i in range(1)]
                wus = Stream(wu, [[(lambda b: b[:, :, :], winv[:, :, c * 128:(c + 1) * 128])] for c in range(4)], 0)
                uT = Buf(k, "s_uT", [128, NT], BF16)
                ys = Buf(k, "s_ys", [128, NT], F32)
                Ec = Buf(k, "s_Ec", [128, 4, SEG + 1], F32)
                Es = Buf(k, "s_Es", [128, 4, SEG + 1], F32)
                Fc = Buf(k, "s_Fc", [128, 4, SEG], F32)
                Fs = Buf(k, "s_Fs", [128, 4, SEG], F32)
                rho = Buf(k, "s_rho", [128, 4, SEG], F32)
                carry = Buf(k, "s_carry", [128, 4, 2], F32)
                glast = Buf(k, "s_glast", [128, 4, 2, NSEG], F32)
                fin = Buf(k, "s_fin", [128, 4, 2, NSEG], F32)
                NW = 2
                bsb = [Buf(k, "s_b%d" % r, [128, 2, 2 * SEG], F32) for r in range(NW)]
                mt = [Buf(k, "s_m%d" % r, [128, 2, SEG], F32) for r in range(4)]
                gin = Buf(k, "s_gin", [128, 2, 2 * SEG], F32)
                gs = Buf(k, "s_gs", [128, 2, 2 * SEG], F32)
                hb = [Buf(k, "s_hb%d" % r, [128, 2, 2 * SEG], BF16) for r in range(1)]
                tn = Buf(k, "s_tn", [128, 8], F32)
                do_, _ = PPL["s5d"]
                for c in range(4):
                    wb = wus.get(c)
                    k.dma(ctb[0][:, :, :], s5ct_d[:, c * 2048:(c + 1) * 2048].rearrange("p (a o) -> p a o", a=16),
                          q=k.pool)
                    for tb in range(4):
                        pb = psum(4, 8)
                        k.mm(pb[:, :], [(wb[:, kt, :], hseg(kt, tb)) for kt in range(KT)])
                        sl = slice(tb * 512, (tb + 1) * 512)
                        k.actf(ys[:, sl], pb[:, :], AF.Identity, scale=pp[:, do_ + c:do_ + c + 1])
                        k.copy("a", uT[:, sl], pb[:, :])
                    for d in range(2):
                        i0 = d * 16 + 4 * c

                        def cs(v, m=None, w=None):
                            o = i0 if m is None else m * 32 + i0
                            vv = v.re(lambda a: a[:, o:o + 4])
                            if w is not None:
                                vv = vv.re(lambda a: a.unsqueeze(2).to_broadcast([128, 4, w]))
                            return vv
                        k.memset("g", Ec[:, :, 0:1], 1.0)
                        k.memset("g", Es[:, :, 0:1], 0.0)
                        k.copy("v", Ec[:, :, 1], cs(cpw, 0))
                        k.copy("v", Es[:, :, 1], cs(spw, 0))
                        e0f = mt[0][:, :, :].re(lambda a: a.rearrange("p a (b t) -> p (a b) t", b=2))
                        e1f = mt[1][:, :, :].re(lambda a: a.rearrange("p a (b t) -> p (a b) t", b=2))
                        for m in range(1, 9):
                            n = 2 ** m
                            hi = min(2 * n, SEG + 1)
                            w = hi - n
                            cm, sm = cs(cpw, m, w), cs(spw, m, w)
                            e0 = e0f.re(lambda a: a[:, :, 0:w])
                            e1 = e1f.re(lambda a: a[:, :, 0:w])
                            k.tt("v", e0, Ec[:, :, 0:w], cm, ALU.mult)
                            k.tt("v", e1, Es[:, :, 0:w], sm, ALU.mult)
                            k.tt("v", Ec[:, :, n:hi], e0, e1, ALU.subtract)
                            k.tt("v", e0, Ec[:, :, 0:w], sm, ALU.mult)
                            k.tt("v", e1, Es[:, :, 0:w], cm, ALU.mult)
                            k.tt("v", Es[:, :, n:hi], e0, e1, ALU.add)
                        for half in range(2):
                            hs_ = slice(half * 128, (half + 1) * 128)
                            fbr, fbi = cs(fr, None, 128), cs(fi, None, 128)
                            k.tt("v", e0f, Ec[:, :, hs_], fbr, ALU.mult)
                            k.tt("v", e1f, Es[:, :, hs_], fbi, ALU.mult)
                            k.tt("v", Fc[:, :, hs_], e0f, e1f, ALU.subtract)
                            k.tt("v", e0f, Ec[:, :, hs_], fbi, ALU.mult)
                            k.tt("v", e1f, Es[:, :, hs_], fbr, ALU.mult)
                            k.tt("v", Fs[:, :, hs_], e0f, e1f, ALU.add)
                        k.copy("a", rho[:, :, :], cs(mag, None, SEG))
                        k.copy("v", carry[:, :, 0], cs(g0r))
                        k.copy("v", carry[:, :, 1], cs(g0i))
                        items = [(sstep, jp) for sstep in range(NSEG) for jp in range(2)]
                        rvm = (lambda v: v) if d == 0 else (lambda v: v.re(lambda a: a[:, ::-1]))
                        lastc = SEG - 1
                        ypbs = {}

                        def segof(sstep):
                            return sstep if d == 0 else NSEG - 1 - sstep

                        def S0(i):
                            sstep, jp = items[i]
                            seg = segof(sstep)
                            sl = slice(seg * SEG, (seg + 1) * SEG)
                            bb = bsb[i % NW]
                            for q in range(2):
                                jj = 2 * jp + q
                                pbb = psum(0, 4)
                                rs = slice(32 * jj, 32 * jj + 32)
                                k.mm(pbb[:, 0:SEG], [(bt[rs, (c * 2 + d) * 2 + 0, :], rvm(uT[rs, sl]))],
                                     tile_position=(32 * jj, 0))
                                k.mm(pbb[:, SEG:2 * SEG], [(bt[rs, (c * 2 + d) * 2 + 1, :], rvm(uT[rs, sl]))],
                                     tile_position=(32 * jj, 0))
                                k.copy("a", bb[:, q, :], pbb[:, :])

                        def S1(i):
                            sstep, jp = items[i]
                            seg = segof(sstep)
                            sl = slice(seg * SEG, (seg + 1) * SEG)
                            bb = bsb[i % NW]
                            hbb = hb[0]
                            br, bi = bb[:, :, 0:SEG], bb[:, :, SEG:2 * SEG]
                            gir, gii = gin[:, :, 0:SEG], gin[:, :, SEG:2 * SEG]
                            gsr, gsi = gs[:, :, 0:SEG], gs[:, :, SEG:2 * SEG]
                            m1, m2 = mt[0][:, :, :], mt[1][:, :, :]

                            def T2(T):
                                return T[:, 2 * jp:2 * jp + 2, 0:SEG]
                            m3, m4 = mt[2][:, :, :], mt[3][:, :, :]
                            k.tt("v", m1, br, T2(Ec), ALU.mult)
                            k.tt("v", m2, bi, T2(Es), ALU.mult)
                            k.tt("v", m3, bi, T2(Ec), ALU.mult)
                            k.tt("v", m4, br, T2(Es), ALU.mult)
                            k.tt("v", gir, m1, m2, ALU.add)
                            k.tt("v", gii, m3, m4, ALU.subtract)
                            for q in range(2):
                                jj = 2 * jp + q
                                cr_ = carry.at(jj, (slice(None), jj, slice(0, 1)))
                                ci_ = carry.at(jj, (slice(None), jj, slice(1, 2)))
                                k.scan(gs.at(q, (slice(None), q, slice(0, SEG))), rho[:, jj, :], gin[:, q, 0:SEG], cr_)
                                k.scan(gs.at(q, (slice(None), q, slice(SEG, 2 * SEG))), rho[:, jj, :],
                                       gin[:, q, SEG:2 * SEG], ci_)
                                glr = gs.at(q, (slice(None), q, slice(lastc, lastc + 1)))
                                gli = gs.at(q, (slice(None), q, slice(SEG + lastc, SEG + lastc + 1)))
                                k.copy("a", glast.at(jj, (slice(None), jj, 0, slice(seg, seg + 1))), glr)
                                k.copy("a", glast.at(jj, (slice(None), jj, 1, slice(seg, seg + 1))), gli)
                                cc = sub(c256, i0 + jj, i0 + jj + 1)
                                sc_ = sub(s256, i0 + jj, i0 + jj + 1)
                                nsc_ = sub(ns256, i0 + jj, i0 + jj + 1)
                                t_a = tn.at(jj, (slice(None), slice(2 * jj, 2 * jj + 1)))
                                t_b = tn.at(jj, (slice(None), slice(2 * jj + 1, 2 * jj + 2)))
                                k.actf(t_a, gli, AF.Identity, scale=nsc_)
                                k.actf(t_b, glr, AF.Identity, scale=sc_)
                                k.actf(cr_, glr, AF.Identity, scale=cc, bias=t_a)
                                k.actf(ci_, gli, AF.Identity, scale=cc, bias=t_b)
                            k.tt("v", m1, gsr, T2(Fc), ALU.mult)
                            k.tt("v", m2, gsi, T2(Fs), ALU.mult)
                            k.tt("v", m3, gsi, T2(Fc), ALU.mult)
                            k.tt("v", m4, gsr, T2(Fs), ALU.mult)
                            k.tt("v", hbb[:, :, 0:SEG], m1, m2, ALU.subtract)
                            k.stt(hbb[:, :, SEG:2 * SEG], m3, -1.0, m4, ALU.mult, ALU.subtract)
                            if jp == 0:
                                ypbs[sstep] = psum(4, 8)
                            ypb = ypbs[sstep]
                            for q in range(2):
                                jj = 2 * jp + q
                                for ri in range(2):
                                    cv_ = ctb[0][:, (jj * 2 + d) * 2 + ri, :]
                                    first = (jj == 0 and ri == 0)
                                    last = (jj == 3 and ri == 1)
                                    hsrc = rvm(hbb[:, q, ri * SEG:(ri + 1) * SEG])
                                    k.emit(k.pe, (lambda cv_=cv_, hsrc=hsrc, first=first, last=last, ypb=ypb:
                                                  k.pe.h.matmul(ypb[:, 0:SEG].ap, cv_.ap, hsrc.ap, start=first, stop=last)),
                                           [cv_, hsrc], [ypb[:, 0:SEG]])
                            if jp == 1:
                                k.tt("v", ys[:, sl], ys[:, sl], ypb[:, 0:SEG], ALU.add)
                                del ypbs[sstep]

                        nI = len(items)
                        for t in range(nI + 1):
                            if t >= 1:
                                S1(t - 1)
                            if t < nI:
                                S0(t)
                        lastt = SEG - 1
                        Fcl = Fc[:, :, lastt:lastt + 1].re(lambda a: a.to_broadcast([128, 4, NSEG]))
                        Fsl = Fs[:, :, lastt:lastt + 1].re(lambda a: a.to_broadcast([128, 4, NSEG]))
                        e0 = e0f.re(lambda a: a[:, :, 0:NSEG])
                        e1 = e1f.re(lambda a: a[:, :, 0:NSEG])
                        k.tt("v", e0, glast[:, :, 0, :], Fcl, ALU.mult)
                        k.tt("v", e1, glast[:, :, 1, :], Fsl, ALU.mult)
                        k.tt("v", fin[:, :, 0, :], e0, e1, ALU.subtract)
                        k.tt("v", e0, glast[:, :, 1, :], Fcl, ALU.mult)
                        k.tt("v", e1, glast[:, :, 0, :], Fsl, ALU.mult)
                        k.tt("v", fin[:, :, 1, :], e0, e1, ALU.add)
                        for jj in range(4):
                            j = 4 * c + jj
                            for ri in range(2):
                                for g2 in range(2):
                                    k.dma(news5_d[:, d, ri, 2 * j + g2, :].rearrange("s n -> n s"),
                                          fin[64 * g2:64 * g2 + 64, jj, ri, :], slow=True)
                    for tb in range(4):
                        sl = slice(tb * 512, (tb + 1) * 512)
                        gelu_tanh("v", ymix[:, c, sl], ys[:, sl], None)
            with k.scope():
                wg = Buf(k, "s_wg", [128, 4, 512], BF16)
                k.dma(wg[:, :, :], w_glu_d.rearrange("(kt p) o -> p kt o", p=128), q=k.pool)
                bgo, _ = PPL["bglu"]
                zs = Buf(k, "s_zs", [128, 4, NT], BF16)
                for co_ in range(4):
                    for tb in range(4):
                        sl = slice(tb * 512, (tb + 1) * 512)
                        pb = psum()
                        k.mm(pb[:, :], [(wg[:, ci, co_ * 128:(co_ + 1) * 128], ymix[:, ci, sl]) for ci in range(4)])
                        k.actf(zs[:, co_, sl], pb[:, :], AF.Sigmoid, bias=pp[:, bgo + co_:bgo + co_ + 1])
                for co_ in range(4):
                    k.tt("v", ymix[:, co_, :], ymix[:, co_, :], zs[:, co_, :], ALU.mult)

        def attn_part(ymix, winv, lam_init):
            dlo, _ = PPL["dalam"]
            nlam = C("da_nlam", 1)
            e01 = C("da_e01", 1)
            e23 = C("da_e23", 1)
            lt = C("da_lt", 128)
            k.tt("v", sub(lt, 0, 64), pp[:, dlo:dlo + 64], pp[:, dlo + 64:dlo + 128], ALU.mult)
            k.tt("v", sub(lt, 64, 128), pp[:, dlo + 128:dlo + 192], pp[:, dlo + 192:dlo + 256], ALU.mult)
            k.reduce_sum(e01, sub(lt, 0, 64))
            k.reduce_sum(e23, sub(lt, 64, 128))
            k.actf(e01, e01, AF.Exp)
            k.actf(e23, e23, AF.Exp)
            k.tt("v", nlam, e23, e01, ALU.subtract)
            k.ts("v", nlam, nlam, -lam_init, ALU.add)
            gq = C("da_gq", 1)
            k.ts("v", gq, P("dag", 0), float((1.0 - lam_init) * math.sqrt(128.0)), ALU.mult)
            with k.scope():
                Vall = Buf(k, "t_V", [128, 20, 512], BF16)
                KTc = Buf(k, "t_KTc", [128, 4, 512], BF16)
                with k.scope():
                    wkvs = [Buf(k, "t_wkv%d" % i, [128, KT, 256], BF16) for i in range(4)]
                    for part in range(4):
                        k.dma(wkvs[part][:, :, :], winv[:, :, 1024 + part * 256:1024 + (part + 1) * 256], q=k.pool)
                    k.dma(Vall[:, 16:20, :], vc_d.rearrange("(a p) f -> p a f", p=128), q=k.pool)
                    kcs = Buf(k, "t_kcs", [128, 4, 512], F32)
                    k.dma(kcs[:, :, :], kc_d.rearrange("(a p) f -> p a f", p=128))
                    kst = [Buf(k, "t_kst%d" % i, [128, 512], F32) for i in range(2)]
                    vst = [Buf(k, "t_vst%d" % i, [128, 512], F32) for i in range(2)]
                    for h in range(4):
                        pb = psum()
                        for a in range(4):
                            k.transpose(pb[:, a * 128:(a + 1) * 128], kcs[:, a, h * 128:(h + 1) * 128], ident[:, :])
                        k.copy("a", KTc[:, h, :], pb[:, :])
                    for tt_ in range(16):
                        seg, off = tt_ // 2, 1 + (tt_ % 2) * 128
                        pk_ = psum()
                        pv_ = psum()
                        for part in range(4):
                            dstp = (pk_ if part < 2 else pv_)[:, (part % 2) * 256:(part % 2 + 1) * 256]
                            k.mm(dstp, [(hT[:, kt, seg, off:off + 128], wkvs[part][:, kt, :]) for kt in range(KT)])
                        ks_, vs_ = kst[tt_ % 2], vst[tt_ % 2]
                        k.copy("a", ks_[:, :], pk_[:, :])
                        k.copy("v", vs_[:, :], pv_[:, :])
                        k.dma(newk_d[tt_ * 128:(tt_ + 1) * 128, :], ks_[:, :])
                        k.dma(newv_d[tt_ * 128:(tt_ + 1) * 128, :], vs_[:, :])
                        k.copy("a", Vall[:, tt_, :], vs_[:, :])
                ropec = Buf(k, "t_ropec", [128, NT], F32)
                ropes = Buf(k, "t_ropes", [128, NT], F32)
                perm = Buf(k, "t_perm", [128, 128], BF16)
                k.dma(ropec[:, :], ropec_d)
                k.dma(ropes[:, :], ropes_d)
                k.dma(perm[:, :], perm_d, q=k.pool)
                QT = Buf(k, "t_QT", [128, NT], BF16)
                KTb = Buf(k, "t_KT", [128, NT], BF16)
                wq = [Buf(k, "t_wq%d" % i, [128, KT, 128], BF16) for i in range(2)]
                qorder = []
                for h in range(4):
                    qorder += [512 + h * 128, 1024 + h * 128]
                wqs = Stream(wq, [[(lambda b: b[:, :, :], winv[:, :, c0:c0 + 128])] for c0 in qorder], 1)
                qraw = Buf(k, "t_qraw", [128, 512], BF16)
                rt = [Buf(k, "t_rt%d" % i, [128, 512], F32) for i in range(4)]
                Pm = [Buf(k, "t_P%d" % i, [128, 512], BF16) for i in range(6)]
                osq = Buf(k, "t_osq", [128, 512], BF16)
                abo, _ = PPL["abias"]
                ip = [0]
                for h in range(4 if "noheads" not in skip else 0):
                    for which, dst in ((0, QT), (1, KTb)):
                        wb = wqs.get(2 * h + which)
                        for tb in range(4):
                            sl = slice(tb * 512, (tb + 1) * 512)
                            pb = psum(4, 8)
                            k.mm(pb[:, :], [(wb[:, kt, :], hseg(kt, tb)) for kt in range(KT)])
                            k.copy("a", qraw[:, :], pb[:, :])
                            pb2 = psum(4, 8)
                            k.mm(pb2[:, :], [(perm[:, :], qraw[:, :])])
                            t1, t2 = rt[(tb % 2) * 2], rt[(tb % 2) * 2 + 1]
                            k.tt("v", t1[:, :], pb[:, :], ropec[:, sl], ALU.mult)
                            k.tt("v", t2[:, :], pb2[:, :], ropes[:, sl], ALU.mult)
                            k.tt("v", dst[:, sl], t1[:, :], t2[:, :], ALU.add)
                    for qsb in range(8 if "noattn" not in skip else 0):
                        qs = slice(qsb * SEG, (qsb + 1) * SEG)
                        acc = [ps[0], ps[1], ps[2], ps[3]]
                        items = list(range(10))
                        LAG = 1
                        pms = {}

                        def kview(kt_, rs):
                            if kt_ < 16:
                                return KTb[rs, kt_ * 128:(kt_ + 1) * 128]
                            return KTc[rs, h, (kt_ - 16) * 128:(kt_ - 15) * 128]

                        def stage1(i):
                            ktp = items[i]
                            sps = [psum(4, 8), psum(4, 8)]
                            for j2 in range(2):
                                for m in range(2):
                                    rs = slice(64 * m, 64 * m + 64)
                                    k.mm(sps[m][:, j2 * SEG:(j2 + 1) * SEG], [(kview(2 * ktp + j2, rs), QT[rs, qs])],
                                         tile_position=(64 * m, 0))
                            col = abo + (2 * ktp) * 8 + qsb
                            cur = []
                            for m in range(2):
                                pm = Pm[ip[0] % len(Pm)]
                                ip[0] += 1
                                k.actf(pm[:, :], sps[m][:, :], AF.Exp, bias=pp[:, col:col + 1], scale=0.125)
                                cur.append(pm)
                            pms[i] = cur

                        def stage2(i):
                            ktp = items[i]
                            cur = pms.pop(i)
                            for m in range(2):
                                pm = cur[m]
                                for j2 in range(2):
                                    kt_ = 2 * ktp + j2
                                    vt = Vall[:, kt_, h * 128:(h + 1) * 128]
                                    pv = pm[:, j2 * SEG:(j2 + 1) * SEG]
                                    k.emit(k.pe, (lambda vt=vt, pv=pv, kt_=kt_, m=m: k.pe.h.matmul(
                                        acc[m][:, 0:SEG].ap, vt.ap, pv.ap, start=(kt_ == 0), stop=(kt_ == 19))),
                                        [vt, pv], [acc[m][:, 0:SEG]])
                            for m in range(2):
                                pm = cur[m]
                                k.emit(k.pe, (lambda pm=pm, m=m: k.pe.h.matmul(
                                    acc[2 + m][:, :].ap, onesb[:, :].ap, pm[:, :].ap, start=(ktp == 0), stop=(ktp == 9))),
                                    [onesb[:, :], pm[:, :]], [acc[2 + m][:, :]])
                        for i in range(len(items) + LAG):
                            if i < len(items):
                                stage1(i)
                            if i >= LAG:
                                stage2(i - LAG)
                        r0, r1, t0, t1 = (rt[i][:, 0:SEG] for i in range(4))
                        k.copy("a", r0, acc[2][:, SEG:2 * SEG])
                        k.copy("a", r1, acc[3][:, SEG:2 * SEG])
                        k.tt("v", r0, acc[2][:, 0:SEG], r0, ALU.add)
                        k.tt("v", r1, acc[3][:, 0:SEG], r1, ALU.add)
                        k.recip(r0, r0)
                        k.recip(r1, r1)
                        k.tt("v", t0, acc[0][:, 0:SEG], r0, ALU.mult)
                        k.tt("v", t1, acc[1][:, 0:SEG], r1, ALU.mult)
                        k.stt(t0, t1, nlam, t0, ALU.mult, ALU.add)
                        k.actf(osq[:, 0:SEG], t0, AF.Square)
                        pb = psum(4, 8)
                        k.mm(pb[:, 0:SEG], [(onesb[:, :], osq[:, 0:SEG])])
                        rsqrt_eps(r0, pb[:, 0:SEG], 128 * EPS)
                        k.tt("v", t0, t0, r0, ALU.mult)
                        k.actf(ymix[:, 4 + h, qs], t0, AF.Identity, scale=gq)

        for ph in phases:
            if ph == "ab":
                mixer_ab(0)
            elif ph == "ffn0":
                ffn(0)
            elif ph == "cd":
                mixer_cd(1)
            elif ph == "ffn1":
                ffn(1)

        with k.scope():
            yst = [Buf(k, "yst%d" % i, [128, D], F32) for i in range(3)]
            for tt_ in range(16):
                st = yst[tt_ % 3]
                for half in range(2):
                    pb = psum()
                    for a in range(4):
                        kt = half * 4 + a
                        k.transpose(pb[:, a * 128:(a + 1) * 128], xT[:, kt, tt_ * 128:(tt_ + 1) * 128], ident[:, :])
                    k.copy("v" if half == 0 else "a", st[:, half * 512:(half + 1) * 512], pb[:, :])
                k.dma(y_d[tt_ * 128:(tt_ + 1) * 128, :], st[:, :])
        k.finish()
    return k


def _rope_tables():
    GRID_W, ROPE_F, THETA = 64, 16, 10000.0
    t = np.arange(NT)
    t_row = (t // GRID_W).astype(np.float32)
    t_col = (t % GRID_W).astype(np.float32)
    inv = (np.float32(THETA) ** (-np.arange(ROPE_F, dtype=np.float32) / np.float32(ROPE_F))).astype(np.float32)
    ang = np.stack([t_row[:, None] * inv, t_col[:, None] * inv], axis=1).astype(np.float32)
    cos, sin = np.cos(ang).astype(np.float32), np.sin(ang).astype(np.float32)
    rc = np.zeros((128, NT), np.float32)
    rs = np.zeros((128, NT), np.float32)
    for m in range(2):
        for a in range(2):
            for hh in range(2):
                for f in range(16):
                    r = m * 64 + a * 32 + hh * 16 + f
                    rc[r] = cos[:, a, f]
                    rs[r] = sin[:, a, f] * (-1.0 if hh == 0 else 1.0)
    return rc, rs


def _make_inputs(inputs):
    f = lambda n: np.asarray(inputs[n], dtype=np.float32)
    x_prompt, x_sample = f("x_prompt"), f("x_sample")
    ck, cv = f("cache_attn_k"), f("cache_attn_v")
    st_s5, st_lru = f("state_s5"), f("state_rglru")
    c, c_ctx = f("c"), f("c_ctx")
    ident = np.eye(128, dtype=np.float32)
    perm = np.zeros((128, 128), np.float32)
    for r in range(128):
        perm[r ^ 16, r] = 1.0
    rc, rs = _rope_tables()
    ones_c, zeros_s = np.ones((128, NT), np.float32), np.zeros((128, NT), np.float32)
    s5b = f("s5_b")[0]
    s5c = f("s5_c")[0]
    bt = np.zeros((128, 4, 2, 2, 128), np.float32)
    ctm = np.zeros((128, 16, 2, 2, 128), np.float32)
    for cc in range(4):
        for jj in range(4):
            j = 4 * cc + jj
            for g2 in range(2):
                g = 2 * j + g2
                for d in range(2):
                    for ri in range(2):
                        bt[32 * jj + 16 * g2:32 * jj + 16 * g2 + 16, cc, d, ri, 64 * g2:64 * g2 + 64] = s5b[d, ri, g].T
                        ctm[64 * g2:64 * g2 + 64, j, d, ri, 32 * jj + 16 * g2:32 * jj + 16 * g2 + 16] = s5c[d, ri, g].T
    lwa, lwx = f("lru_w_a")[0], f("lru_w_x")[0]
    lruw = np.zeros((128, 2, 2, 4, 128), np.float32)
    for ax, wsrc in enumerate((lwa, lwx)):
        for d in range(2):
            for cc in range(4):
                for k2 in range(2):
                    lruw[64 * k2:64 * k2 + 64, ax, d, cc, 64 * k2:64 * k2 + 64] = wsrc[d, 2 * cc + k2]
    shared = {
        "ident": ident, "perm": perm,
        "w_mod": f("w_mod"), "w_in_ab": f("w_in_ab")[0], "w_out_ab": f("w_out_ab")[0],
        "s5bt": bt.reshape(128, -1), "s5ct": ctm.reshape(128, -1), "w_glu": f("s5_w_glu")[0],
        "w_in_cd": f("w_in_cd")[0], "w_out_cd": f("w_out_cd")[0], "lruw": lruw.reshape(128, -1),
        "w_up": f("ffn_w_up"), "w_down": f("ffn_w_down"),
    }
    in_maps = []
    for core in range(8):
        prompt = core < 4
        if prompt:
            x = x_prompt[8 * core:8 * core + 8].reshape(NT, D)
            cond, flagS = c_ctx, 0.0
            kc = np.zeros((512, 512), np.float32)
            vc = np.zeros((512, 512), np.float32)
            s5h0 = np.zeros((2, 2, 32, 64), np.float32)
            lh0 = np.zeros((2, 512), np.float32)
            ropec, ropes = ones_c, zeros_s
            ab = np.full((20, 8), -30000.0, np.float32)
            for kt_ in range(16):
                ab[kt_, kt_ // 2] = 0.0
        else:
            b = core - 4
            x = x_sample[b]
            cond, flagS = c[b], 1.0
            kc = ck[b, 0].reshape(512, 512)
            vc = cv[b, 0].reshape(512, 512)
            s5h0 = st_s5[b, 0]
            lh0 = st_lru[b, 0]
            ropec, ropes = rc, rs
            ab = np.zeros((20, 8), np.float32)
        abias = np.broadcast_to(ab.reshape(1, 160), (128, 160))
        pp = host_pack(cond, flagS, f("b_mod"), f("norm_g"), f("s5_lam_re"), f("s5_lam_im"), f("s5_log_dt"), s5h0,
                       f("s5_d"), f("s5_b_glu"), f("da_g"), f("da_lam"), f("sc_conv_w"), f("lru_conv_w"),
                       f("lru_conv_b"), f("lru_b_a"), f("lru_b_x"), f("lru_lam"), lh0, f("ffn_conv_w"),
                       f("ffn_conv_b"), abias)
        m = dict(shared)
        m.update({"x": np.ascontiguousarray(x), "pp": pp, "ropec": ropec, "ropes": ropes,
                  "kc": np.ascontiguousarray(kc), "vc": np.ascontiguousarray(vc)})
        in_maps.append(m)
    return in_maps


_PROG = {}


def run(inputs, phases=("ab", "ffn0", "cd", "ffn1"), skip=(), trace=False):
    key = (tuple(phases), tuple(skip))
    if key not in _PROG:
        _PROG[key] = build_program(phases, skip)
    k = _PROG[key]
    in_maps = _make_inputs(inputs)
    res = run_bass_kernel_spmd(k.nc, in_maps, core_ids=list(range(8)), trace=trace)
    return res


def kernel(**inputs):
    res = run(inputs)
    r = res.results
    y_prompt = np.concatenate([r[c]["y"].reshape(8, 256, D) for c in range(4)], axis=0)
    y_sample = np.stack([r[c]["y"] for c in range(4, 8)], axis=0)
    newk = np.concatenate([r[c]["newk"].reshape(8, 256, 4, 2, 64) for c in range(4)], axis=0)[:, None]
    newv = np.concatenate([r[c]["newv"].reshape(8, 256, 4, 128) for c in range(4)], axis=0)[:, None]
    news5 = np.concatenate([r[c]["news5"] for c in range(4)], axis=0)[:, None]
    newlru = np.concatenate([r[c]["newlru"] for c in range(4)], axis=0)[:, None]
    return (y_prompt.astype(np.float32), y_sample.astype(np.float32), newk.astype(np.float32),
            newv.astype(np.float32), news5.astype(np.float32), newlru.astype(np.float32))
```

```python
import contextlib
import math
import numpy as np
import concourse.bass as bass
import concourse.mybir as mybir
from concourse.bass_utils import run_bass_kernel_spmd

F32 = mybir.dt.float32
BF16 = mybir.dt.bfloat16
ALU = mybir.AluOpType
AF = mybir.ActivationFunctionType

NT = 2048
D = 1024
KT = 8
NSEG = 8
SEG = 256
SEGW = SEG + 2
EPS = 1e-6
DFF = 2816
NCT = 22
SEM_MAX = 24000


class _State:
    __slots__ = ("w", "r")

    def __init__(self):
        self.w = None
        self.r = {}


class Buf:
    def __init__(self, k, name, shape, dtype, space="sbuf"):
        self.k = k
        k.nbuf += 1
        name = "b%d_%s" % (k.nbuf, name)
        self.name = name
        self.psum = space != "sbuf"
        if space == "sbuf":
            self.h = k.es_cur.enter_context(k.nc.sbuf_tensor(name, list(shape), dtype))
        else:
            self.h = k.es_cur.enter_context(k.nc.psum_tensor(name, list(shape), dtype))
        self.states = {None: _State()}
        self.states[None].r = dict(k.freed)
        k.live.setdefault(id(k.es_cur), []).append(self)
        self.dsem = None
        self.dcnt = 0

    def __getitem__(self, idx):
        return V(self.h[idx], self, None)

    def at(self, key, idx):
        return V(self.h[idx], self, key)

    def states_for(self, key):
        if key is None:
            return list(self.states.values())
        if key not in self.states:
            self.states[key] = _State()
        return [self.states[key], self.states[None]]

    def states_upd(self, key):
        if key is None:
            return list(self.states.values())
        if key not in self.states:
            self.states[key] = _State()
        return [self.states[key]]


class V:
    __slots__ = ("ap", "buf", "key")

    def __init__(self, ap, buf, key):
        self.ap = ap
        self.buf = buf
        self.key = key

    def re(self, fn):
        return V(fn(self.ap), self.buf, self.key)


class Eng:
    def __init__(self, k, name, h):
        self.k = k
        self.name = name
        self.h = h
        self.sem = None
        self.cnt = 0
        self.nsem = 0
        self.waited = {}
        self.own = set()

    def next_event(self):
        if self.sem is None or self.cnt >= SEM_MAX:
            self.sem = self.k.new_sem("e_%s_%d" % (self.name, self.nsem))
            self.own.add(id(self.sem))
            self.nsem += 1
            self.cnt = 0
        self.cnt += 1
        return (self.sem, self.cnt)

    def wait(self, ev):
        sem, val = ev
        key = id(sem)
        if self.name == "pe" and key in self.own:
            return
        if self.waited.get(key, 0) < val:
            self.h.wait_ge(sem, val)
            self.waited[key] = val


class K:
    def __init__(self):
        self.nc = bass.Bass("TRN2", target_bir_lowering=False)
        self.es = contextlib.ExitStack()
        self.es_cur = self.es
        self.sems = []
        nc = self.nc
        self.pe = Eng(self, "pe", nc.tensor)
        self.dve = Eng(self, "dve", nc.vector)
        self.act = Eng(self, "act", nc.scalar)
        self.pool = Eng(self, "pool", nc.gpsimd)
        self.sp = Eng(self, "sp", nc.sync)
        self.dma_bufs = []
        self.nsem = 0
        self.nbuf = 0
        self.freed = {}
        self.live = {}

    def new_sem(self, name):
        s = self.es.enter_context(self.nc.semaphore(name))
        self.nsem += 1
        return s

    def _deps(self, reads, writes, eng=None):
        deps = {}
        own = eng.own if (eng is not None and eng.name in ("dve", "act")) else ()

        def add(ev):
            key = id(ev[0])
            if key not in deps or deps[key][1] < ev[1]:
                deps[key] = ev

        for v in reads:
            for st in v.buf.states_for(v.key):
                if st.w is not None:
                    add(st.w)
                if v.buf.psum:
                    for ev in st.r.values():
                        add(ev)
        for v in writes:
            for st in v.buf.states_for(v.key):
                if st.w is not None:
                    add(st.w)
                for ev in st.r.values():
                    if id(ev[0]) in own and not v.buf.psum:
                        continue
                    add(ev)
        return deps

    def _update(self, ev, reads, writes):
        for v in reads:
            for st in v.buf.states_upd(v.key):
                key = id(ev[0])
                if key not in st.r or st.r[key][1] < ev[1]:
                    st.r[key] = ev
        for v in writes:
            for st in v.buf.states_upd(v.key):
                st.w = ev
                st.r = {}

    def emit(self, eng, fn, reads, writes):
        deps = self._deps(reads, writes, eng)
        for ev in deps.values():
            eng.wait(ev)
        ins = fn()
        ev = eng.next_event()
        ins.then_inc(ev[0], 1)
        self._update(ev, reads, writes)

    def emit_group(self, eng, fns, reads, writes):
        deps = self._deps(reads, writes)
        for ev in deps.values():
            eng.wait(ev)
        ins = None
        for fn in fns:
            ins = fn()
        ev = eng.next_event()
        ins.then_inc(ev[0], 1)
        self._update(ev, reads, writes)

    def dma(self, out, in_, q=None, slow=False):
        q = q or self.sp
        sb = out if isinstance(out, V) else in_
        is_load = isinstance(out, V)
        buf = sb.buf
        if buf.dsem is None:
            buf.dsem = self.new_sem("d_" + buf.name)
            self.dma_bufs.append(buf)
        reads, writes = ([], [sb]) if is_load else ([sb], [])
        deps = self._deps(reads, writes)
        for ev in deps.values():
            q.wait(ev)
        oa = out.ap if isinstance(out, V) else out
        ia = in_.ap if isinstance(in_, V) else in_
        if slow:
            ins = q.h.dma_start(out=oa, in_=ia, allow_slow_non_contiguous=True)
        else:
            ins = q.h.dma_start(out=oa, in_=ia)
        buf.dcnt += 16
        ins.then_inc(buf.dsem, 16)
        ev = (buf.dsem, buf.dcnt)
        self._update(ev, reads, writes)

    def barrier(self):
        evs = []
        for e in (self.pe, self.dve, self.act, self.pool):
            if e.sem is not None:
                evs.append((e.sem, e.cnt))
        for b in self.dma_bufs:
            evs.append((b.dsem, b.dcnt))
        for e in (self.pe, self.dve, self.act, self.pool, self.sp):
            for ev in evs:
                e.wait(ev)

    @contextlib.contextmanager
    def scope(self):
        prev = self.es_cur
        with contextlib.ExitStack() as s:
            self.es_cur = s
            try:
                yield s
            finally:
                for b in self.live.pop(id(s), []):
                    for st in b.states.values():
                        evs = list(st.r.values()) + ([st.w] if st.w is not None else [])
                        for ev in evs:
                            key = id(ev[0])
                            if key not in self.freed or self.freed[key][1] < ev[1]:
                                self.freed[key] = ev
                self.es_cur = prev

    def finish(self):
        for b in self.dma_bufs:
            self.sp.wait((b.dsem, b.dcnt))
        for e in (self.pe, self.dve, self.act, self.pool):
            if e.sem is not None:
                self.sp.wait((e.sem, e.cnt))

    def _eng(self, e):
        return {"v": self.dve, "g": self.pool, "a": self.act}[e]

    def tt(self, e, out, in0, in1, op):
        eng = self._eng(e)
        self.emit(eng, lambda: eng.h.tensor_tensor(out=out.ap, in0=in0.ap, in1=in1.ap, op=op),
                  [in0, in1], [out])

    def ts(self, e, out, in0, s1, op0, s2=None, op1=None):
        eng = self._eng(e)
        reads = [in0]
        a1 = s1
        if isinstance(s1, V):
            reads.append(s1)
            a1 = s1.ap
        a2 = s2
        if isinstance(s2, V):
            reads.append(s2)
            a2 = s2.ap
        if op1 is None:
            fn = lambda: eng.h.tensor_scalar(out=out.ap, in0=in0.ap, scalar1=a1, scalar2=None, op0=op0)
        else:
            fn = lambda: eng.h.tensor_scalar(out=out.ap, in0=in0.ap, scalar1=a1, scalar2=a2, op0=op0, op1=op1)
        self.emit(eng, fn, reads, [out])

    def stt(self, out, in0, scalar, in1, op0, op1):
        eng = self.dve
        reads = [in0, in1]
        sc = scalar
        if isinstance(scalar, V):
            reads.append(scalar)
            sc = scalar.ap
        self.emit(eng, lambda: eng.h.scalar_tensor_tensor(out=out.ap, in0=in0.ap, scalar=sc, in1=in1.ap,
                                                          op0=op0, op1=op1), reads, [out])

    def scan(self, out, d0, d1, init):
        eng = self.dve
        reads = [d0, d1]
        ini = init
        if isinstance(init, V):
            reads.append(init)
            ini = init.ap
        self.emit(eng, lambda: eng.h.tensor_tensor_scan(out=out.ap, data0=d0.ap, data1=d1.ap, initial=ini,
                                                        op0=ALU.mult, op1=ALU.add), reads, [out])

    def copy(self, e, out, in_):
        if e == "a":
            return self.actf(out, in_, AF.Copy)
        eng = self._eng(e)
        self.emit(eng, lambda: eng.h.tensor_copy(out=out.ap, in_=in_.ap), [in_], [out])

    def memset(self, e, out, val):
        eng = self._eng(e)
        self.emit(eng, lambda: eng.h.memset(out.ap, val), [], [out])

    def recip(self, out, in_):
        eng = self.dve
        self.emit(eng, lambda: eng.h.reciprocal(out=out.ap, in_=in_.ap), [in_], [out])

    def reduce_sum(self, out, in_):
        eng = self.dve
        self.emit(eng, lambda: eng.h.reduce_sum(out=out.ap, in_=in_.ap, axis=mybir.AxisListType.X), [in_], [out])

    def actf(self, out, in_, func, bias=None, scale=None):
        eng = self.act
        reads = [in_]
        kw = {}
        if bias is not None:
            if isinstance(bias, V):
                reads.append(bias)
                kw["bias"] = bias.ap
            else:
                kw["bias"] = bias
        if scale is not None:
            if isinstance(scale, V):
                reads.append(scale)
                kw["scale"] = scale.ap
            else:
                kw["scale"] = scale
        self.emit(eng, lambda: eng.h.activation(out=out.ap, in_=in_.ap, func=func, **kw), reads, [out])

    def mm(self, out, pairs, tile_position=None):
        eng = self.pe
        n = len(pairs)
        fns = []
        reads = []
        for i, (l, r) in enumerate(pairs):
            reads += [l, r]
            kw = {}
            if tile_position is not None:
                kw["tile_position"] = tile_position

            def fn(l=l, r=r, i=i, kw=kw):
                return eng.h.matmul(out.ap, l.ap, r.ap, start=(i == 0), stop=(i == n - 1), **kw)
            fns.append(fn)
        self.emit_group(eng, fns, reads, [out])

    def transpose(self, out, in_, ident):
        eng = self.pe
        self.emit(eng, lambda: eng.h.transpose(out.ap, in_.ap, ident.ap), [in_, ident], [out])


class Pack:
    def __init__(self):
        self.cols = {}
        self.n = 0
        self.parts = []

    def add(self, name, arr):
        arr = np.ascontiguousarray(arr, dtype=np.float32).reshape(128, -1)
        self.cols[name] = (self.n, arr.shape[1])
        self.n += arr.shape[1]
        self.parts.append(arr)

    def build(self):
        return np.concatenate(self.parts, axis=1)


def _pp_layout():
    L = {}
    n = 0
    for name, w in [("cond", 8), ("flag", 2), ("bmod", 96), ("ng", 64),
                    ("s5lre", 32), ("s5lim", 32), ("s5ldt", 32), ("s5h0", 64), ("s5d", 4), ("bglu", 4),
                    ("dag", 1), ("dalam", 256),
                    ("scw", 12), ("lcw", 16), ("lcb", 4), ("lba", 8), ("lbx", 8), ("llam", 8), ("lh0", 8),
                    ("fcw", 2 * 44 * 3), ("fcb", 2 * 44), ("abias", 160)]:
        L[name] = (n, w)
        n += w
    return L, n


PPL, PPN = _pp_layout()


def host_pack(cond, flagS, b_mod, norm_g, s5_lam_re, s5_lam_im, s5_log_dt, s5h0, s5_d, s5_b_glu, da_g, da_lam,
              sc_conv_w, lru_conv_w, lru_conv_b, lru_b_a, lru_b_x, lru_lam, lruh0, ffn_conv_w, ffn_conv_b, abias):
    pk = Pack()
    pk.add("cond", cond.reshape(8, 128).T)
    pk.add("flag", np.full((128, 2), flagS, np.float32))
    pk.add("bmod", b_mod.reshape(2, 48, 128).transpose(2, 0, 1))
    pk.add("ng", norm_g.reshape(2, 4, 8, 128).transpose(3, 0, 1, 2))

    def st(a):
        return a.reshape(2, 16, 2, 64).transpose(2, 3, 0, 1).reshape(128, 32)
    pk.add("s5lre", st(s5_lam_re[0]))
    pk.add("s5lim", st(s5_lam_im[0]))
    pk.add("s5ldt", st(np.broadcast_to(s5_log_dt[0][:, :, None], (2, 32, 64))))
    pk.add("s5h0", s5h0.reshape(2, 2, 16, 2, 64).transpose(3, 4, 0, 1, 2).reshape(128, 64))
    pk.add("s5d", s5_d[0].reshape(4, 128).T)
    pk.add("bglu", s5_b_glu[0].reshape(4, 128).T)
    pk.add("dag", da_g[0].reshape(128, 1))
    pk.add("dalam", np.broadcast_to(da_lam[0].reshape(1, 256), (128, 256)))
    pk.add("scw", sc_conv_w[0].reshape(3, 4, 128).transpose(2, 1, 0))
    pk.add("lcw", lru_conv_w[0].reshape(4, 4, 128).transpose(2, 1, 0))
    pk.add("lcb", lru_conv_b[0].reshape(4, 128).T)
    pk.add("lba", lru_b_a[0].reshape(2, 4, 128).transpose(2, 0, 1))
    pk.add("lbx", lru_b_x[0].reshape(2, 4, 128).transpose(2, 0, 1))
    pk.add("llam", lru_lam[0].reshape(2, 4, 128).transpose(2, 0, 1))
    pk.add("lh0", lruh0.reshape(2, 4, 128).transpose(2, 0, 1))
    pk.add("fcw", ffn_conv_w.reshape(2, 3, 44, 128).transpose(3, 0, 2, 1))
    pk.add("fcb", ffn_conv_b.reshape(2, 44, 128).transpose(2, 0, 1))
    pk.add("abias", abias)
    for name, (o, w) in pk.cols.items():
        assert PPL[name] == (o, w), (name, PPL[name], (o, w))
    return pk.build()


def build_program(phases=("ab", "ffn0", "cd", "ffn1"), skip=()):
    k = K()
    nc = k.nc

    def din(name, shape, dt=F32):
        return nc.dram_tensor(name, list(shape), dt, kind="ExternalInput").ap()

    def dout(name, shape, dt=F32):
        return nc.dram_tensor(name, list(shape), dt, kind="ExternalOutput").ap()

    x_d = din("x", [NT, D])
    pp_d = din("pp", [128, PPN])
    ident_d = din("ident", [128, 128])
    perm_d = din("perm", [128, 128])
    ropec_d = din("ropec", [128, NT])
    ropes_d = din("ropes", [128, NT])
    kc_d = din("kc", [512, 512])
    vc_d = din("vc", [512, 512])
    w_mod_d = din("w_mod", [2, D, 6 * D])
    w_in_ab_d = din("w_in_ab", [D, 2048])
    w_out_ab_d = din("w_out_ab", [D, D])
    s5bt_d = din("s5bt", [128, 16 * 128])
    s5ct_d = din("s5ct", [128, 64 * 128])
    w_glu_d = din("w_glu", [512, 512])
    w_in_cd_d = din("w_in_cd", [D, 2560])
    w_out_cd_d = din("w_out_cd", [D, D])
    lruw_d = din("lruw", [128, 16 * 128])
    w_up_d = din("w_up", [2, D, 2 * DFF])
    w_down_d = din("w_down", [2, DFF, D])

    y_d = dout("y", [NT, D])
    newk_d = dout("newk", [NT, 512])
    newv_d = dout("newv", [NT, 512])
    news5_d = dout("news5", [8, 2, 2, 32, 64])
    newlru_d = dout("newlru", [8, 2, 512])

    GC1 = 0.044715
    GC2 = 2.0 * math.sqrt(2.0 / math.pi)

    with k.es:
        xT = Buf(k, "xT", [128, KT, NT], F32)
        hT = Buf(k, "hT", [128, KT, NSEG, SEGW], BF16)
        pp = Buf(k, "pp", [128, PPN], F32)
        cst = Buf(k, "cst", [128, 1312], F32)
        ident = Buf(k, "ident", [128, 128], F32)
        onesb = Buf(k, "onesb", [128, 128], BF16)
        mvec = Buf(k, "mvec", [128, 96], F32)
        ps = [Buf(k, "ps%d" % i, [128, 512], F32, space="psum") for i in range(8)]
        rr = {}

        def psum(lo=0, hi=8):
            i = rr.get((lo, hi), 0)
            rr[(lo, hi)] = i + 1
            return ps[lo + i % (hi - lo)]

        def P(name, i=0, w=1):
            o, _ = PPL[name]
            return pp[:, o + i:o + i + w]

        CO = {}
        co_n = [0]

        def C(name, w=1):
            if name not in CO:
                CO[name] = (co_n[0], w)
                co_n[0] += w
                assert co_n[0] <= 1312
            o, ww = CO[name]
            return cst.at(name, (slice(None), slice(o, o + ww)))

        def sub(v, lo, hi):
            return v.re(lambda a: a[:, lo:hi])

        def r3(v, s=2):
            return v.re(lambda a: a.rearrange("p (s t) -> p s t", s=s))

        def gelu_tanh(e_mul, out, x, t):
            k.actf(out, x, AF.Gelu_apprx_tanh)

        k.dma(pp[:, :], pp_d)
        k.dma(ident[:, :], ident_d)
        k.memset("v", onesb[:, :], 1.0)
        k.memset("g", hT[:, :, :, :], 0.0)
        flag = P("flag", 0)

        with k.scope():
            xst = [Buf(k, "xst%d" % i, [128, 4, D], F32) for i in range(2)]
            x_v = x_d.rearrange("(a p) f -> p a f", p=128)
            for tb in range(4):
                st = xst[tb % 2]
                k.dma(st[:, :, :], x_v[:, tb * 4:(tb + 1) * 4, :])
                for kt in range(KT):
                    pb = psum()
                    for a in range(4):
                        k.transpose(pb[:, a * 128:(a + 1) * 128], st[:, a, kt * 128:(kt + 1) * 128], ident[:, :])
                    k.copy("v" if kt % 2 == 0 else "a", xT.at(tb, (slice(None), kt, slice(tb * 512, (tb + 1) * 512))),
                           pb[:, :])

        with k.scope():
            sc = Buf(k, "silu_c", [128, 8], F32)
            k.actf(sc[:, :], P("cond", 0, 8), AF.Silu)
            wm = [Buf(k, "wm%d" % i, [128, KT, 512], F32) for i in range(3)]
            loads = []
            for l in range(2):
                wv = w_mod_d[l].rearrange("(kt p) o -> p kt o", p=128)
                for oc in range(12):
                    loads.append(wv[:, :, oc * 512:(oc + 1) * 512])
            nld = [0]

            def ensure(i):
                while nld[0] <= min(i, len(loads) - 1):
                    j = nld[0]
                    k.dma(wm[j % 3][:, :, :], loads[j], q=(k.sp if j % 2 == 0 else k.act))
                    nld[0] += 1
            mrow = Buf(k, "mrow", [1, 6 * D], F32)
            i = 0
            for l in range(2):
                for oc in range(12):
                    ensure(i + 2)
                    wb = wm[i % 3]
                    i += 1
                    pr = psum()
                    k.mm(pr[0:1, :], [(sc[:, kt:kt + 1], wb[:, kt, :]) for kt in range(KT)])
                    k.copy("a" if oc % 2 == 0 else "v", mrow[0:1, oc * 512:(oc + 1) * 512], pr[0:1, :])
                pb = psum()
                for o in range(48):
                    k.mm(pb[:, o:o + 1], [(mrow[0:1, o * 128:(o + 1) * 128], ident[0:1, 0:1])])
                bo, _ = PPL["bmod"]
                k.tt("v", mvec[:, l * 48:(l + 1) * 48], pb[:, 0:48], pp[:, bo + l * 48:bo + (l + 1) * 48], ALU.add)

        def M(l, j):
            return mvec[:, l * 48 + j * 8:l * 48 + j * 8 + 8]

        def NG(l, i):
            o, _ = PPL["ng"]
            return pp[:, o + (l * 4 + i) * 8:o + (l * 4 + i) * 8 + 8]

        for l in range(2):
            for j in range(2):
                a = C("A%d%d" % (l, j), 8)
                k.ts("v", a, M(l, 3 * j + 1), 1.0, ALU.add, 32.0, ALU.mult)
                k.tt("v", a, a, NG(l, 2 * j), ALU.mult)
                g = C("G%d%d" % (l, j), 8)
                k.ts("v", g, M(l, 3 * j + 2), 32.0, ALU.mult)
                k.tt("v", g, g, NG(l, 2 * j + 1), ALU.mult)

        def hseg(kt, tb):
            return hT.at(("h", tb), (slice(None), kt, slice(2 * tb, 2 * tb + 2), slice(1, 1 + SEG)))

        def xblk(kt, tb):
            return xT.at(tb, (slice(None), kt, slice(tb * 512, (tb + 1) * 512)))

        def rsqrt_eps(out, in_, eps_n):
            k.ts("v", out, in_, float(eps_n), ALU.add)
            k.actf(out, out, AF.Sqrt)
            k.recip(out, out)

        def norm_work(pfx):
            sq = [Buf(k, pfx + "_sq%d" % i, [128, 512], BF16) for i in range(2)]
            rstd = Buf(k, pfx + "_rstd", [128, 512], F32)
            tmp = [Buf(k, pfx + "_tmp%d" % i, [128, 512], F32) for i in range(2)]
            return (sq, rstd, tmp)

        def sumsq_rstd(srcs, work, eps_n):
            sq, rstd, tmp = work
            pb = psum()
            n = len(srcs)
            for i, s in enumerate(srcs):
                k.actf(sq[i % 2][:, :], s, AF.Square)
                k.emit(k.pe, (lambda i=i, pb=pb: k.pe.h.matmul(pb[:, :].ap, onesb[:, :].ap, sq[i % 2][:, :].ap,
                                                                start=(i == 0), stop=(i == n - 1))),
                       [onesb[:, :], sq[i % 2][:, :]], [pb[:, :]])
            rsqrt_eps(rstd[:, :], pb[:, :], eps_n)

        def pre_norm(l, j):
            A = C("A%d%d" % (l, j), 8)
            S = M(l, 3 * j)
            with k.scope():
                work = norm_work("pn")
                sq, rstd, tmp = work
                for tb in range(4):
                    sumsq_rstd([xblk(kt, tb) for kt in range(KT)], work, D * EPS)
                    for kt in range(KT):
                        t = tmp[kt % 2]
                        k.tt("v", t[:, :], xblk(kt, tb), rstd[:, :], ALU.mult)
                        k.actf(hseg(kt, tb), r3(t[:, :]), AF.Identity, bias=sub(S, kt, kt + 1), scale=sub(A, kt, kt + 1))

        def halos():
            k.ts("v", hT[:, :, 1:NSEG, 0], hT[:, :, 0:NSEG - 1, SEG], flag, ALU.mult)
            k.ts("v", hT[:, :, 0:NSEG - 1, SEG + 1], hT[:, :, 1:NSEG, 1], flag, ALU.mult)

        def post_norm_block(l, j, tb, yo, work):
            G = C("G%d%d" % (l, j), 8)
            sq, rstd, tmp = work
            sumsq_rstd([yo[:, o, :] for o in range(KT)], work, D * EPS)
            for o in range(KT):
                t = tmp[o % 2]
                k.tt("v", t[:, :], yo[:, o, :], rstd[:, :], ALU.mult)
                k.stt(xblk(o, tb), t[:, :], sub(G, o, o + 1), xblk(o, tb), ALU.mult, ALU.add)

        def out_proj_post(l, w_d, ymix):
            with k.scope():
                work = norm_work("op")
                yo = Buf(k, "op_yo", [128, KT, 512], F32)
                wos = [Buf(k, "op_wo%d" % o, [128, KT, 128], BF16) for o in range(KT)]
                wov = w_d.rearrange("(kt p) o -> p kt o", p=128)
                for o in range(KT):
                    k.dma(wos[o][:, :, :], wov[:, :, o * 128:(o + 1) * 128], q=k.pool)
                for tb in range(4):
                    for o in range(KT):
                        pb = psum()
                        k.mm(pb[:, :], [(wos[o][:, kt, :], ymix[:, kt, tb * 512:(tb + 1) * 512]) for kt in range(KT)])
                        k.copy("a", yo[:, o, :], pb[:, :])
                    post_norm_block(l, 0, tb, yo, work)

        class Stream:
            def __init__(self, bufs, loads, ahead):
                self.bufs, self.loads, self.ahead, self.n = bufs, loads, ahead, 0

            def get(self, i):
                while self.n <= min(i + self.ahead, len(self.loads) - 1):
                    j = self.n
                    for (dst_fn, src) in self.loads[j]:
                        k.dma(dst_fn(self.bufs[j % len(self.bufs)]), src, q=k.pool)
                    self.n += 1
                return self.bufs[i % len(self.bufs)]

        def ffn(l):
            pre_norm(l, 1)
            halos()
            with k.scope():
                work = norm_work("f")
                actb = Buf(k, "f_act", [128, NCT, 1024], BF16)
                wup = [Buf(k, "f_wup%d" % i, [128, KT, 256], BF16) for i in range(3)]
                wdn = [Buf(k, "f_wdn%d" % i, [128, NCT, 128], BF16) for i in range(2)]
                ca = [Buf(k, "f_ca%d" % i, [128, SEG], F32) for i in range(8)]
                yo = Buf(k, "f_yo", [128, KT, 512], F32)
                wupv = w_up_d[l].rearrange("(kt p) c -> p kt c", p=128)
                wdnv = w_down_d[l].rearrange("(c p) o -> p c o", p=128)
                fo, _ = PPL["fcw"]
                bo, _ = PPL["fcb"]

                def cw(ct, j):
                    o = fo + (l * 44 + ct) * 3 + j
                    return pp[:, o:o + 1]

                def cb(ct):
                    o = bo + l * 44 + ct
                    return pp[:, o:o + 1]

                up_loads = []
                for half in range(2):
                    for c in range(NCT):
                        up_loads.append([(lambda b: b[:, :, 0:128], wupv[:, :, c * 128:(c + 1) * 128]),
                                         (lambda b: b[:, :, 128:256], wupv[:, :, DFF + c * 128:DFF + (c + 1) * 128])])
                dn_loads = []
                for tb in range(4):
                    for o in range(KT):
                        dn_loads.append([(lambda b: b[:, :, :], wdnv[:, :, o * 128:(o + 1) * 128])])
                ups = Stream(wup, up_loads, 2)
                dns = Stream(wdn, dn_loads, 1)
                for half in range(2):
                    items = [(c, sg) for c in range(NCT) for sg in range(4)]
                    resv = {}

                    def stA(i):
                        c, sg = items[i]
                        wb = ups.get(half * NCT + c)
                        seg = half * 4 + sg
                        res = []
                        for which in range(2):
                            ct = c + which * NCT
                            pb = psum()
                            k.mm(pb[:, 0:SEGW], [(wb[:, kt, which * 128:(which + 1) * 128], hT[:, kt, seg, :])
                                                 for kt in range(KT)])
                            a = ca[(2 * i + which) % 8]
                            k.actf(a[:, :], pb[:, 1:1 + SEG], AF.Identity, bias=cb(ct), scale=cw(ct, 1))
                            k.stt(a[:, :], pb[:, 0:SEG], cw(ct, 0), a[:, :], ALU.mult, ALU.add)
                            k.stt(a[:, :], pb[:, 2:2 + SEG], cw(ct, 2), a[:, :], ALU.mult, ALU.add)
                            res.append(a)
                        resv[i] = res

                    def stB(i):
                        c, sg = items[i]
                        res = resv.pop(i)
                        k.actf(res[0][:, :], res[0][:, :], AF.Gelu_apprx_tanh)
                        k.tt("v", actb[:, c, sg * SEG:(sg + 1) * SEG], res[0][:, :], res[1][:, :], ALU.mult)
                    LAGF = 2
                    for t in range(len(items) + LAGF):
                        if t - LAGF >= 0:
                            stB(t - LAGF)
                        if t < len(items):
                            stA(t)
                    for tbh in range(2):
                        tb = half * 2 + tbh
                        for o in range(KT):
                            wd = dns.get(tb * KT + o)
                            pb = psum()
                            k.mm(pb[:, :], [(wd[:, c, :], actb[:, c, tbh * 512:(tbh + 1) * 512]) for c in range(NCT)])
                            k.copy("a", yo[:, o, :], pb[:, :])
                        post_norm_block(l, 1, tb, yo, work)

        def mixer_cd(l):
            pre_norm(l, 0)
            with k.scope():
                ymix = Buf(k, "c_ymix", [128, KT, NT], BF16)
                win = [Buf(k, "c_win%d" % i, [128, KT, 128], BF16) for i in range(3)]
                winv = w_in_cd_d.rearrange("(kt p) c -> p kt c", p=128)
                order = []
                for c in range(4):
                    order += [0 + c * 128, 1024 + c * 128, 512 + c * 128]
                for c in range(4):
                    order += [1536 + c * 128, 2048 + c * 128]
                wst = Stream(win, [[(lambda b: b[:, :, :], winv[:, :, c0:c0 + 128])] for c0 in order], 1)
                ip = [0]

                def proj(col0):
                    assert order[ip[0]] == col0
                    wb = wst.get(ip[0])
                    ip[0] += 1
                    outs = []
                    for tb in range(4):
                        pb = psum()
                        k.mm(pb[:, :], [(wb[:, kt, :], hseg(kt, tb)) for kt in range(KT)])
                        outs.append(pb)
                    return outs

                with k.scope():
                    pbuf = Buf(k, "c_p", [128, NSEG, SEGW], F32)
                    xin = Buf(k, "c_xin", [128, NT], F32)
                    acc = Buf(k, "c_acc", [128, NT], F32)
                    k.memset("g", pbuf[:, :, :], 0.0)
                    so, _ = PPL["scw"]
                    for c in range(4):
                        px = proj(0 + c * 128)
                        for tb in range(4):
                            k.copy("a", xin[:, tb * 512:(tb + 1) * 512], px[tb][:, :])
                        pc = proj(1024 + c * 128)
                        for tb in range(4):
                            k.tt("v", pbuf[:, 2 * tb:2 * tb + 2, 1:1 + SEG], r3(pc[tb][:, :]),
                                 r3(xin[:, tb * 512:(tb + 1) * 512]), ALU.mult)
                        k.ts("v", pbuf[:, 1:NSEG, 0], pbuf[:, 0:NSEG - 1, SEG], flag, ALU.mult)
                        k.ts("v", pbuf[:, 0:NSEG - 1, SEG + 1], pbuf[:, 1:NSEG, 1], flag, ALU.mult)
                        w0, w1, w2 = (pp[:, so + c * 3 + j:so + c * 3 + j + 1] for j in range(3))
                        a3 = r3(acc[:, :], NSEG)
                        k.actf(a3, pbuf[:, :, 1:1 + SEG], AF.Identity, scale=w1)
                        k.stt(a3, pbuf[:, :, 0:SEG], w0, a3, ALU.mult, ALU.add)
                        k.stt(a3, pbuf[:, :, 2:2 + SEG], w2, a3, ALU.mult, ALU.add)
                        pbg = proj(512 + c * 128)
                        for tb in range(4):
                            k.tt("v", ymix[:, c, tb * 512:(tb + 1) * 512], pbg[tb][:, :], acc[:, tb * 512:(tb + 1) * 512],
                                 ALU.mult)

                with k.scope():
                    lw = Buf(k, "c_lw", [128, 16, 128], BF16)
                    k.dma(lw[:, :, :], lruw_d.rearrange("p (a o) -> p a o", a=16), q=k.pool)
                    lo, _ = PPL["llam"]
                    nsc = C("lru_nsc", 8)
                    zz = C("lru_z", 8)
                    ser = C("lru_ser", 8)
                    lnv = C("lru_ln", 8)
                    msk = C("lru_msk", 8)
                    k.actf(zz, pp[:, lo:lo + 8], AF.Exp, scale=-1.0)
                    k.ts("v", msk, zz, 0.25, ALU.min)
                    k.memset("v", ser, -1.0 / 12.0)
                    for kk in range(11, 0, -1):
                        k.tt("v", ser, ser, msk, ALU.mult)
                        k.ts("v", ser, ser, (1.0 if kk % 2 == 1 else -1.0) / kk, ALU.add)
                    k.tt("v", ser, ser, msk, ALU.mult)
                    k.ts("v", lnv, zz, 1.0, ALU.add)
                    k.actf(lnv, lnv, AF.Ln)
                    k.ts("v", msk, msk, 1.0, ALU.add)
                    k.actf(msk, msk, AF.Ln)
                    k.tt("v", lnv, lnv, msk, ALU.subtract)
                    k.tt("v", nsc, ser, lnv, ALU.add)
                    k.ts("v", nsc, nsc, -8.0, ALU.mult)
                    xr = Buf(k, "c_xr", [128, NSEG, SEG + 3], F32)
                    xc = Buf(k, "c_xc", [128, NT], F32)
                    xcb = Buf(k, "c_xcb", [128, NT], BF16)
                    ra = Buf(k, "c_ra", [128, NT], F32)
                    ia = Buf(k, "c_ia", [128, NT], F32)

                    class _T1:
                        def __getitem__(self, idx):
                            sl = idx[1]
                            lo = 0 if sl.start is None else sl.start
                            hi = NT if sl.stop is None else sl.stop
                            return xr[:, :, :].re(lambda a: a.rearrange("p s t -> p (s t)")[:, lo:hi])
                    t1 = _T1()
                    hs = [Buf(k, "c_hs%d" % i, [128, NT], F32) for i in range(2)]
                    fin = Buf(k, "c_fin", [128, 4, 2, NSEG], F32)
                    k.memset("g", xr[:, :, :], 0.0)
                    co, _ = PPL["lcw"]
                    cbo, _ = PPL["lcb"]
                    bao, _ = PPL["lba"]
                    bxo, _ = PPL["lbx"]
                    h0o, _ = PPL["lh0"]
                    for c in range(4):
                        px = proj(1536 + c * 128)
                        for tb in range(4):
                            k.copy("a", xr[:, 2 * tb:2 * tb + 2, 2:2 + SEG], r3(px[tb][:, :]))
                        k.memset("g", xr[:, 0, 0:2], 0.0)
                        k.memset("g", xr[:, NSEG - 1, SEG + 2:SEG + 3], 0.0)
                        k.ts("v", xr[:, 1:NSEG, 0:2], xr[:, 0:NSEG - 1, SEG:SEG + 2], flag, ALU.mult)
                        k.ts("v", xr[:, 0:NSEG - 1, SEG + 2], xr[:, 1:NSEG, 2], flag, ALU.mult)
                        w = [pp[:, co + c * 4 + j:co + c * 4 + j + 1] for j in range(4)]
                        x3 = r3(xc[:, :], NSEG)
                        k.actf(x3, xr[:, :, 2:2 + SEG], AF.Identity, scale=w[2], bias=pp[:, cbo + c:cbo + c + 1])
                        k.stt(x3, xr[:, :, 0:SEG], w[0], x3, ALU.mult, ALU.add)
                        k.stt(x3, xr[:, :, 1:1 + SEG], w[1], x3, ALU.mult, ALU.add)
                        k.stt(x3, xr[:, :, 3:3 + SEG], w[3], x3, ALU.mult, ALU.add)
                        k.copy("a", xcb[:, :], xc[:, :])
                        for d in range(2):
                            for tb in range(4):
                                sl = slice(tb * 512, (tb + 1) * 512)
                                pa = psum()
                                k.mm(pa[:, :], [(lw[:, (0 * 2 + d) * 4 + c, :], xcb[:, sl])])
                                k.actf(ra[:, sl], pa[:, :], AF.Sigmoid, bias=pp[:, bao + d * 4 + c:bao + d * 4 + c + 1])
                                pi = psum()
                                k.mm(pi[:, :], [(lw[:, (1 * 2 + d) * 4 + c, :], xcb[:, sl])])
                                k.actf(ia[:, sl], pi[:, :], AF.Sigmoid, bias=pp[:, bxo + d * 4 + c:bxo + d * 4 + c + 1])
                            k.actf(ra[:, :], ra[:, :], AF.Exp, scale=sub(nsc, d * 4 + c, d * 4 + c + 1))
                            k.tt("v", t1[:, :], ra[:, :], ra[:, :], ALU.mult)
                            k.ts("v", t1[:, :], t1[:, :], -1.0, ALU.mult, 1.0, ALU.add)
                            k.actf(t1[:, :], t1[:, :], AF.Sqrt)
                            k.tt("v", ia[:, :], ia[:, :], xc[:, :], ALU.mult)
                            k.tt("v", ia[:, :], ia[:, :], t1[:, :], ALU.mult)
                            a3 = r3(ra[:, :], NSEG)
                            h0 = pp[:, h0o + d * 4 + c:h0o + d * 4 + c + 1]
                            if d == 0:
                                rs_ = a3.re(lambda a: a[:, 1:NSEG, 0])
                                k.ts("v", rs_, rs_, flag, ALU.mult)
                                k.scan(hs[0][:, :], ra[:, :], ia[:, :], h0)
                                k.copy("v", fin[:, c, 0, :], r3(hs[0][:, :], NSEG).re(lambda a: a[:, :, SEG - 1]))
                            else:
                                rs_ = a3.re(lambda a: a[:, 0:NSEG - 1, SEG - 1])
                                k.ts("v", rs_, rs_, flag, ALU.mult)
                                rv = lambda v: v.re(lambda a: a[:, ::-1])
                                k.scan(rv(hs[1][:, :]), rv(ra[:, :]), rv(ia[:, :]), h0)
                                k.copy("v", fin[:, c, 1, :], r3(hs[1][:, :], NSEG).re(lambda a: a[:, :, 0]))
                        k.tt("v", hs[0][:, :], hs[0][:, :], hs[1][:, :], ALU.add)
                        pg = proj(2048 + c * 128)
                        for tb in range(4):
                            sl = slice(tb * 512, (tb + 1) * 512)
                            k.copy("a", xc[:, sl], pg[tb][:, :])
                            gelu_tanh("v", t1[:, sl], xc[:, sl], t1[:, sl])
                            k.tt("v", ymix[:, 4 + c, sl], hs[0][:, sl], t1[:, sl], ALU.mult)
                    for c in range(4):
                        for d in range(2):
                            k.dma(newlru_d[:, d, c * 128:(c + 1) * 128].rearrange("s p -> p s"), fin[:, c, d, :],
                                  slow=True)
                out_proj_post(l, w_out_cd_d, ymix)

        def mixer_ab(l):
            lam_init = 0.8 - 0.6 * math.exp(-0.3 * l)
            pre_norm(l, 0)
            with k.scope():
                ymix = Buf(k, "a_ymix", [128, KT, NT], BF16)
                winv = w_in_ab_d.rearrange("(kt p) c -> p kt c", p=128)
                if "s5" not in skip:
                    s5_part(ymix, winv)
                else:
                    k.memset("g", ymix[:, 0:4, :], 0.0)
                if "attn" not in skip:
                    attn_part(ymix, winv, lam_init)
                else:
                    k.memset("g", ymix[:, 4:8, :], 0.0)
                out_proj_post(l, w_out_ab_d, ymix)

        def s5_part(ymix, winv):
            dt = C("s5_dt", 32)
            mag = C("s5_mag", 32)
            ang = C("s5_ang", 32)
            c1 = C("s5_c1", 32)
            s1 = C("s5_s1", 32)
            fr = C("s5_fr", 32)
            fi = C("s5_fi", 32)
            ta = C("s5_ta", 32)
            tb_ = C("s5_tb", 32)
            tc = C("s5_tc", 32)
            lre = P("s5lre", 0, 32)
            lim = P("s5lim", 0, 32)
            k.actf(dt, P("s5ldt", 0, 32), AF.Exp)
            k.tt("v", mag, lre, dt, ALU.mult)
            k.actf(mag, mag, AF.Exp)
            k.tt("v", ang, lim, dt, ALU.mult)
            MAGIC = 12582912.0

            def sin_of(out, shift):
                k.ts("v", tb_, ang, float(shift), ALU.add)
                k.ts("v", ta, tb_, 1.0 / (2 * math.pi), ALU.mult)
                k.ts("v", ta, ta, MAGIC, ALU.add)
                k.ts("v", ta, ta, -MAGIC, ALU.add)
                k.stt(ta, ta, -2 * math.pi, tb_, ALU.mult, ALU.add)
                k.ts("v", ta, ta, math.pi, ALU.min, -math.pi, ALU.max)
                k.actf(out, ta, AF.Sin)
            sin_of(s1, 0.0)
            sin_of(c1, 0.5 * math.pi)
            k.tt("v", ta, mag, c1, ALU.mult)
            k.ts("v", ta, ta, -1.0, ALU.add)
            k.tt("v", tb_, mag, s1, ALU.mult)
            k.tt("v", tc, lre, lre, ALU.mult)
            k.tt("v", fr, lim, lim, ALU.mult)
            k.tt("v", tc, tc, fr, ALU.add)
            k.recip(tc, tc)
            k.tt("v", fr, ta, lre, ALU.mult)
            k.tt("v", fi, tb_, lim, ALU.mult)
            k.tt("v", fr, fr, fi, ALU.add)
            k.tt("v", fr, fr, tc, ALU.mult)
            k.tt("v", fi, tb_, lre, ALU.mult)
            k.tt("v", tb_, ta, lim, ALU.mult)
            k.tt("v", fi, fi, tb_, ALU.subtract)
            k.tt("v", fi, fi, tc, ALU.mult)
            cpw = C("s5_cpw", 9 * 32)
            spw = C("s5_spw", 9 * 32)
            k.copy("v", sub(cpw, 0, 32), c1)
            k.copy("v", sub(spw, 0, 32), s1)
            for m in range(1, 9):
                cp = sub(cpw, (m - 1) * 32, m * 32)
                sp_ = sub(spw, (m - 1) * 32, m * 32)
                cn = sub(cpw, m * 32, (m + 1) * 32)
                sn = sub(spw, m * 32, (m + 1) * 32)
                k.tt("v", ta, cp, cp, ALU.mult)
                k.tt("v", tb_, sp_, sp_, ALU.mult)
                k.tt("v", cn, ta, tb_, ALU.subtract)
                k.tt("v", ta, cp, sp_, ALU.mult)
                k.ts("v", sn, ta, 2.0, ALU.mult)
            c256 = C("s5_c256", 32)
            s256 = C("s5_s256", 32)
            k.ts("v", c256, sub(cpw, 8 * 32, 9 * 32), flag, ALU.mult)
            k.ts("v", s256, sub(spw, 8 * 32, 9 * 32), flag, ALU.mult)
            ns256 = C("s5_ns256", 32)
            k.ts("v", ns256, s256, -1.0, ALU.mult)
            g0r = C("s5_g0r", 32)
            g0i = C("s5_g0i", 32)
            ho, _ = PPL["s5h0"]
            h3 = pp[:, ho:ho + 64].re(lambda a: a.rearrange("p (d r j) -> p d r j", d=2, r=2))
            hr_ = h3.re(lambda a: a[:, :, 0, :])
            hi_ = h3.re(lambda a: a[:, :, 1, :])

            def v3(v):
                return v.re(lambda a: a.rearrange("p (d j) -> p d j", d=2))
            k.tt("v", ta, fr, fr, ALU.mult)
            k.tt("v", tb_, fi, fi, ALU.mult)
            k.tt("v", ta, ta, tb_, ALU.add)
            k.recip(ta, ta)
            k.tt("v", v3(tb_), hr_, v3(fr), ALU.mult)
            k.tt("v", v3(tc), hi_, v3(fi), ALU.mult)
            k.tt("v", tb_, tb_, tc, ALU.add)
            k.tt("v", g0r, tb_, ta, ALU.mult)
            k.tt("v", v3(tb_), hi_, v3(fr), ALU.mult)
            k.tt("v", v3(tc), hr_, v3(fi), ALU.mult)
            k.tt("v", tb_, tb_, tc, ALU.subtract)
            k.tt("v", g0i, tb_, ta, ALU.mult)
            k.tt("v", ta, g0r, c1, ALU.mult)
            k.tt("v", tb_, g0i, s1, ALU.mult)
            k.tt("v", tc, g0r, s1, ALU.mult)
            k.tt("v", g0r, ta, tb_, ALU.subtract)
            k.tt("v", ta, g0i, c1, ALU.mult)
            k.tt("v", g0i, ta, tc, ALU.add)

            with k.scope():
                bt = Buf(k, "s_bt", [128, 16, 128], BF16)
                k.dma(bt[:, :, :], s5bt_d.rearrange("p (a o) -> p a o", a=16), q=k.pool)
                ctb = [Buf(k, "s_ct%d" % i, [128, 16, 128], BF16) for i in range(1)]
                wu = [Buf(k, "s_wu%d" % i, [128, KT, 128], BF16) for i in range(2)]
                wus = Stream(wu, [[(lambda b: b[:, :, :], winv[:, :, c * 128:(c + 1) * 128])] for c in range(4)], 1)
                uT = Buf(k, "s_uT", [128, NT], BF16)
                ys = Buf(k, "s_ys", [128, NT], F32)
                Ec = Buf(k, "s_Ec", [128, 4, SEG + 1], F32)
                Es = Buf(k, "s_Es", [128, 4, SEG + 1], F32)
                Fc = Buf(k, "s_Fc", [128, 4, SEG], F32)
                Fs = Buf(k, "s_Fs", [128, 4, SEG], F32)
                rho = Buf(k, "s_rho", [128, 4, SEG], F32)
                carry = Buf(k, "s_carry", [128, 4, 2], F32)
                glast = Buf(k, "s_glast", [128, 4, 2, NSEG], F32)
                fin = Buf(k, "s_fin", [128, 4, 2, NSEG], F32)
                NW = 2
                wsi = [0]
                bsb = [Buf(k, "s_b%d" % r, [128, 2 * SEG], F32) for r in range(NW)]
                mt = [Buf(k, "s_m%d" % r, [128, 4 * SEG], F32) for r in range(NW)]
                gin = [Buf(k, "s_gin%d" % r, [128, 2 * SEG], F32) for r in range(NW)]
                gs = [Buf(k, "s_gs%d" % r, [128, 2 * SEG], F32) for r in range(NW)]
                hb = [Buf(k, "s_hb%d" % r, [128, 2 * SEG], BF16) for r in range(NW)]
                tn = Buf(k, "s_tn", [128, 8], F32)
                do_, _ = PPL["s5d"]
                for c in range(4):
                    wb = wus.get(c)
                    k.dma(ctb[0][:, :, :], s5ct_d[:, c * 2048:(c + 1) * 2048].rearrange("p (a o) -> p a o", a=16),
                          q=k.pool)
                    for tb in range(4):
                        pb = psum(4, 8)
                        k.mm(pb[:, :], [(wb[:, kt, :], hseg(kt, tb)) for kt in range(KT)])
                        sl = slice(tb * 512, (tb + 1) * 512)
                        k.actf(ys[:, sl], pb[:, :], AF.Identity, scale=pp[:, do_ + c:do_ + c + 1])
                        k.copy("a", uT[:, sl], pb[:, :])
                    for d in range(2):
                        i0 = d * 16 + 4 * c

                        def cs(v, m=None, w=None):
                            o = i0 if m is None else m * 32 + i0
                            vv = v.re(lambda a: a[:, o:o + 4])
                            if w is not None:
                                vv = vv.re(lambda a: a.unsqueeze(2).to_broadcast([128, 4, w]))
                            return vv
                        k.memset("g", Ec[:, :, 0:1], 1.0)
                        k.memset("g", Es[:, :, 0:1], 0.0)
                        k.copy("v", Ec[:, :, 1], cs(cpw, 0))
                        k.copy("v", Es[:, :, 1], cs(spw, 0))
                        e0f = mt[0][:, 0:512].re(lambda a: a.rearrange("p (j t) -> p j t", j=4))
                        e1f = mt[1][:, 0:512].re(lambda a: a.rearrange("p (j t) -> p j t", j=4))
                        for m in range(1, 9):
                            n = 2 ** m
                            hi = min(2 * n, SEG + 1)
                            w = hi - n
                            cm, sm = cs(cpw, m, w), cs(spw, m, w)
                            e0 = e0f.re(lambda a: a[:, :, 0:w])
                            e1 = e1f.re(lambda a: a[:, :, 0:w])
                            k.tt("v", e0, Ec[:, :, 0:w], cm, ALU.mult)
                            k.tt("v", e1, Es[:, :, 0:w], sm, ALU.mult)
                            k.tt("v", Ec[:, :, n:hi], e0, e1, ALU.subtract)
                            k.tt("v", e0, Ec[:, :, 0:w], sm, ALU.mult)
                            k.tt("v", e1, Es[:, :, 0:w], cm, ALU.mult)
                            k.tt("v", Es[:, :, n:hi], e0, e1, ALU.add)
                        for half in range(2):
                            hs_ = slice(half * 128, (half + 1) * 128)
                            fbr, fbi = cs(fr, None, 128), cs(fi, None, 128)
                            k.tt("v", e0f, Ec[:, :, hs_], fbr, ALU.mult)
                            k.tt("v", e1f, Es[:, :, hs_], fbi, ALU.mult)
                            k.tt("v", Fc[:, :, hs_], e0f, e1f, ALU.subtract)
                            k.tt("v", e0f, Ec[:, :, hs_], fbi, ALU.mult)
                            k.tt("v", e1f, Es[:, :, hs_], fbr, ALU.mult)
                            k.tt("v", Fs[:, :, hs_], e0f, e1f, ALU.add)
                        k.copy("a", rho[:, :, :], cs(mag, None, SEG))
                        k.copy("v", carry[:, :, 0], cs(g0r))
                        k.copy("v", carry[:, :, 1], cs(g0i))
                        items = [(sstep, jj) for sstep in range(NSEG) for jj in range(4)]
                        rv = (lambda v: v)
                        rvm = (lambda v: v) if d == 0 else (lambda v: v.re(lambda a: a[:, ::-1]))
                        lastc = SEG - 1
                        ypbs = {}

                        def segof(sstep):
                            return sstep if d == 0 else NSEG - 1 - sstep

                        def tabf(T, jj):
                            return rv(T[:, jj, 0:SEG])

                        def views(i):
                            ws = i % NW
                            return dict(
                                br=bsb[ws][:, 0:SEG], bi=bsb[ws][:, SEG:2 * SEG],
                                m2=mt[ws][:, 0:SEG], m4=mt[ws][:, SEG:2 * SEG],
                                tA=mt[ws][:, 2 * SEG:3 * SEG], tB=mt[ws][:, 3 * SEG:4 * SEG],
                                gir=gin[ws][:, 0:SEG], gii=gin[ws][:, SEG:2 * SEG],
                                gsr=gs[ws][:, 0:SEG], gsi=gs[ws][:, SEG:2 * SEG],
                                glr=gs[ws][:, lastc:lastc + 1], gli=gs[ws][:, SEG + lastc:SEG + lastc + 1],
                                hr=hb[ws][:, 0:SEG], hi=hb[ws][:, SEG:2 * SEG], bb=bsb[ws][:, :])

                        def S0(i):
                            sstep, jj = items[i]
                            seg = segof(sstep)
                            sl = slice(seg * SEG, (seg + 1) * SEG)
                            v = views(i)
                            pbb = psum(0, 4)
                            rs = slice(32 * jj, 32 * jj + 32)
                            k.mm(pbb[:, 0:SEG], [(bt[rs, (c * 2 + d) * 2 + 0, :], rvm(uT[rs, sl]))], tile_position=(32 * jj, 0))
                            k.mm(pbb[:, SEG:2 * SEG], [(bt[rs, (c * 2 + d) * 2 + 1, :], rvm(uT[rs, sl]))],
                                 tile_position=(32 * jj, 0))
                            k.copy("a", v["bb"], pbb[:, :])

                        def S1a(i):
                            sstep, jj = items[i]
                            v = views(i)
                            k.tt("v", v["gir"], v["br"], tabf(Ec, jj), ALU.mult)
                            k.tt("v", v["m2"], v["bi"], tabf(Es, jj), ALU.mult)
                            k.tt("v", v["m4"], v["br"], tabf(Es, jj), ALU.mult)
                            k.tt("v", v["gii"], v["bi"], tabf(Ec, jj), ALU.mult)

                        def S1b(i):
                            v = views(i)
                            k.tt("v", v["gir"], v["gir"], v["m2"], ALU.add)
                            k.tt("v", v["gii"], v["gii"], v["m4"], ALU.subtract)

                        def S2(i):
                            sstep, jj = items[i]
                            seg = segof(sstep)
                            v = views(i)
                            cr_ = carry.at(jj, (slice(None), jj, slice(0, 1)))
                            ci_ = carry.at(jj, (slice(None), jj, slice(1, 2)))
                            k.scan(rv(v["gsr"]), rho[:, jj, :], rv(v["gir"]), cr_)
                            k.scan(rv(v["gsi"]), rho[:, jj, :], rv(v["gii"]), ci_)
                            k.copy("a", glast.at(jj, (slice(None), jj, 0, slice(seg, seg + 1))), v["glr"])
                            k.copy("a", glast.at(jj, (slice(None), jj, 1, slice(seg, seg + 1))), v["gli"])
                            cc = sub(c256, i0 + jj, i0 + jj + 1)
                            sc_ = sub(s256, i0 + jj, i0 + jj + 1)
                            nsc_ = sub(ns256, i0 + jj, i0 + jj + 1)
                            t_a = tn.at(jj, (slice(None), slice(2 * jj, 2 * jj + 1)))
                            t_b = tn.at(jj, (slice(None), slice(2 * jj + 1, 2 * jj + 2)))
                            k.actf(t_a, v["gli"], AF.Identity, scale=nsc_)
                            k.actf(t_b, v["glr"], AF.Identity, scale=sc_)
                            k.actf(cr_, v["glr"], AF.Identity, scale=cc, bias=t_a)
                            k.actf(ci_, v["gli"], AF.Identity, scale=cc, bias=t_b)

                        def S3a(i):
                            sstep, jj = items[i]
                            v = views(i)
                            k.tt("v", v["tA"], v["gsr"], tabf(Fs, jj), ALU.mult)
                            k.tt("v", v["tB"], v["gsi"], tabf(Fs, jj), ALU.mult)
                            k.tt("v", v["gsr"], v["gsr"], tabf(Fc, jj), ALU.mult)
                            k.tt("v", v["gsi"], v["gsi"], tabf(Fc, jj), ALU.mult)

                        def S3b(i):
                            sstep, jj = items[i]
                            seg = segof(sstep)
                            sl = slice(seg * SEG, (seg + 1) * SEG)
                            v = views(i)
                            k.tt("v", v["hr"], v["gsr"], v["tB"], ALU.subtract)
                            k.stt(v["hi"], v["gsi"], -1.0, v["tA"], ALU.mult, ALU.subtract)
                            if jj == 0:
                                ypbs[sstep] = psum(4, 8)
                            ypb = ypbs[sstep]
                            for ri, hsrc in ((0, v["hr"]), (1, v["hi"])):
                                cv_ = ctb[0][:, (jj * 2 + d) * 2 + ri, :]
                                first = (jj == 0 and ri == 0)
                                last = (jj == 3 and ri == 1)
                                hsrc = rvm(hsrc)
                                k.emit(k.pe, (lambda cv_=cv_, hsrc=hsrc, first=first, last=last, ypb=ypb:
                                              k.pe.h.matmul(ypb[:, 0:SEG].ap, cv_.ap, hsrc.ap, start=first, stop=last)),
                                       [cv_, hsrc], [ypb[:, 0:SEG]])
                            if jj == 3:
                                k.tt("v", ys[:, sl], ys[:, sl], ypb[:, 0:SEG], ALU.add)
                                del ypbs[sstep]

                        stages = [S0, S1a, S1b, S2, S3a, S3b]
                        nI = len(items)
                        for t in range(nI + len(stages) - 1):
                            for si in range(len(stages) - 1, -1, -1):
                                i = t - si
                                if 0 <= i < nI:
                                    stages[si](i)
                        lastt = SEG - 1
                        Fcl = Fc[:, :, lastt:lastt + 1].re(lambda a: a.to_broadcast([128, 4, NSEG]))
                        Fsl = Fs[:, :, lastt:lastt + 1].re(lambda a: a.to_broadcast([128, 4, NSEG]))
                        e0 = e0f.re(lambda a: a[:, :, 0:NSEG])
                        e1 = e1f.re(lambda a: a[:, :, 0:NSEG])
                        k.tt("v", e0, glast[:, :, 0, :], Fcl, ALU.mult)
                        k.tt("v", e1, glast[:, :, 1, :], Fsl, ALU.mult)
                        k.tt("v", fin[:, :, 0, :], e0, e1, ALU.subtract)
                        k.tt("v", e0, glast[:, :, 1, :], Fcl, ALU.mult)
                        k.tt("v", e1, glast[:, :, 0, :], Fsl, ALU.mult)
                        k.tt("v", fin[:, :, 1, :], e0, e1, ALU.add)
                        for jj in range(4):
                            j = 4 * c + jj
                            for ri in range(2):
                                for g2 in range(2):
                                    k.dma(news5_d[:, d, ri, 2 * j + g2, :].rearrange("s n -> n s"),
                                          fin[64 * g2:64 * g2 + 64, jj, ri, :], slow=True)
                    for tb in range(4):
                        sl = slice(tb * 512, (tb + 1) * 512)
                        gelu_tanh("v", ymix[:, c, sl], ys[:, sl], None)
            with k.scope():
                wg = Buf(k, "s_wg", [128, 4, 512], BF16)
                k.dma(wg[:, :, :], w_glu_d.rearrange("(kt p) o -> p kt o", p=128), q=k.pool)
                bgo, _ = PPL["bglu"]
                zs = Buf(k, "s_zs", [128, 4, NT], BF16)
                for co_ in range(4):
                    for tb in range(4):
                        sl = slice(tb * 512, (tb + 1) * 512)
                        pb = psum()
                        k.mm(pb[:, :], [(wg[:, ci, co_ * 128:(co_ + 1) * 128], ymix[:, ci, sl]) for ci in range(4)])
                        k.actf(zs[:, co_, sl], pb[:, :], AF.Sigmoid, bias=pp[:, bgo + co_:bgo + co_ + 1])
                for co_ in range(4):
                    k.tt("v", ymix[:, co_, :], ymix[:, co_, :], zs[:, co_, :], ALU.mult)

        def attn_part(ymix, winv, lam_init):
            dlo, _ = PPL["dalam"]
            nlam = C("da_nlam", 1)
            e01 = C("da_e01", 1)
            e23 = C("da_e23", 1)
            lt = C("da_lt", 128)
            k.tt("v", sub(lt, 0, 64), pp[:, dlo:dlo + 64], pp[:, dlo + 64:dlo + 128], ALU.mult)
            k.tt("v", sub(lt, 64, 128), pp[:, dlo + 128:dlo + 192], pp[:, dlo + 192:dlo + 256], ALU.mult)
            k.reduce_sum(e01, sub(lt, 0, 64))
            k.reduce_sum(e23, sub(lt, 64, 128))
            k.actf(e01, e01, AF.Exp)
            k.actf(e23, e23, AF.Exp)
            k.tt("v", nlam, e23, e01, ALU.subtract)
            k.ts("v", nlam, nlam, -lam_init, ALU.add)
            gq = C("da_gq", 1)
            k.ts("v", gq, P("dag", 0), float((1.0 - lam_init) * math.sqrt(128.0)), ALU.mult)
            with k.scope():
                Vall = Buf(k, "t_V", [128, 20, 512], BF16)
                KTc = Buf(k, "t_KTc", [128, 4, 512], BF16)
                with k.scope():
                    wkvs = [Buf(k, "t_wkv%d" % i, [128, KT, 256], BF16) for i in range(4)]
                    for part in range(4):
                        k.dma(wkvs[part][:, :, :], winv[:, :, 1024 + part * 256:1024 + (part + 1) * 256], q=k.pool)
                    k.dma(Vall[:, 16:20, :], vc_d.rearrange("(a p) f -> p a f", p=128), q=k.pool)
                    kcs = Buf(k, "t_kcs", [128, 4, 512], F32)
                    k.dma(kcs[:, :, :], kc_d.rearrange("(a p) f -> p a f", p=128))
                    kst = [Buf(k, "t_kst%d" % i, [128, 512], F32) for i in range(2)]
                    vst = [Buf(k, "t_vst%d" % i, [128, 512], F32) for i in range(2)]
                    for h in range(4):
                        pb = psum()
                        for a in range(4):
                            k.transpose(pb[:, a * 128:(a + 1) * 128], kcs[:, a, h * 128:(h + 1) * 128], ident[:, :])
                        k.copy("a", KTc[:, h, :], pb[:, :])
                    for tt_ in range(16):
                        seg, off = tt_ // 2, 1 + (tt_ % 2) * 128
                        pk_ = psum()
                        pv_ = psum()
                        for part in range(4):
                            dstp = (pk_ if part < 2 else pv_)[:, (part % 2) * 256:(part % 2 + 1) * 256]
                            k.mm(dstp, [(hT[:, kt, seg, off:off + 128], wkvs[part][:, kt, :]) for kt in range(KT)])
                        ks_, vs_ = kst[tt_ % 2], vst[tt_ % 2]
                        k.copy("a", ks_[:, :], pk_[:, :])
                        k.copy("v", vs_[:, :], pv_[:, :])
                        k.dma(newk_d[tt_ * 128:(tt_ + 1) * 128, :], ks_[:, :])
                        k.dma(newv_d[tt_ * 128:(tt_ + 1) * 128, :], vs_[:, :])
                        k.copy("a", Vall[:, tt_, :], vs_[:, :])
                ropec = Buf(k, "t_ropec", [128, NT], F32)
                ropes = Buf(k, "t_ropes", [128, NT], F32)
                perm = Buf(k, "t_perm", [128, 128], BF16)
                k.dma(ropec[:, :], ropec_d)
                k.dma(ropes[:, :], ropes_d)
                k.dma(perm[:, :], perm_d, q=k.pool)
                QT = Buf(k, "t_QT", [128, NT], BF16)
                KTb = Buf(k, "t_KT", [128, NT], BF16)
                wq = [Buf(k, "t_wq%d" % i, [128, KT, 128], BF16) for i in range(2)]
                qorder = []
                for h in range(4):
                    qorder += [512 + h * 128, 1024 + h * 128]
                wqs = Stream(wq, [[(lambda b: b[:, :, :], winv[:, :, c0:c0 + 128])] for c0 in qorder], 1)
                qraw = Buf(k, "t_qraw", [128, 512], BF16)
                rt = [Buf(k, "t_rt%d" % i, [128, 512], F32) for i in range(4)]
                Pm = [Buf(k, "t_P%d" % i, [128, 512], BF16) for i in range(6)]
                osq = Buf(k, "t_osq", [128, 512], BF16)
                abo, _ = PPL["abias"]
                ip = [0]
                for h in range(4 if "noheads" not in skip else 0):
                    for which, dst in ((0, QT), (1, KTb)):
                        wb = wqs.get(2 * h + which)
                        for tb in range(4):
                            sl = slice(tb * 512, (tb + 1) * 512)
                            pb = psum(4, 8)
                            k.mm(pb[:, :], [(wb[:, kt, :], hseg(kt, tb)) for kt in range(KT)])
                            k.copy("a", qraw[:, :], pb[:, :])
                            pb2 = psum(4, 8)
                            k.mm(pb2[:, :], [(perm[:, :], qraw[:, :])])
                            t1, t2 = rt[(tb % 2) * 2], rt[(tb % 2) * 2 + 1]
                            k.tt("v", t1[:, :], pb[:, :], ropec[:, sl], ALU.mult)
                            k.tt("v", t2[:, :], pb2[:, :], ropes[:, sl], ALU.mult)
                            k.tt("v", dst[:, sl], t1[:, :], t2[:, :], ALU.add)
                    for qsb in range(8 if "noattn" not in skip else 0):
                        qs = slice(qsb * SEG, (qsb + 1) * SEG)
                        acc = [ps[0], ps[1], ps[2], ps[3]]
                        items = list(range(10))
                        LAG = 1
                        pms = {}

                        def kview(kt_, rs):
                            if kt_ < 16:
                                return KTb[rs, kt_ * 128:(kt_ + 1) * 128]
                            return KTc[rs, h, (kt_ - 16) * 128:(kt_ - 15) * 128]

                        def stage1(i):
                            ktp = items[i]
                            sps = [psum(4, 8), psum(4, 8)]
                            for j2 in range(2):
                                for m in range(2):
                                    rs = slice(64 * m, 64 * m + 64)
                                    k.mm(sps[m][:, j2 * SEG:(j2 + 1) * SEG], [(kview(2 * ktp + j2, rs), QT[rs, qs])],
                                         tile_position=(64 * m, 0))
                            col = abo + (2 * ktp) * 8 + qsb
                            cur = []
                            for m in range(2):
                                pm = Pm[ip[0] % len(Pm)]
                                ip[0] += 1
                                k.actf(pm[:, :], sps[m][:, :], AF.Exp, bias=pp[:, col:col + 1], scale=0.125)
                                cur.append(pm)
                            pms[i] = cur

                        def stage2(i):
                            ktp = items[i]
                            cur = pms.pop(i)
                            for m in range(2):
                                pm = cur[m]
                                for j2 in range(2):
                                    kt_ = 2 * ktp + j2
                                    vt = Vall[:, kt_, h * 128:(h + 1) * 128]
                                    pv = pm[:, j2 * SEG:(j2 + 1) * SEG]
                                    k.emit(k.pe, (lambda vt=vt, pv=pv, kt_=kt_, m=m: k.pe.h.matmul(
                                        acc[m][:, 0:SEG].ap, vt.ap, pv.ap, start=(kt_ == 0), stop=(kt_ == 19))),
                                        [vt, pv], [acc[m][:, 0:SEG]])
                            for m in range(2):
                                pm = cur[m]
                                k.emit(k.pe, (lambda pm=pm, m=m: k.pe.h.matmul(
                                    acc[2 + m][:, :].ap, onesb[:, :].ap, pm[:, :].ap, start=(ktp == 0), stop=(ktp == 9))),
                                    [onesb[:, :], pm[:, :]], [acc[2 + m][:, :]])
                        for i in range(len(items) + LAG):
                            if i < len(items):
                                stage1(i)
                            if i >= LAG:
                                stage2(i - LAG)
                        r0, r1, t0, t1 = (rt[i][:, 0:SEG] for i in range(4))
                        k.copy("a", r0, acc[2][:, SEG:2 * SEG])
                        k.copy("a", r1, acc[3][:, SEG:2 * SEG])
                        k.tt("v", r0, acc[2][:, 0:SEG], r0, ALU.add)
                        k.tt("v", r1, acc[3][:, 0:SEG], r1, ALU.add)
                        k.recip(r0, r0)
                        k.recip(r1, r1)
                        k.tt("v", t0, acc[0][:, 0:SEG], r0, ALU.mult)
                        k.tt("v", t1, acc[1][:, 0:SEG], r1, ALU.mult)
                        k.stt(t0, t1, nlam, t0, ALU.mult, ALU.add)
                        k.actf(osq[:, 0:SEG], t0, AF.Square)
                        pb = psum(4, 8)
                        k.mm(pb[:, 0:SEG], [(onesb[:, :], osq[:, 0:SEG])])
                        rsqrt_eps(r0, pb[:, 0:SEG], 128 * EPS)
                        k.tt("v", t0, t0, r0, ALU.mult)
                        k.actf(ymix[:, 4 + h, qs], t0, AF.Identity, scale=gq)

        for ph in phases:
            if ph == "ab":
                mixer_ab(0)
            elif ph == "ffn0":
                ffn(0)
            elif ph == "cd":
                mixer_cd(1)
            elif ph == "ffn1":
                ffn(1)

        with k.scope():
            yst = [Buf(k, "yst%d" % i, [128, D], F32) for i in range(3)]
            for tt_ in range(16):
                st = yst[tt_ % 3]
                for half in range(2):
                    pb = psum()
                    for a in range(4):
                        kt = half * 4 + a
                        k.transpose(pb[:, a * 128:(a + 1) * 128], xT[:, kt, tt_ * 128:(tt_ + 1) * 128], ident[:, :])
                    k.copy("v" if half == 0 else "a", st[:, half * 512:(half + 1) * 512], pb[:, :])
                k.dma(y_d[tt_ * 128:(tt_ + 1) * 128, :], st[:, :])
        k.finish()
    return k


def _rope_tables():
    GRID_W, ROPE_F, THETA = 64, 16, 10000.0
    t = np.arange(NT)
    t_row = (t // GRID_W).astype(np.float32)
    t_col = (t % GRID_W).astype(np.float32)
    inv = (np.float32(THETA) ** (-np.arange(ROPE_F, dtype=np.float32) / np.float32(ROPE_F))).astype(np.float32)
    ang = np.stack([t_row[:, None] * inv, t_col[:, None] * inv], axis=1).astype(np.float32)
    cos, sin = np.cos(ang).astype(np.float32), np.sin(ang).astype(np.float32)
    rc = np.zeros((128, NT), np.float32)
    rs = np.zeros((128, NT), np.float32)
    for m in range(2):
        for a in range(2):
            for hh in range(2):
                for f in range(16):
                    r = m * 64 + a * 32 + hh * 16 + f
                    rc[r] = cos[:, a, f]
                    rs[r] = sin[:, a, f] * (-1.0 if hh == 0 else 1.0)
    return rc, rs


def _make_inputs(inputs):
    f = lambda n: np.asarray(inputs[n], dtype=np.float32)
    x_prompt, x_sample = f("x_prompt"), f("x_sample")
    ck, cv = f("cache_attn_k"), f("cache_attn_v")
    st_s5, st_lru = f("state_s5"), f("state_rglru")
    c, c_ctx = f("c"), f("c_ctx")
    ident = np.eye(128, dtype=np.float32)
    perm = np.zeros((128, 128), np.float32)
    for r in range(128):
        perm[r ^ 16, r] = 1.0
    rc, rs = _rope_tables()
    ones_c, zeros_s = np.ones((128, NT), np.float32), np.zeros((128, NT), np.float32)
    s5b = f("s5_b")[0]
    s5c = f("s5_c")[0]
    bt = np.zeros((128, 4, 2, 2, 128), np.float32)
    ctm = np.zeros((128, 16, 2, 2, 128), np.float32)
    for cc in range(4):
        for jj in range(4):
            j = 4 * cc + jj
            for g2 in range(2):
                g = 2 * j + g2
                for d in range(2):
                    for ri in range(2):
                        bt[32 * jj + 16 * g2:32 * jj + 16 * g2 + 16, cc, d, ri, 64 * g2:64 * g2 + 64] = s5b[d, ri, g].T
                        ctm[64 * g2:64 * g2 + 64, j, d, ri, 32 * jj + 16 * g2:32 * jj + 16 * g2 + 16] = s5c[d, ri, g].T
    lwa, lwx = f("lru_w_a")[0], f("lru_w_x")[0]
    lruw = np.zeros((128, 2, 2, 4, 128), np.float32)
    for ax, wsrc in enumerate((lwa, lwx)):
        for d in range(2):
            for cc in range(4):
                for k2 in range(2):
                    lruw[64 * k2:64 * k2 + 64, ax, d, cc, 64 * k2:64 * k2 + 64] = wsrc[d, 2 * cc + k2]
    shared = {
        "ident": ident, "perm": perm,
        "w_mod": f("w_mod"), "w_in_ab": f("w_in_ab")[0], "w_out_ab": f("w_out_ab")[0],
        "s5bt": bt.reshape(128, -1), "s5ct": ctm.reshape(128, -1), "w_glu": f("s5_w_glu")[0],
        "w_in_cd": f("w_in_cd")[0], "w_out_cd": f("w_out_cd")[0], "lruw": lruw.reshape(128, -1),
        "w_up": f("ffn_w_up"), "w_down": f("ffn_w_down"),
    }
    in_maps = []
    for core in range(8):
        prompt = core < 4
        if prompt:
            x = x_prompt[8 * core:8 * core + 8].reshape(NT, D)
            cond, flagS = c_ctx, 0.0
            kc = np.zeros((512, 512), np.float32)
            vc = np.zeros((512, 512), np.float32)
            s5h0 = np.zeros((2, 2, 32, 64), np.float32)
            lh0 = np.zeros((2, 512), np.float32)
            ropec, ropes = ones_c, zeros_s
            ab = np.full((20, 8), -30000.0, np.float32)
            for kt_ in range(16):
                ab[kt_, kt_ // 2] = 0.0
        else:
            b = core - 4
            x = x_sample[b]
            cond, flagS = c[b], 1.0
            kc = ck[b, 0].reshape(512, 512)
            vc = cv[b, 0].reshape(512, 512)
            s5h0 = st_s5[b, 0]
            lh0 = st_lru[b, 0]
            ropec, ropes = rc, rs
            ab = np.zeros((20, 8), np.float32)
        abias = np.broadcast_to(ab.reshape(1, 160), (128, 160))
        pp = host_pack(cond, flagS, f("b_mod"), f("norm_g"), f("s5_lam_re"), f("s5_lam_im"), f("s5_log_dt"), s5h0,
                       f("s5_d"), f("s5_b_glu"), f("da_g"), f("da_lam"), f("sc_conv_w"), f("lru_conv_w"),
                       f("lru_conv_b"), f("lru_b_a"), f("lru_b_x"), f("lru_lam"), lh0, f("ffn_conv_w"),
                       f("ffn_conv_b"), abias)
        m = dict(shared)
        m.update({"x": np.ascontiguousarray(x), "pp": pp, "ropec": ropec, "ropes": ropes,
                  "kc": np.ascontiguousarray(kc), "vc": np.ascontiguousarray(vc)})
        in_maps.append(m)
    return in_maps


_PROG = {}


def run(inputs, phases=("ab", "ffn0", "cd", "ffn1"), skip=(), trace=False):
    key = (tuple(phases), tuple(skip))
    if key not in _PROG:
        _PROG[key] = build_program(phases, skip)
    k = _PROG[key]
    in_maps = _make_inputs(inputs)
    res = run_bass_kernel_spmd(k.nc, in_maps, core_ids=list(range(8)), trace=trace)
    return res


def kernel(**inputs):
    res = run(inputs)
    r = res.results
    y_prompt = np.concatenate([r[c]["y"].reshape(8, 256, D) for c in range(4)], axis=0)
    y_sample = np.stack([r[c]["y"] for c in range(4, 8)], axis=0)
    newk = np.concatenate([r[c]["newk"].reshape(8, 256, 4, 2, 64) for c in range(4)], axis=0)[:, None]
    newv = np.concatenate([r[c]["newv"].reshape(8, 256, 4, 128) for c in range(4)], axis=0)[:, None]
    news5 = np.concatenate([r[c]["news5"] for c in range(4)], axis=0)[:, None]
    newlru = np.concatenate([r[c]["newlru"] for c in range(4)], axis=0)[:, None]
    return (y_prompt.astype(np.float32), y_sample.astype(np.float32), newk.astype(np.float32),
            newv.astype(np.float32), news5.astype(np.float32), newlru.astype(np.float32))
```
